# Optimizing a Trainium2 kernel written in Bass

```python
import math
import jax, jax.numpy as jnp
from jax import lax
import numpy as np

D_MODEL = 1024
BATCH = 2
SEQ = 16384
DEPTH = 4

N_EVEN = (DEPTH + 1) // 2
N_ODD = DEPTH // 2
D_FF = 4 * D_MODEL
EPS = 1e-6
NEG = -1e30
FORCE = 1e9
HEAD_DIM = 64

SSD_HEADS = 8
SSD_INNER = SSD_HEADS * HEAD_DIM
SSD_GROUPS = 2
SSD_RPG = SSD_HEADS // SSD_GROUPS
SSD_STATE = 128
SSD_CONV = 4
SSD_CHUNK = 128
SSD_CONV_DIM = SSD_INNER + 2 * SSD_GROUPS * SSD_STATE
SSD_IN = SSD_INNER + SSD_CONV_DIM + SSD_HEADS
DT_MIN = 0.001
DT_MAX = 0.1

NSA_HEADS = 8
NSA_KV = 2
NSA_RPG = NSA_HEADS // NSA_KV
NSA_CMP_LEN = 32
NSA_CMP_STRIDE = 16
NSA_SLC_LEN = 64
NSA_TOPK = 16
NSA_WIN = 512
NSA_CMP_HIDDEN = 256
NSA_QBLK = 128
NSA_Q = NSA_HEADS * HEAD_DIM
NSA_KVW = NSA_KV * HEAD_DIM
NSA_IN = NSA_Q + 6 * NSA_KVW + 3 * NSA_HEADS

EVEN_IN = SSD_IN + NSA_IN
EVEN_MIX = SSD_INNER + NSA_Q

SWA_HEADS = 8
SWA_KV = 2
SWA_RPG = SWA_HEADS // SWA_KV
SWA_WIN = 128
SWA_QBLK = 128
SWA_Q = SWA_HEADS * HEAD_DIM
SWA_KVW = SWA_KV * HEAD_DIM
SWA_IN = SWA_Q + 2 * SWA_KVW

S5_CH = 512
S5_GROUP_CH = 16
S5_GROUPS = S5_CH // S5_GROUP_CH
S5_STATE = 64

ODD_IN = SWA_IN + S5_CH
ODD_MIX = SWA_Q + S5_CH

kernel_name = 'hybrid_ssd_nsa_swa_s5_trunk'


def _rmsnorm(x, g):
    xf = x.astype(jnp.float32)
    y = xf * lax.rsqrt(jnp.mean(xf * xf, axis=-1, keepdims=True) + EPS)
    return (y * g.astype(jnp.float32)).astype(x.dtype)


def _masked_softmax(s, mask):
    s = jnp.where(mask, s.astype(jnp.float32), NEG)
    p = jax.nn.softmax(s, axis=-1)
    return jnp.where(mask, p, 0.0)


def _causal_depthwise_conv(x, w, b):
    k, c = w.shape
    y = lax.conv_general_dilated(x, w.astype(x.dtype)[:, None, :], window_strides=(1,),
                                 padding=[(k - 1, 0)], dimension_numbers=('NWC', 'WIO', 'NWC'),
                                 feature_group_count=c)
    return y + b.astype(y.dtype)


def _ssd_chunked_scan(xs, dt, a, bm, cm):
    bsz, seq, g, r, p = xs.shape
    n = bm.shape[-1]
    t = SSD_CHUNK
    nc = seq // t
    dt = dt.reshape(bsz, nc, t, g, r)
    a_cum = jnp.cumsum(dt * a, axis=2)
    xd = xs.astype(jnp.float32).reshape(bsz, nc, t, g, r, p) * dt[..., None]
    bc = bm.astype(jnp.float32).reshape(bsz, nc, t, g, n)
    cc = cm.astype(jnp.float32).reshape(bsz, nc, t, g, n)
    causal = jnp.tril(jnp.ones((t, t), dtype=bool))[:, :, None, None]
    seg = a_cum[:, :, :, None] - a_cum[:, :, None, :]
    decay = jnp.exp(jnp.where(causal, seg, NEG))
    cb = jnp.einsum('bctgn,bcsgn->bctsg', cc, bc)
    y_diag = jnp.einsum('bctsg,bctsgr,bcsgrp->bctgrp', cb, decay, xd)
    decay_to_end = jnp.exp(a_cum[:, :, -1:] - a_cum)
    states = jnp.einsum('bctgn,bctgr,bctgrp->bcgrpn', bc, decay_to_end, xd)
    chunk_decay = jnp.exp(a_cum[:, :, -1])

    def carry_state(h, inp):
        s_c, d_c = inp
        return h * d_c[..., None, None] + s_c, h

    h0 = jnp.zeros((bsz, g, r, p, n), jnp.float32)
    _, h_in = lax.scan(carry_state, h0, (jnp.moveaxis(states, 1, 0), jnp.moveaxis(chunk_decay, 1, 0)))
    h_in = jnp.moveaxis(h_in, 0, 1)
    y_off = jnp.einsum('bctgn,bcgrpn,bctgr->bctgrp', cc, h_in, jnp.exp(a_cum))
    return (y_diag + y_off).reshape(bsz, seq, g, r, p)


def _ssd_mixer(u, conv_w, conv_b, dt_bias, a_log, d_skip, norm_g):
    bsz, seq, _ = u.shape
    gn = SSD_GROUPS * SSD_STATE
    z = u[..., :SSD_INNER]
    xbc = u[..., SSD_INNER:SSD_INNER + SSD_CONV_DIM]
    dt_raw = u[..., SSD_INNER + SSD_CONV_DIM:]
    xbc = jax.nn.silu(_causal_depthwise_conv(xbc, conv_w, conv_b))
    xs = xbc[..., :SSD_INNER].reshape(bsz, seq, SSD_GROUPS, SSD_RPG, HEAD_DIM)
    bm = xbc[..., SSD_INNER:SSD_INNER + gn].reshape(bsz, seq, SSD_GROUPS, SSD_STATE)
    cm = xbc[..., SSD_INNER + gn:].reshape(bsz, seq, SSD_GROUPS, SSD_STATE)
    dt = jax.nn.softplus(dt_raw.astype(jnp.float32) + dt_bias.astype(jnp.float32))
    dt = dt.reshape(bsz, seq, SSD_GROUPS, SSD_RPG)
    a = -jnp.exp(a_log.astype(jnp.float32)).reshape(SSD_GROUPS, SSD_RPG)
    y = _ssd_chunked_scan(xs, dt, a, bm, cm)
    y = y + xs.astype(jnp.float32) * d_skip.astype(jnp.float32).reshape(SSD_GROUPS, SSD_RPG, 1)
    y = y.reshape(bsz, seq, SSD_INNER) * jax.nn.silu(z.astype(jnp.float32))
    y = _rmsnorm(y.reshape(bsz, seq, SSD_GROUPS, SSD_INNER // SSD_GROUPS),
                 norm_g.reshape(SSD_GROUPS, SSD_INNER // SSD_GROUPS))
    return y.reshape(bsz, seq, SSD_INNER).astype(u.dtype)


def _nsa_compress(kv, pe, w1, b1, w2, b2):
    bsz, seq, g, d = kv.shape
    ratio = NSA_CMP_LEN // NSA_CMP_STRIDE
    nc = seq // NSA_CMP_STRIDE - ratio + 1
    pieces = kv.reshape(bsz, seq // NSA_CMP_STRIDE, NSA_CMP_STRIDE, g, d)
    blocks = jnp.concatenate([pieces[:, j:j + nc] for j in range(ratio)], axis=2)
    blocks = blocks + pe[:, None, :]
    flat = jnp.moveaxis(blocks, 3, 2).reshape(bsz, nc, g, NSA_CMP_LEN * d)
    hid = jax.nn.gelu(flat @ w1 + b1)
    return hid @ w2 + b2


def _nsa_mixer(u, pe, w1, b1, w2, b2):
    bsz, seq, _ = u.shape
    g_, r_, d_ = NSA_KV, NSA_RPG, HEAD_DIM
    q = u[..., :NSA_Q].reshape(bsz, seq, g_, r_, d_)
    parts = [u[..., NSA_Q + i * NSA_KVW:NSA_Q + (i + 1) * NSA_KVW].reshape(bsz, seq, g_, d_) for i in range(6)]
    k_cmp, v_cmp, k_slc, v_slc, k_win, v_win = parts
    gates = jax.nn.sigmoid(u[..., NSA_Q + 6 * NSA_KVW:].astype(jnp.float32)).reshape(bsz, seq, g_, r_, 3)
    kc = _nsa_compress(k_cmp, pe[0], w1[0], b1[0], w2[0], b2[0])
    vc = _nsa_compress(v_cmp, pe[1], w1[1], b1[1], w2[1], b2[1])
    nc = kc.shape[1]
    ns = seq // NSA_SLC_LEN
    topk = min(NSA_TOPK, ns)
    cmp_end = jnp.arange(nc) * NSA_CMP_STRIDE + NSA_CMP_LEN - 1
    c_start = jnp.arange(nc)[:, None] * NSA_CMP_STRIDE
    s_start = jnp.arange(ns)[None, :] * NSA_SLC_LEN
    overlap = ((c_start < s_start + NSA_SLC_LEN) & (c_start + NSA_CMP_LEN > s_start)).astype(jnp.float32)
    ks_blocks = jnp.moveaxis(k_slc.reshape(bsz, ns, NSA_SLC_LEN, g_, d_), 3, 1)
    vs_blocks = jnp.moveaxis(v_slc.reshape(bsz, ns, NSA_SLC_LEN, g_, d_), 3, 1)
    pad = ((0, 0), (NSA_WIN, 0), (0, 0), (0, 0))
    kw_pad = jnp.pad(k_win, pad)
    vw_pad = jnp.pad(v_win, pad)
    gather = jax.vmap(jax.vmap(lambda blocks, ix: blocks[ix]))
    scale = HEAD_DIM ** -0.5
    span = NSA_WIN + NSA_QBLK
    x_sel = topk * NSA_SLC_LEN

    def query_block(qb):
        s0 = qb * NSA_QBLK
        t = s0 + jnp.arange(NSA_QBLK)
        qblk = lax.dynamic_slice_in_dim(q, s0, NSA_QBLK, axis=1)
        gblk = lax.dynamic_slice_in_dim(gates, s0, NSA_QBLK, axis=1)
        cmask = cmp_end[None, :] <= t[:, None]
        p_cmp = _masked_softmax(jnp.einsum('bqgrd,bngd->bgrqn', qblk, kc) * scale, cmask)
        o_cmp = jnp.einsum('bgrqn,bngd->bqgrd', p_cmp, vc)
        imp = jnp.einsum('bgrqn,nj->bgqj', p_cmp, overlap)
        cur = (t // NSA_SLC_LEN)[:, None]
        j = jnp.arange(ns)[None, :]
        forced = (j == 0) | (j == cur) | (j == cur - 1)
        imp = jnp.where(forced, FORCE, jnp.where(j <= cur, imp, -FORCE))
        _, idx = lax.top_k(imp, topk)
        ks = gather(ks_blocks, idx).reshape(bsz, g_, NSA_QBLK, x_sel, d_)
        vs = gather(vs_blocks, idx).reshape(bsz, g_, NSA_QBLK, x_sel, d_)
        kpos = (idx[..., None] * NSA_SLC_LEN + jnp.arange(NSA_SLC_LEN)).reshape(bsz, g_, NSA_QBLK, x_sel)
        smask = (kpos <= t[None, None, :, None])[:, :, None]
        p_slc = _masked_softmax(jnp.einsum('bqgrd,bgqxd->bgrqx', qblk, ks) * scale, smask)
        o_slc = jnp.einsum('bgrqx,bgqxd->bqgrd', p_slc, vs)
        kw = lax.dynamic_slice_in_dim(kw_pad, s0, span, axis=1)
        vw = lax.dynamic_slice_in_dim(vw_pad, s0, span, axis=1)
        kp = (s0 - NSA_WIN + jnp.arange(span))[None, :]
        wmask = (kp <= t[:, None]) & (kp > t[:, None] - NSA_WIN) & (kp >= 0)
        p_win = _masked_softmax(jnp.einsum('bqgrd,bkgd->bgrqk', qblk, kw) * scale, wmask)
        o_win = jnp.einsum('bgrqk,bkgd->bqgrd', p_win, vw)
        return gblk[..., 0:1] * o_cmp + gblk[..., 1:2] * o_slc + gblk[..., 2:3] * o_win

    out = lax.map(query_block, jnp.arange(seq // NSA_QBLK))
    return jnp.moveaxis(out, 0, 1).reshape(bsz, seq, NSA_Q).astype(u.dtype)


def _swa_sinks_mixer(u, sinks):
    bsz, seq, _ = u.shape
    g_, r_, d_, t_ = SWA_KV, SWA_RPG, HEAD_DIM, SWA_QBLK
    nb = seq // t_
    q = u[..., :SWA_Q].reshape(bsz, nb, t_, g_, r_, d_)
    k = u[..., SWA_Q:SWA_Q + SWA_KVW].reshape(bsz, nb, t_, g_, d_)
    v = u[..., SWA_Q + SWA_KVW:].reshape(bsz, nb, t_, g_, d_)

    def band(a):
        prev = jnp.concatenate([jnp.zeros_like(a[:, :1]), a[:, :-1]], axis=1)
        return jnp.concatenate([prev, a], axis=2)

    kb, vb = band(k), band(v)
    s = jnp.einsum('bnqgrd,bnkgd->bngrqk', q, kb).astype(jnp.float32) * (HEAD_DIM ** -0.5)
    qpos = jnp.arange(t_)[:, None]
    kpos = jnp.arange(2 * t_)[None, :] - t_
    in_win = (kpos <= qpos) & (kpos > qpos - SWA_WIN)
    first = jnp.arange(nb)[:, None, None] == 0
    mask = (in_win[None] & ~(first & (kpos < 0)[None]))[None, :, None, None]
    sink = sinks.astype(jnp.float32).reshape(1, 1, g_, r_, 1, 1)
    s = jnp.where(mask, s, NEG)
    m = jnp.maximum(jnp.max(s, axis=-1, keepdims=True), sink)
    e = jnp.where(mask, jnp.exp(s - m), 0.0)
    p = e / (jnp.sum(e, axis=-1, keepdims=True) + jnp.exp(sink - m))
    o = jnp.einsum('bngrqk,bnkgd->bnqgrd', p, vb)
    return o.reshape(bsz, seq, SWA_Q).astype(u.dtype)


def _s5_mixer(u, a_re, a_im, log_dt, b_re, b_im, c_re, c_im, d_skip, glu_w, glu_b):
    bsz, seq, _ = u.shape
    f32 = jnp.float32
    uf = u.astype(f32).reshape(bsz, seq, S5_GROUPS, S5_GROUP_CH)
    lam = lax.complex(a_re.astype(f32), a_im.astype(f32))
    step = jnp.exp(log_dt.astype(f32))[:, None]
    lam_bar = jnp.exp(lam * step)
    b_bar = ((lam_bar - 1.0) / lam)[..., None] * lax.complex(b_re.astype(f32), b_im.astype(f32))
    bu = jnp.einsum('gph,blgh->blgp', b_bar, uf.astype(jnp.complex64))
    a = jnp.broadcast_to(lam_bar, bu.shape)

    def combine(e1, e2):
        a1, x1 = e1
        a2, x2 = e2
        return a1 * a2, a2 * x1 + x2

    _, states = lax.associative_scan(combine, (a, bu), axis=1)
    c = lax.complex(c_re.astype(f32), c_im.astype(f32))
    y = jnp.einsum('ghp,blgp->blgh', c, states).real + d_skip.astype(f32).reshape(S5_GROUPS, S5_GROUP_CH) * uf
    y = jax.nn.gelu(y.reshape(bsz, seq, S5_CH))
    y = y * jax.nn.sigmoid(y @ glu_w.astype(f32) + glu_b.astype(f32))
    return y.astype(u.dtype)


def _sq_relu_mlp(x, w_up, w_down):
    return jnp.square(jax.nn.relu(x @ w_up)) @ w_down


def setup_inputs(seed: int = 0) -> dict:
    key = jax.random.key(seed)
    k = jax.random.split(key, 32)
    f32 = jnp.float32

    def nrm(i, shape, scale):
        return scale * jax.random.normal(k[i], shape, f32)

    def uni(i, shape, lo, hi):
        return jax.random.uniform(k[i], shape, f32, lo, hi)

    ssd_dt = jnp.exp(uni(10, (N_EVEN, SSD_HEADS), math.log(DT_MIN), math.log(DT_MAX)))
    return {
        'x': nrm(0, (BATCH, SEQ, D_MODEL), 1.0),
        'norm_mix': 1.0 + nrm(1, (DEPTH, D_MODEL), 0.02),
        'norm_mlp': 1.0 + nrm(2, (DEPTH, D_MODEL), 0.02),
        'norm_final': 1.0 + nrm(3, (D_MODEL,), 0.02),
        'mlp_w_up': nrm(4, (DEPTH, D_MODEL, D_FF), D_MODEL ** -0.5),
        'mlp_w_down': nrm(5, (DEPTH, D_FF, D_MODEL), D_FF ** -0.5),
        'ev_w_in': nrm(6, (N_EVEN, D_MODEL, EVEN_IN), D_MODEL ** -0.5),
        'ev_w_out': nrm(7, (N_EVEN, EVEN_MIX, D_MODEL), EVEN_MIX ** -0.5),
        'ssd_conv_w': nrm(8, (N_EVEN, SSD_CONV, SSD_CONV_DIM), SSD_CONV ** -0.5),
        'ssd_conv_b': nrm(9, (N_EVEN, SSD_CONV_DIM), 0.01),
        'ssd_dt_bias': ssd_dt + jnp.log(-jnp.expm1(-ssd_dt)),
        'ssd_a_log': jnp.log(uni(11, (N_EVEN, SSD_HEADS), 1.0, 16.0)),
        'ssd_d': 1.0 + nrm(12, (N_EVEN, SSD_HEADS), 0.1),
        'ssd_norm': 1.0 + nrm(13, (N_EVEN, SSD_INNER), 0.02),
        'nsa_pe': nrm(14, (N_EVEN, 2, NSA_CMP_LEN, HEAD_DIM), 0.02),
        'nsa_cmp_w1': nrm(15, (N_EVEN, 2, NSA_CMP_LEN * HEAD_DIM, NSA_CMP_HIDDEN), (NSA_CMP_LEN * HEAD_DIM) ** -0.5),
        'nsa_cmp_b1': nrm(16, (N_EVEN, 2, NSA_CMP_HIDDEN), 0.01),
        'nsa_cmp_w2': nrm(17, (N_EVEN, 2, NSA_CMP_HIDDEN, HEAD_DIM), NSA_CMP_HIDDEN ** -0.5),
        'nsa_cmp_b2': nrm(18, (N_EVEN, 2, HEAD_DIM), 0.01),
        'od_w_in': nrm(19, (N_ODD, D_MODEL, ODD_IN), D_MODEL ** -0.5),
        'od_w_out': nrm(20, (N_ODD, ODD_MIX, D_MODEL), ODD_MIX ** -0.5),
        'swa_sinks': nrm(21, (N_ODD, SWA_HEADS), 0.5),
        's5_a_re': -0.5 + nrm(22, (N_ODD, S5_GROUPS, S5_STATE), 0.005),
        's5_a_im': jnp.broadcast_to(math.pi * jnp.arange(S5_STATE, dtype=f32), (N_ODD, S5_GROUPS, S5_STATE)),
        's5_log_dt': uni(23, (N_ODD, S5_GROUPS), math.log(DT_MIN), math.log(DT_MAX)),
        's5_b_re': nrm(24, (N_ODD, S5_GROUPS, S5_STATE, S5_GROUP_CH), (2 * S5_GROUP_CH) ** -0.5),
        's5_b_im': nrm(25, (N_ODD, S5_GROUPS, S5_STATE, S5_GROUP_CH), (2 * S5_GROUP_CH) ** -0.5),
        's5_c_re': nrm(26, (N_ODD, S5_GROUPS, S5_GROUP_CH, S5_STATE), (2 * S5_STATE) ** -0.5),
        's5_c_im': nrm(27, (N_ODD, S5_GROUPS, S5_GROUP_CH, S5_STATE), (2 * S5_STATE) ** -0.5),
        's5_d': nrm(28, (N_ODD, S5_CH), 1.0),
        's5_glu_w': nrm(29, (N_ODD, S5_CH, S5_CH), S5_CH ** -0.5),
        's5_glu_b': nrm(30, (N_ODD, S5_CH), 0.01),
    }


def reference(x, norm_mix, norm_mlp, norm_final, mlp_w_up, mlp_w_down,
              ev_w_in, ev_w_out, ssd_conv_w, ssd_conv_b, ssd_dt_bias, ssd_a_log, ssd_d, ssd_norm,
              nsa_pe, nsa_cmp_w1, nsa_cmp_b1, nsa_cmp_w2, nsa_cmp_b2,
              od_w_in, od_w_out, swa_sinks, s5_a_re, s5_a_im, s5_log_dt, s5_b_re, s5_b_im,
              s5_c_re, s5_c_im, s5_d, s5_glu_w, s5_glu_b):
    h = x
    for layer in range(DEPTH):
        hn = _rmsnorm(h, norm_mix[layer])
        i = layer // 2
        if layer % 2 == 0:
            u = hn @ ev_w_in[i]
            y_a = _ssd_mixer(u[..., :SSD_IN], ssd_conv_w[i], ssd_conv_b[i], ssd_dt_bias[i],
                             ssd_a_log[i], ssd_d[i], ssd_norm[i])
            y_b = _nsa_mixer(u[..., SSD_IN:], nsa_pe[i], nsa_cmp_w1[i], nsa_cmp_b1[i],
                             nsa_cmp_w2[i], nsa_cmp_b2[i])
            y = jnp.concatenate([y_a, y_b], axis=-1) @ ev_w_out[i]
        else:
            u = hn @ od_w_in[i]
            y_c = _swa_sinks_mixer(u[..., :SWA_IN], swa_sinks[i])
            y_d = _s5_mixer(u[..., SWA_IN:], s5_a_re[i], s5_a_im[i], s5_log_dt[i], s5_b_re[i], s5_b_im[i],
                            s5_c_re[i], s5_c_im[i], s5_d[i], s5_glu_w[i], s5_glu_b[i])
            y = jnp.concatenate([y_c, y_d], axis=-1) @ od_w_out[i]
        h = h + y.astype(h.dtype)
        h = h + _sq_relu_mlp(_rmsnorm(h, norm_mlp[layer]), mlp_w_up[layer], mlp_w_down[layer]).astype(h.dtype)
    return _rmsnorm(h, norm_final)
```

```python
import numpy as np
from contextlib import ExitStack
import concourse.bass as bass
import concourse.mybir as mybir
from concourse.bass_utils import run_bass_kernel_spmd

F32 = mybir.dt.float32
BF16 = mybir.dt.bfloat16
AF = mybir.ActivationFunctionType
ALU = mybir.AluOpType
AX = mybir.AxisListType

NS = 8
SAME_ENGINE_SYNC = True


class TT:
    def __init__(self, t, name):
        self.t = t
        self.name = name
        self.last_w = None
        self.readers = []

    def __getitem__(self, idx):
        return V(self, self.t[idx])

    def v(self, ap):
        return V(self, ap)


class V:
    def __init__(self, tt, ap):
        self.tt = tt
        self.ap = ap


class Prog:
    DMAC = ('dsp', 'dpool', 'dact')
    ENG = {'pe': 'tensor', 'dve': 'vector', 'act': 'scalar', 'pool': 'gpsimd', 'sp': 'sync'}

    def __init__(self):
        self.nc = bass.Bass("TRN2", target_bir_lowering=False)
        self.es = ExitStack()
        self.q = {e: [] for e in self.ENG}
        self.ctr = ['pe', 'dve', 'act', 'pool', 'dsp', 'dpool', 'dact']
        self.cnt = {c: 0 for c in self.ctr}
        self.sems = {c: [self.es.enter_context(self.nc.semaphore(f"s_{c}{i}")) for i in range(NS)]
                     for c in self.ctr}
        self.known = {e: {} for e in self.ENG}
        self.nbuf = 0
        self.dram = {}
        self.ninstr = 0
        self.stack = [self.es]

    def sb(self, shape, dtype=F32, name=None):
        self.nbuf += 1
        name = name or f"sb{self.nbuf}"
        t = self.stack[-1].enter_context(self.nc.sbuf_tensor(name, list(shape), dtype))
        return TT(t, name)

    def ps(self, shape, dtype=F32, name=None):
        self.nbuf += 1
        name = name or f"ps{self.nbuf}"
        t = self.stack[-1].enter_context(self.nc.psum_tensor(name, list(shape), dtype))
        return TT(t, name)

    def din(self, name, shape, dtype=F32):
        t = self.nc.dram_tensor(name, list(shape), dtype, kind="ExternalInput")
        tt = TT(t.ap(), name)
        self.dram[name] = tt
        return tt

    def dout(self, name, shape, dtype=F32):
        t = self.nc.dram_tensor(name, list(shape), dtype, kind="ExternalOutput")
        tt = TT(t.ap(), name)
        self.dram[name] = tt
        return tt

    def dint(self, name, shape, dtype=F32):
        t = self.nc.dram_tensor(name, list(shape), dtype, kind="Internal")
        tt = TT(t.ap(), name)
        self.dram[name] = tt
        return tt

    def _semval(self, c, k):
        mult = 16 if c.startswith('d') and c != 'dve' else 1
        return self.sems[c][(k - 1) % NS], mult * ((k - 1) // NS + 1)

    def op(self, e, fn, outs=(), ins=(), ctr=None):
        c = ctr or e
        deps = set()
        for v in ins:
            tt = v.tt if isinstance(v, V) else v
            if tt.last_w is not None:
                deps.add(tt.last_w)
        for v in outs:
            tt = v.tt if isinstance(v, V) else v
            if tt.last_w is not None:
                deps.add(tt.last_w)
            for r in tt.readers:
                deps.add(r)
        waits = []
        best = {}
        for (f, k) in deps:
            if f == c and (c == 'pe' or not SAME_ENGINE_SYNC):
                continue
            key = (f, (k - 1) % NS) if f in self.DMAC else f
            if k > best.get(key, (None, 0))[1]:
                best[key] = (f, k)
        for key, (f, k) in best.items():
            if self.known[e].get(key, 0) >= k:
                continue
            self.known[e][key] = k
            waits.append(self._semval(f, k))
        self.cnt[c] += 1
        k_me = self.cnt[c]
        is_dma = c in ('dsp', 'dpool', 'dact')
        if is_dma and k_me > NS:
            kk = k_me - NS
            key = (c, (kk - 1) % NS)
            if self.known[e].get(key, 0) < kk:
                self.known[e][key] = kk
                waits.append(self._semval(c, kk))
        sem = self.sems[c][(k_me - 1) % NS]
        inc = 16 if is_dma else 1

        eng = getattr(self.nc, self.ENG[e])
        for (s, val) in waits:
            eng.wait_ge(s, val)
        fn(eng).then_inc(sem, inc)
        self.ninstr += 1
        for v in outs:
            tt = v.tt if isinstance(v, V) else v
            tt.last_w = (c, k_me)
            tt.readers = []
        for v in ins:
            tt = v.tt if isinstance(v, V) else v
            tt.readers.append((c, k_me))
        return (c, k_me)

    def dma(self, out, in_, q='sp'):
        e, c = {'sp': ('sp', 'dsp'), 'pool': ('pool', 'dpool'), 'act': ('act', 'dact')}[q]
        return self.op(e, lambda eng: eng.dma_start(out=out.ap, in_=in_.ap), outs=[out], ins=[in_], ctr=c)

    def mm(self, o, out, lhsT, rhs, start, stop, ins):
        return self.op('pe', lambda e: e.matmul(out, lhsT=lhsT, rhs=rhs, start=start, stop=stop), outs=[o], ins=ins)

    def tr(self, o, out, in_, ident, ins):
        return self.op('pe', lambda e: e.transpose(out=out, in_=in_, identity=ident), outs=[o], ins=ins)

    def act(self, o, out, in_, func, ins, bias=None, scale=None):
        kw = {}
        if bias is not None:
            kw['bias'] = bias
        if scale is not None:
            kw['scale'] = scale
        return self.op('act', lambda e: e.activation(out=out, in_=in_, func=func, **kw), outs=[o], ins=ins)

    def tt(self, eng, o, out, a, b, op, ins):
        return self.op(eng, lambda e: e.tensor_tensor(out=out, in0=a, in1=b, op=op), outs=[o], ins=ins)

    def ts(self, eng, o, out, in_, s1, op0, ins, s2=None, op1=None):
        if op1 is None:
            return self.op(eng, lambda e: e.tensor_scalar(out=out, in0=in_, scalar1=s1, scalar2=None, op0=op0), outs=[o], ins=ins)
        return self.op(eng, lambda e: e.tensor_scalar(out=out, in0=in_, scalar1=s1, scalar2=s2, op0=op0, op1=op1), outs=[o], ins=ins)

    def stt(self, o, out, in0, scalar, in1, op0, op1, ins, eng='dve'):
        return self.op(eng, lambda e: e.scalar_tensor_tensor(out=out, in0=in0, scalar=scalar, in1=in1, op0=op0, op1=op1), outs=[o], ins=ins)

    def cp(self, eng, o, out, in_, ins):
        if eng == 'act':
            return self.op('act', lambda e: e.copy(out=out, in_=in_), outs=[o], ins=ins)
        return self.op(eng, lambda e: e.tensor_copy(out=out, in_=in_), outs=[o], ins=ins)

    def memset(self, o, ap, val, eng='pool'):
        return self.op(eng, lambda e: e.memset(ap, val), outs=[o])

    def barrier(self, engines=None):
        for e in (engines or self.ENG):
            eng = getattr(self.nc, self.ENG[e])
            for c in self.ctr:
                k = self.cnt[c]
                if k == 0:
                    continue
                if c in self.DMAC:
                    for kk in range(max(1, k - NS + 1), k + 1):
                        key = (c, (kk - 1) % NS)
                        if self.known[e].get(key, 0) < kk:
                            self.known[e][key] = kk
                            sm, val = self._semval(c, kk)
                            eng.wait_ge(sm, val)
                else:
                    if c == e and c == 'pe':
                        continue
                    if self.known[e].get(c, 0) < k:
                        self.known[e][c] = k
                        sm, val = self._semval(c, k)
                        eng.wait_ge(sm, val)

    from contextlib import contextmanager

    @contextmanager
    def scope(self):
        st = ExitStack()
        self.stack.append(st)
        try:
            yield
        finally:
            self.barrier()
            self.stack.pop()
            st.close()

    def finish(self):
        self.barrier(['sp'])
        self.es.close()
        return self.nc


I32 = mybir.dt.int32
NEGB = -30000.0
SSD_IN = 1544


EPS = 1e-6
TT_TOK = 512


def build_tok(post, odd, n_in, final, ntt=8):
    P = Prog()
    NT = ntt * TT_TOK
    hT = P.din("hT", [1024, NT])
    if post:
        ycT = P.din("ycT", [1024, NT])
        w_out = P.din("w_out", [1024, 1024])
        g_mlp = P.din("g_mlp", [128, 8])
        w_up = P.din("w_up", [1024, 4096])
        w_down = P.din("w_down", [4096, 1024])
        if odd:
            glu_w = P.din("glu_w", [512, 512])
            glu_b = P.din("glu_b", [128, 4])
        else:
            ssdn_d = P.din("ssdn", [128, 4])
    if n_in:
        g_in = P.din("g_in", [128, 8])
        w_in = P.din("w_in", [1024, n_in])
        uT = P.dout("uT", [n_in, NT])
    if final:
        g_fin = P.din("g_fin", [128, 8])
    if post or final:
        hTo = P.dout("hTo", [1024, NT])

    def cast_rows(dst, src, K, C):
        nd = (C + 1023) // 1024
        kg = 4 if nd == 1 else 1
        sv = src.t.rearrange("(k p) c -> p k c", p=128)
        for k0 in range(0, K, kg):
            k1 = min(K, k0 + kg)
            P.op('pool', lambda e, k0=k0, k1=k1: e.dma_start(out=dst.t[:, k0:k1, :], in_=sv[:, k0:k1, :], max_dma_last_dim=4096),
                 outs=[dst], ins=[src], ctr='dpool')
    if post:
        s_up = P.dint("s_up", [128, 8, 4096], BF16)
        s_down = P.dint("s_down", [128, 32, 1024], BF16)
        s_out = P.dint("s_out", [128, 8, 1024], BF16)
        cast_rows(s_up, w_up, 8, 4096)
        cast_rows(s_down, w_down, 32, 1024)
        cast_rows(s_out, w_out, 8, 1024)
        if odd:
            s_glu = P.dint("s_glu", [128, 4, 512], BF16)
            cast_rows(s_glu, glu_w, 4, 512)
    n_oc = (n_in + 127) // 128
    if n_in:
        s_in = P.dint("s_in", [128, 8, n_in], BF16)
        cast_rows(s_in, w_in, 8, n_in)

    ones = P.sb([128, 128], F32, "ones")
    P.op('pool', lambda e: e.memset(ones.t[:], 1.0), outs=[ones])
    h = [P.sb([128, TT_TOK], F32, f"h{k}") for k in range(8)]
    hn = [P.sb([128, TT_TOK], BF16, f"hn{k}") for k in range(8)]
    sq = [P.sb([128, TT_TOK], F32, f"sq{i}") for i in range(2)]
    rstd = P.sb([128, TT_TOK], F32, "rstd")
    pss = [P.ps([128, TT_TOK], F32, f"pp{i}") for i in range(4)]
    ps_ss = P.ps([128, TT_TOK], F32, "ps_ss")
    psi = [0]

    def next_ps():
        psi[0] = (psi[0] + 1) % 4
        return pss[psi[0]]

    if post:
        yc = [P.sb([128, TT_TOK], F32, f"yc{k}") for k in range(8)]
        ycb = [P.sb([128, TT_TOK], BF16, f"ycb{k}") for k in range(8)]
        wo = P.sb([128, 8, 1024], BF16, "wo")
        P.dma(wo[:], s_out[:])
        gm = P.sb([128, 8], F32, "gm")
        P.dma(gm[:], g_mlp[:])
        a = [P.sb([128, TT_TOK], BF16, f"a{f}") for f in range(32)]
        rl = [P.sb([128, TT_TOK], F32, f"rl{i}") for i in range(2)]
        wup = [P.sb([128, 8, 512], BF16, f"wup{i}") for i in range(2)]
        wdn = [P.sb([128, 32, 128], BF16, f"wdn{i}") for i in range(2)]
        if odd:
            wg = P.sb([128, 4, 512], BF16, "wg")
            P.dma(wg[:], s_glu[:])
            gb = P.sb([128, 4], F32, "gb")
            P.dma(gb[:], glu_b[:])
            gate = [P.sb([128, TT_TOK], F32, f"gate{i}") for i in range(2)]
        else:
            ssdn = P.sb([128, 4], F32, "ssdn_s")
            P.dma(ssdn[:], ssdn_d[:])
    if n_in:
        gi = P.sb([128, 8], F32, "gi")
        P.dma(gi[:], g_in[:])
        win = [P.sb([128, 8, 128], BF16, f"win{i}") for i in range(2)]
        uo = [P.sb([128, TT_TOK], F32, f"uo{i}") for i in range(2)]
    if final:
        gf = P.sb([128, 8], F32, "gf")
        P.dma(gf[:], g_fin[:])
        fo = [P.sb([128, TT_TOK], F32, f"fo{i}") for i in range(2)]

    def rmsnorm(g, outs, out_dtype_f32=False):
        for k in range(8):
            s = sq[k % 2]
            P.op('act', lambda e, s=s, k=k: e.activation(out=s.t[:], in_=h[k].t[:], func=AF.Square), outs=[s], ins=[h[k]])
            P.op('pe', lambda e, s=s, k=k: e.matmul(ps_ss.t[:], lhsT=ones.t[:], rhs=s.t[:], start=(k == 0), stop=(k == 7)),
                 outs=[ps_ss], ins=[ones, s])
        P.op('act', lambda e: e.activation(out=rstd.t[:], in_=ps_ss.t[:], func=AF.Sqrt, bias=EPS, scale=1.0 / 1024), outs=[rstd], ins=[ps_ss])
        P.op('dve', lambda e: e.reciprocal(out=rstd.t[:], in_=rstd.t[:]), outs=[rstd], ins=[rstd])
        for k in range(8):
            P.op('dve', lambda e, k=k: e.scalar_tensor_tensor(out=outs[k].t[:], in0=h[k].t[:], scalar=g.t[:, k:k + 1], in1=rstd.t[:],
                                                             op0=ALU.mult, op1=ALU.mult), outs=[outs[k]], ins=[h[k], g, rstd])

    for tt in range(ntt):
        tok = slice(tt * TT_TOK, (tt + 1) * TT_TOK)
        for k in range(8):
            P.dma(h[k][:], V(hT, hT.t[k * 128:(k + 1) * 128, tok]))
        if post:
            for k in range(8):
                P.dma(yc[k][:], V(ycT, ycT.t[k * 128:(k + 1) * 128, tok]))
            if not odd:
                for g in range(2):
                    for kk in range(2):
                        k = 2 * g + kk
                        sq_ = sq[kk]
                        P.op('act', lambda e, sq_=sq_, k=k: e.activation(out=sq_.t[:], in_=yc[k].t[:], func=AF.Square), outs=[sq_], ins=[yc[k]])
                        P.op('pe', lambda e, sq_=sq_, kk=kk: e.matmul(ps_ss.t[:], lhsT=ones.t[:], rhs=sq_.t[:], start=(kk == 0), stop=(kk == 1)),
                             outs=[ps_ss], ins=[ones, sq_])
                    P.op('act', lambda e: e.activation(out=rstd.t[:], in_=ps_ss.t[:], func=AF.Sqrt, bias=EPS, scale=1.0 / 256), outs=[rstd], ins=[ps_ss])
                    P.op('dve', lambda e: e.reciprocal(out=rstd.t[:], in_=rstd.t[:]), outs=[rstd], ins=[rstd])
                    for kk in range(2):
                        k = 2 * g + kk
                        P.op('dve', lambda e, k=k: e.scalar_tensor_tensor(out=ycb[k].t[:], in0=yc[k].t[:], scalar=ssdn.t[:, k:k + 1], in1=rstd.t[:],
                                                                         op0=ALU.mult, op1=ALU.mult), outs=[ycb[k]], ins=[yc[k], ssdn, rstd])
            for k in (range(4) if odd else range(4, 8)):
                eng = 'pool' if k % 2 else 'dve'
                P.op(eng, lambda e, k=k: e.tensor_copy(out=ycb[k].t[:], in_=yc[k].t[:]), outs=[ycb[k]], ins=[yc[k]])
            if odd:
                for k in range(4, 8):
                    P.op('pool', lambda e, k=k: e.tensor_copy(out=hn[k].t[:], in_=yc[k].t[:]), outs=[hn[k]], ins=[yc[k]])
                for j in range(4):
                    pp = next_ps()
                    for k in range(4):
                        P.op('pe', lambda e, pp=pp, j=j, k=k: e.matmul(pp.t[:], lhsT=wg.t[:, k, j * 128:(j + 1) * 128], rhs=hn[4 + k].t[:],
                                                                      start=(k == 0), stop=(k == 3)), outs=[pp], ins=[wg, hn[4 + k]])
                    gt = gate[j % 2]
                    P.op('act', lambda e, pp=pp, gt=gt, j=j: e.activation(out=gt.t[:], in_=pp.t[:], func=AF.Sigmoid, bias=gb.t[:, j:j + 1], scale=1.0),
                         outs=[gt], ins=[pp, gb])
                    P.op('dve', lambda e, gt=gt, j=j: e.tensor_tensor(out=ycb[4 + j].t[:], in0=yc[4 + j].t[:], in1=gt.t[:], op=ALU.mult),
                         outs=[ycb[4 + j]], ins=[yc[4 + j], gt])
            for j in range(8):
                pp = next_ps()
                for k in range(8):
                    P.op('pe', lambda e, pp=pp, j=j, k=k: e.matmul(pp.t[:], lhsT=wo.t[:, k, j * 128:(j + 1) * 128], rhs=ycb[k].t[:],
                                                                  start=(k == 0), stop=(k == 7)), outs=[pp], ins=[wo, ycb[k]])
                P.op('dve', lambda e, pp=pp, j=j: e.tensor_tensor(out=h[j].t[:], in0=h[j].t[:], in1=pp.t[:], op=ALU.add), outs=[h[j]], ins=[h[j], pp])
            rmsnorm(gm, hn)
            for fg in range(8):
                wb = wup[fg % 2]
                P.dma(wb[:], V(s_up, s_up.t[:, :, fg * 512:(fg + 1) * 512]))
                for fi in range(4):
                    f = fg * 4 + fi
                    pp = next_ps()
                    for k in range(8):
                        P.op('pe', lambda e, pp=pp, wb=wb, fi=fi, k=k: e.matmul(pp.t[:], lhsT=wb.t[:, k, fi * 128:(fi + 1) * 128], rhs=hn[k].t[:],
                                                                               start=(k == 0), stop=(k == 7)), outs=[pp], ins=[wb, hn[k]])
                    r = rl[f % 2]
                    P.op('act', lambda e, pp=pp, r=r: e.activation(out=r.t[:], in_=pp.t[:], func=AF.Relu), outs=[r], ins=[pp])
                    P.op('pool', lambda e, r=r, f=f: e.tensor_tensor(out=a[f].t[:], in0=r.t[:], in1=r.t[:], op=ALU.mult), outs=[a[f]], ins=[r])
            for j in range(8):
                wb = wdn[j % 2]
                P.dma(wb[:], V(s_down, s_down.t[:, :, j * 128:(j + 1) * 128]))
                pp = next_ps()
                for f in range(32):
                    P.op('pe', lambda e, pp=pp, wb=wb, f=f: e.matmul(pp.t[:], lhsT=wb.t[:, f, :], rhs=a[f].t[:], start=(f == 0), stop=(f == 31)),
                         outs=[pp], ins=[wb, a[f]])
                P.op('dve', lambda e, pp=pp, j=j: e.tensor_tensor(out=h[j].t[:], in0=h[j].t[:], in1=pp.t[:], op=ALU.add), outs=[h[j]], ins=[h[j], pp])
        if post and not final:
            for k in range(8):
                P.dma(V(hTo, hTo.t[k * 128:(k + 1) * 128, tok]), h[k][:])
        if n_in:
            rmsnorm(gi, hn)
            for o in range(n_oc):
                cw = min(128, n_in - o * 128)
                wb = win[o % 2]
                P.dma(V(wb, wb.t[:, :, 0:cw]), V(s_in, s_in.t[:, :, o * 128:o * 128 + cw]))
                pp = next_ps()
                for k in range(8):
                    P.op('pe', lambda e, pp=pp, wb=wb, k=k, cw=cw: e.matmul(pp.t[0:cw, :], lhsT=wb.t[:, k, 0:cw], rhs=hn[k].t[:],
                                                                           start=(k == 0), stop=(k == 7)), outs=[pp], ins=[wb, hn[k]])
                u = uo[o % 2]
                eng = 'act' if o % 2 else 'dve'
                if eng == 'act':
                    P.op('act', lambda e, pp=pp, u=u, cw=cw: e.copy(out=u.t[0:cw, :], in_=pp.t[0:cw, :]), outs=[u], ins=[pp])
                else:
                    P.op('dve', lambda e, pp=pp, u=u, cw=cw: e.tensor_copy(out=u.t[0:cw, :], in_=pp.t[0:cw, :]), outs=[u], ins=[pp])
                P.dma(V(uT, uT.t[o * 128:o * 128 + cw, tok]), V(u, u.t[0:cw, :]))
        if final:
            for k in range(8):
                s = sq[k % 2]
                P.op('act', lambda e, s=s, k=k: e.activation(out=s.t[:], in_=h[k].t[:], func=AF.Square), outs=[s], ins=[h[k]])
                P.op('pe', lambda e, s=s, k=k: e.matmul(ps_ss.t[:], lhsT=ones.t[:], rhs=s.t[:], start=(k == 0), stop=(k == 7)),
                     outs=[ps_ss], ins=[ones, s])
            P.op('act', lambda e: e.activation(out=rstd.t[:], in_=ps_ss.t[:], func=AF.Sqrt, bias=EPS, scale=1.0 / 1024), outs=[rstd], ins=[ps_ss])
            P.op('dve', lambda e: e.reciprocal(out=rstd.t[:], in_=rstd.t[:]), outs=[rstd], ins=[rstd])
            for k in range(8):
                o_ = fo[k % 2]
                P.op('dve', lambda e, k=k, o_=o_: e.scalar_tensor_tensor(out=o_.t[:], in0=h[k].t[:], scalar=gf.t[:, k:k + 1], in1=rstd.t[:],
                                                                        op0=ALU.mult, op1=ALU.mult), outs=[o_], ins=[h[k], gf, rstd])
                P.dma(V(hTo, hTo.t[k * 128:(k + 1) * 128, tok]), o_[:])
    return P.finish()


TWO_PI = 2 * np.pi
T5 = 512


def range_reduce(P, x, n, tmp_i, tmp_f, tmp_c):
    xs, ni, nf, c1 = x.t[:, :n], tmp_i.t[:, :n], tmp_f.t[:, :n], tmp_c.t[:, :n]
    P.op('dve', lambda e: e.tensor_scalar(out=ni, in0=xs, scalar1=1.0 / TWO_PI, scalar2=None, op0=ALU.mult), outs=[tmp_i], ins=[x])
    P.op('dve', lambda e: e.tensor_copy(out=nf, in_=ni), outs=[tmp_f], ins=[tmp_i])
    P.op('dve', lambda e: e.scalar_tensor_tensor(out=xs, in0=nf, scalar=-6.28125, in1=xs, op0=ALU.mult, op1=ALU.add), outs=[x], ins=[tmp_f, x])
    P.op('dve', lambda e: e.scalar_tensor_tensor(out=xs, in0=nf, scalar=-(TWO_PI - 6.28125), in1=xs, op0=ALU.mult, op1=ALU.add), outs=[x], ins=[tmp_f, x])
    P.op('dve', lambda e: e.tensor_single_scalar(out=c1, in_=xs, scalar=np.pi, op=ALU.is_gt), outs=[tmp_c], ins=[x])
    P.op('dve', lambda e: e.scalar_tensor_tensor(out=xs, in0=c1, scalar=-TWO_PI, in1=xs, op0=ALU.mult, op1=ALU.add), outs=[x], ins=[tmp_c, x])
    P.op('dve', lambda e: e.tensor_single_scalar(out=c1, in_=xs, scalar=-np.pi, op=ALU.is_lt), outs=[tmp_c], ins=[x])
    P.op('dve', lambda e: e.scalar_tensor_tensor(out=xs, in0=c1, scalar=TWO_PI, in1=xs, op0=ALU.mult, op1=ALU.add), outs=[x], ins=[tmp_c, x])
    P.op('dve', lambda e: e.tensor_scalar(out=xs, in0=xs, scalar1=np.pi, scalar2=-np.pi, op0=ALU.min, op1=ALU.max), outs=[x], ins=[x])


def build_odd(nq=32, nch=32):
    P = Prog()
    NQT = nq * 128
    L = nch * T5
    qT = P.din("qT", [512, NQT]); kT = P.din("kT", [128, NQT + 128]); vT = P.din("vT", [128, NQT + 128])
    mprev0 = P.din("mprev0", [128, 512]); sinks = P.din("sinks", [8])
    usT = P.din("usT", [128, L]); s5p = P.din("s5p", [128, 4, 3])
    bT = P.din("bT", [4, 128, 2, 128]); cT = P.din("cT", [4, 128, 2, 128]); s5d = P.din("s5d", [128, 1])
    ocT = P.dout("ocT", [512, NQT]); ydT = P.dout("ydT", [128, L])

    ps = [P.ps([128, 512], F32, f"ps{i}") for i in range(8)]
    io = P.sb([128, 4, 128], I32, "io")
    P.op('pool', lambda e: e.iota(io.t[:], pattern=[[0, 4], [1, 128]], base=0, channel_multiplier=-1), outs=[io])
    mcur = P.sb([128, 512], BF16, "mcur"); mprev = P.sb([128, 512], BF16, "mprev"); mp0 = P.sb([128, 512], BF16, "mp0")
    iov = io.t[:].rearrange("p h q -> p (h q)")
    P.op('dve', lambda e: e.tensor_single_scalar(out=mcur.t[:], in_=iov, scalar=0.0, op=ALU.is_ge), outs=[mcur], ins=[io])
    P.op('dve', lambda e: e.tensor_single_scalar(out=mprev.t[:], in_=iov, scalar=0.0, op=ALU.is_lt), outs=[mprev], ins=[io])
    mp0f = P.sb([128, 512], F32, "mp0f")
    P.dma(mp0f[:], mprev0[:])
    P.op('dve', lambda e: e.tensor_copy(out=mp0.t[:], in_=mp0f.t[:]), outs=[mp0], ins=[mp0f])
    ident = P.sb([128, 128], F32, "ident")
    P.op('dve', lambda e: e.tensor_single_scalar(out=ident.t[:], in_=io.t[:, 0, :], scalar=0.0, op=ALU.is_equal), outs=[ident], ins=[io])
    esink = P.sb([128, 8], F32, "esink")
    P.dma(esink[:], V(sinks, sinks.t.partition_broadcast(128)))
    P.op('act', lambda e: e.activation(out=esink.t[:], in_=esink.t[:], func=AF.Exp), outs=[esink], ins=[esink])
    zl = P.sb([1, 128], BF16, "zl"); zr_ = P.sb([1, 512], BF16, "zr_")
    P.op('pool', lambda e: e.memset(zl.t[:], 0.0), outs=[zl])
    P.op('pool', lambda e: e.memset(zr_.t[:], 0.0), outs=[zr_])

    prm = P.sb([128, 4, 3], F32, "prm")
    P.dma(prm[:], s5p[:])
    dsk = P.sb([128, 1], F32, "dsk")
    P.dma(dsk[:], s5d[:])
    bts = P.sb([128, 4, 2, 128], F32, "bts"); cts = P.sb([128, 4, 2, 128], F32, "cts")
    for i in range(4):
        P.dma(V(bts, bts.t[:, i]), bT[i])
        P.dma(V(cts, cts.t[:, i]), cT[i])
    P.op('dve', lambda e: e.tensor_scalar(out=cts.t[:, :, 1, :], in0=cts.t[:, :, 1, :], scalar1=-1.0, scalar2=None, op0=ALU.mult), outs=[cts], ins=[cts])
    sm = {n: P.sb([128, 4], F32, "sm_" + n) for n in ["step", "ars", "th", "rho", "c", "s", "xr", "xi", "den", "fr", "fi", "t1", "t2", "thc"]}
    tmp_i = P.sb([128, T5], I32, "tmp_i"); tmp_f = P.sb([128, T5], F32, "tmp_f"); tmp_c = P.sb([128, T5], F32, "tmp_c")
    are, aim, ldt = prm.t[:, :, 0], prm.t[:, :, 1], prm.t[:, :, 2]

    def tt_(eng, out, a, b, op, o_tt, ins):
        P.op(eng, lambda e: e.tensor_tensor(out=out, in0=a, in1=b, op=op), outs=[o_tt], ins=ins)
    P.op('act', lambda e: e.activation(out=sm["step"].t[:], in_=ldt, func=AF.Exp), outs=[sm["step"]], ins=[prm])
    tt_('dve', sm["ars"].t[:], are, sm["step"].t[:], ALU.mult, sm["ars"], [prm, sm["step"]])
    tt_('dve', sm["th"].t[:], aim, sm["step"].t[:], ALU.mult, sm["th"], [prm, sm["step"]])
    P.op('act', lambda e: e.activation(out=sm["rho"].t[:], in_=sm["ars"].t[:], func=AF.Exp), outs=[sm["rho"]], ins=[sm["ars"]])
    P.op('dve', lambda e: e.tensor_copy(out=sm["s"].t[:], in_=sm["th"].t[:]), outs=[sm["s"]], ins=[sm["th"]])
    range_reduce(P, sm["s"], 4, tmp_i, tmp_f, tmp_c)
    P.op('act', lambda e: e.activation(out=sm["s"].t[:], in_=sm["s"].t[:], func=AF.Sin), outs=[sm["s"]], ins=[sm["s"]])
    P.op('dve', lambda e: e.tensor_scalar(out=sm["c"].t[:], in0=sm["th"].t[:], scalar1=np.pi / 2, scalar2=None, op0=ALU.add), outs=[sm["c"]], ins=[sm["th"]])
    range_reduce(P, sm["c"], 4, tmp_i, tmp_f, tmp_c)
    P.op('act', lambda e: e.activation(out=sm["c"].t[:], in_=sm["c"].t[:], func=AF.Sin), outs=[sm["c"]], ins=[sm["c"]])
    tt_('dve', sm["xr"].t[:], sm["rho"].t[:], sm["c"].t[:], ALU.mult, sm["xr"], [sm["rho"], sm["c"]])
    P.op('dve', lambda e: e.tensor_scalar(out=sm["xr"].t[:], in0=sm["xr"].t[:], scalar1=-1.0, scalar2=None, op0=ALU.add), outs=[sm["xr"]], ins=[sm["xr"]])
    tt_('dve', sm["xi"].t[:], sm["rho"].t[:], sm["s"].t[:], ALU.mult, sm["xi"], [sm["rho"], sm["s"]])
    tt_('dve', sm["t1"].t[:], are, are, ALU.mult, sm["t1"], [prm])
    tt_('dve', sm["t2"].t[:], aim, aim, ALU.mult, sm["t2"], [prm])
    tt_('dve', sm["den"].t[:], sm["t1"].t[:], sm["t2"].t[:], ALU.add, sm["den"], [sm["t1"], sm["t2"]])
    P.op('dve', lambda e: e.reciprocal(out=sm["den"].t[:], in_=sm["den"].t[:]), outs=[sm["den"]], ins=[sm["den"]])
    tt_('dve', sm["t1"].t[:], sm["xr"].t[:], are, ALU.mult, sm["t1"], [sm["xr"], prm])
    tt_('dve', sm["t2"].t[:], sm["xi"].t[:], aim, ALU.mult, sm["t2"], [sm["xi"], prm])
    tt_('dve', sm["fr"].t[:], sm["t1"].t[:], sm["t2"].t[:], ALU.add, sm["fr"], [sm["t1"], sm["t2"]])
    tt_('dve', sm["fr"].t[:], sm["fr"].t[:], sm["den"].t[:], ALU.mult, sm["fr"], [sm["fr"], sm["den"]])
    tt_('dve', sm["t1"].t[:], sm["xi"].t[:], are, ALU.mult, sm["t1"], [sm["xi"], prm])
    tt_('dve', sm["t2"].t[:], sm["xr"].t[:], aim, ALU.mult, sm["t2"], [sm["xr"], prm])
    tt_('dve', sm["fi"].t[:], sm["t1"].t[:], sm["t2"].t[:], ALU.subtract, sm["fi"], [sm["t1"], sm["t2"]])
    tt_('dve', sm["fi"].t[:], sm["fi"].t[:], sm["den"].t[:], ALU.mult, sm["fi"], [sm["fi"], sm["den"]])
    jt_i = P.sb([128, T5], I32, "jt_i"); jt = P.sb([128, T5], F32, "jt")
    P.op('pool', lambda e: e.iota(jt_i.t[:], pattern=[[1, T5]], base=1, channel_multiplier=0), outs=[jt_i])
    P.op('dve', lambda e: e.tensor_copy(out=jt.t[:], in_=jt_i.t[:]), outs=[jt], ins=[jt_i])
    Cn = [P.sb([128, T5], F32, f"Cn{i}") for i in range(4)]; Sn = [P.sb([128, T5], F32, f"Sn{i}") for i in range(4)]
    Fr = [P.sb([128, T5], F32, f"Fr{i}") for i in range(4)]; Fi = [P.sb([128, T5], F32, f"Fi{i}") for i in range(4)]
    for i in range(4):
        th_i = sm["th"].t[:, i:i + 1]
        P.op('dve', lambda e, i=i, th_i=th_i: e.tensor_scalar(out=Sn[i].t[:], in0=jt.t[:], scalar1=th_i, scalar2=None, op0=ALU.mult), outs=[Sn[i]], ins=[jt, sm["th"]])
        P.op('dve', lambda e, i=i: e.tensor_scalar(out=Cn[i].t[:], in0=Sn[i].t[:], scalar1=np.pi / 2, scalar2=None, op0=ALU.add), outs=[Cn[i]], ins=[Sn[i]])
        range_reduce(P, Sn[i], T5, tmp_i, tmp_f, tmp_c)
        range_reduce(P, Cn[i], T5, tmp_i, tmp_f, tmp_c)
        P.op('act', lambda e, i=i: e.activation(out=Sn[i].t[:], in_=Sn[i].t[:], func=AF.Sin), outs=[Sn[i]], ins=[Sn[i]])
        P.op('act', lambda e, i=i: e.activation(out=Cn[i].t[:], in_=Cn[i].t[:], func=AF.Sin), outs=[Cn[i]], ins=[Cn[i]])
        fr_i, fi_i = sm["fr"].t[:, i:i + 1], sm["fi"].t[:, i:i + 1]
        P.op('dve', lambda e, i=i, fi_i=fi_i: e.tensor_scalar(out=tmp_f.t[:], in0=Sn[i].t[:], scalar1=fi_i, scalar2=None, op0=ALU.mult), outs=[tmp_f], ins=[Sn[i], sm["fi"]])
        P.op('dve', lambda e, i=i, fr_i=fr_i: e.scalar_tensor_tensor(out=Fr[i].t[:], in0=Cn[i].t[:], scalar=fr_i, in1=tmp_f.t[:], op0=ALU.mult, op1=ALU.add),
             outs=[Fr[i]], ins=[Cn[i], sm["fr"], tmp_f])
        P.op('dve', lambda e, i=i, fr_i=fr_i: e.tensor_scalar(out=tmp_f.t[:], in0=Sn[i].t[:], scalar1=fr_i, scalar2=None, op0=ALU.mult), outs=[tmp_f], ins=[Sn[i], sm["fr"]])
        P.op('dve', lambda e, i=i, fi_i=fi_i: e.scalar_tensor_tensor(out=Fi[i].t[:], in0=Cn[i].t[:], scalar=fi_i, in1=tmp_f.t[:], op0=ALU.mult, op1=ALU.subtract),
             outs=[Fi[i]], ins=[Cn[i], sm["fi"], tmp_f])
    us = [P.sb([128, T5], F32, f"us{i}") for i in range(2)]
    m = [[P.sb([128, T5], F32, f"m{s}_{j}") for j in range(4)] for s in range(2)]
    zin = [[P.sb([128, T5], F32, f"zin{s}_{j}") for j in range(2)] for s in range(2)]
    zz = [[P.sb([128, T5], F32, f"zz{s}_{j}") for j in range(2)] for s in range(2)]
    nn = [[P.sb([128, T5], F32, f"nn{s}_{j}") for j in range(4)] for s in range(2)]
    xst = [[P.sb([128, T5], F32, f"xst{i}_{j}") for j in range(2)] for i in range(4)]
    yv = [P.sb([128, T5], F32, f"yv{i}") for i in range(2)]
    for c in range(nch):
        u = us[c % 2]
        P.dma(u[:], V(usT, usT.t[:, c * T5:(c + 1) * T5]))
        yps = ps[4 + c % 2]
        for i in range(4):
            s = i % 2
            brp, bip = ps[2 * s], ps[2 * s + 1]
            P.op('pe', lambda e, brp=brp, i=i, u=u: e.matmul(brp.t[:], lhsT=bts.t[:, i, 0, :], rhs=u.t[:], start=True, stop=True), outs=[brp], ins=[bts, u])
            P.op('pe', lambda e, bip=bip, i=i, u=u: e.matmul(bip.t[:], lhsT=bts.t[:, i, 1, :], rhs=u.t[:], start=True, stop=True), outs=[bip], ins=[bts, u])
            mm_ = m[s]
            tt_('dve', mm_[0].t[:], brp.t[:], Fr[i].t[:], ALU.mult, mm_[0], [brp, Fr[i]])
            tt_('dve', mm_[1].t[:], bip.t[:], Fi[i].t[:], ALU.mult, mm_[1], [bip, Fi[i]])
            tt_('dve', mm_[2].t[:], brp.t[:], Fi[i].t[:], ALU.mult, mm_[2], [brp, Fi[i]])
            tt_('dve', mm_[3].t[:], bip.t[:], Fr[i].t[:], ALU.mult, mm_[3], [bip, Fr[i]])
            tt_('pool', zin[s][0].t[:], mm_[0].t[:], mm_[1].t[:], ALU.subtract, zin[s][0], [mm_[0], mm_[1]])
            tt_('pool', zin[s][1].t[:], mm_[2].t[:], mm_[3].t[:], ALU.add, zin[s][1], [mm_[2], mm_[3]])
            rho_b = sm["rho"].t[:, i:i + 1].to_broadcast([128, T5])
            for j in range(2):
                init = 0.0 if c == 0 else xst[i][j].t[:, T5 - 1:T5]
                ins_ = [sm["rho"], zin[s][j]] + ([] if c == 0 else [xst[i][j]])
                P.op('dve', lambda e, s=s, j=j, rho_b=rho_b, init=init: e.tensor_tensor_scan(out=zz[s][j].t[:], data0=rho_b, data1=zin[s][j].t[:], initial=init,
                                                                                              op0=ALU.mult, op1=ALU.add), outs=[zz[s][j]], ins=ins_)
            n_ = nn[s]
            tt_('pool', n_[0].t[:], zz[s][0].t[:], Cn[i].t[:], ALU.mult, n_[0], [zz[s][0], Cn[i]])
            tt_('pool', n_[1].t[:], zz[s][1].t[:], Sn[i].t[:], ALU.mult, n_[1], [zz[s][1], Sn[i]])
            tt_('pool', n_[2].t[:], zz[s][0].t[:], Sn[i].t[:], ALU.mult, n_[2], [zz[s][0], Sn[i]])
            tt_('pool', n_[3].t[:], zz[s][1].t[:], Cn[i].t[:], ALU.mult, n_[3], [zz[s][1], Cn[i]])
            tt_('pool', xst[i][0].t[:], n_[0].t[:], n_[1].t[:], ALU.subtract, xst[i][0], [n_[0], n_[1]])
            tt_('pool', xst[i][1].t[:], n_[2].t[:], n_[3].t[:], ALU.add, xst[i][1], [n_[2], n_[3]])
            P.op('pe', lambda e, yps=yps, i=i: e.matmul(yps.t[:], lhsT=cts.t[:, i, 0, :], rhs=xst[i][0].t[:], start=(i == 0), stop=False), outs=[yps], ins=[cts, xst[i][0]])
            P.op('pe', lambda e, yps=yps, i=i: e.matmul(yps.t[:], lhsT=cts.t[:, i, 1, :], rhs=xst[i][1].t[:], start=False, stop=(i == 3)), outs=[yps], ins=[cts, xst[i][1]])
        y_ = yv[c % 2]
        P.op('dve', lambda e, y_=y_, u=u, yps=yps: e.scalar_tensor_tensor(out=y_.t[:], in0=u.t[:], scalar=dsk.t[:, 0:1], in1=yps.t[:], op0=ALU.mult, op1=ALU.add),
             outs=[y_], ins=[u, dsk, yps])
        P.op('act', lambda e, y_=y_: e.activation(out=y_.t[:], in_=y_.t[:], func=AF.Gelu_apprx_tanh), outs=[y_], ins=[y_])
        P.dma(V(ydT, ydT.t[:, c * T5:(c + 1) * T5]), y_[:])

    NR = 3
    kf = [P.sb([64, 2, 128], F32, f"kf{i}") for i in range(NR)]
    kb = [P.sb([64, 2, 128], BF16, f"kb{i}") for i in range(NR)]
    vf = [P.sb([128, 128], F32, f"vf{i}") for i in range(NR)]
    vb = [P.sb([128, 2, 65], BF16, f"vb{i}") for i in range(NR)]
    for i in range(NR):
        P.op('pool', lambda e, i=i: e.memset(vb[i].t[:], 1.0), outs=[vb[i]])
    qf = [P.sb([64, 8, 128], F32, f"qf{i}") for i in range(2)]
    qb = [P.sb([64, 8, 128], BF16, f"qb{i}") for i in range(2)]
    ex = [P.sb([128, 512], BF16, f"ex{i}") for i in range(2)]
    pm = [P.sb([128, 512], BF16, f"pm{i}") for i in range(4)]
    lsum = P.sb([128, 4], F32, "lsum")
    osb = [P.sb([128, 512], F32, f"osb{i}") for i in range(2)]
    otr = [P.sb([128, 128], F32, f"otr{i}") for i in range(2)]
    pmi = [0]

    def load_kv(j):
        r = j % NR
        P.dma(kf[r][:], V(kT, kT.t[:, j * 128:(j + 1) * 128].rearrange("(g d) k -> d g k", g=2)))
        P.op('pool', lambda e: e.tensor_copy(out=kb[r].t[:], in_=kf[r].t[:]), outs=[kb[r]], ins=[kf[r]])
        P.dma(vf[r][:], V(vT, vT.t[:, j * 128:(j + 1) * 128]))
        tp = ps[6]
        P.op('pe', lambda e: e.transpose(out=tp.t[:, 0:128], in_=vf[r].t[:], identity=ident.t[:]), outs=[tp], ins=[vf[r], ident])
        P.op('dve', lambda e: e.tensor_copy(out=vb[r].t[:, :, 0:64], in_=tp.t[:, 0:128].rearrange("k (g d) -> k g d", g=2)), outs=[vb[r]], ins=[tp])

    load_kv(0)
    for t in range(nq):
        load_kv(t + 1)
        qf_, qb_ = qf[t % 2], qb[t % 2]
        P.dma(qf_[:], V(qT, qT.t[:, t * 128:(t + 1) * 128].rearrange("(h d) q -> d h q", h=8)))
        P.op('pool', lambda e, qf_=qf_, qb_=qb_: e.tensor_copy(out=qb_.t[:], in_=qf_.t[:]), outs=[qb_], ins=[qf_])
        o_ = osb[t % 2]
        for g in range(2):
            ops_ = ps[7]
            P.op('pe', lambda e, ops_=ops_: e.matmul(ops_.t[:, 0:260], lhsT=zl.t[:], rhs=zr_.t[:, 0:260], start=True, stop=False), outs=[ops_], ins=[zl, zr_])
            for which in range(2):
                r = (t + which) % NR
                sp = ps[which]
                P.op('pe', lambda e, sp=sp, r=r, g=g, qb_=qb_: e.matmul(sp.t[:], lhsT=kb[r].t[:, g, :], rhs=qb_.t[:, 4 * g:4 * g + 4, :].rearrange("d h q -> d (h q)"),
                                                                      start=True, stop=True), outs=[sp], ins=[kb[r], qb_])
                e_ = ex[which]
                P.op('act', lambda e, sp=sp, e_=e_: e.activation(out=e_.t[:], in_=sp.t[:], func=AF.Exp, scale=0.125), outs=[e_], ins=[sp])
                mk = mcur if which == 1 else (mp0 if t == 0 else mprev)
                p_ = pm[pmi[0] % 4]; pmi[0] += 1
                P.op('dve', lambda e, p_=p_, e_=e_, mk=mk: e.tensor_tensor(out=p_.t[:], in0=e_.t[:], in1=mk.t[:], op=ALU.mult), outs=[p_], ins=[e_, mk])
                for h in range(4):
                    last = (which == 1 and h == 3)
                    P.op('pe', lambda e, ops_=ops_, p_=p_, r=r, g=g, h=h, last=last: e.matmul(ops_.t[:, h * 65:(h + 1) * 65], lhsT=p_.t[:, h * 128:(h + 1) * 128], rhs=vb[r].t[:, g, :],
                                                                                            start=False, stop=last), outs=[ops_], ins=[p_, vb[r]])
            ov = ops_.t[:, 0:260].rearrange("q (h e) -> q h e", h=4)
            P.op('dve', lambda e, ov=ov, g=g: e.tensor_tensor(out=lsum.t[:], in0=ov[:, :, 64], in1=esink.t[:, 4 * g:4 * g + 4], op=ALU.add), outs=[lsum], ins=[ops_, esink])
            P.op('dve', lambda e: e.reciprocal(out=lsum.t[:], in_=lsum.t[:]), outs=[lsum], ins=[lsum])
            for h in range(4):
                P.op('dve', lambda e, ov=ov, o_=o_, g=g, h=h: e.tensor_scalar(out=o_.t[:, (4 * g + h) * 64:(4 * g + h + 1) * 64], in0=ov[:, h, 0:64], scalar1=lsum.t[:, h:h + 1],
                                                                           scalar2=None, op0=ALU.mult), outs=[o_], ins=[ops_, lsum])
        for cc in range(4):
            tp = ps[6]
            P.op('pe', lambda e, tp=tp, o_=o_, cc=cc: e.transpose(out=tp.t[:, 0:128], in_=o_.t[:, cc * 128:(cc + 1) * 128], identity=ident.t[:]), outs=[tp], ins=[o_, ident])
            ot = otr[cc % 2]
            P.op('act', lambda e, tp=tp, ot=ot: e.copy(out=ot.t[:], in_=tp.t[:, 0:128]), outs=[ot], ins=[tp])
            P.dma(V(ocT, ocT.t[cc * 128:(cc + 1) * 128, t * 128:(t + 1) * 128]), ot[:])
    return P.finish()


VSC = 2048


def nsa_phase(P, L, ident, nsa_in, onT):
    qT4, kcT, vcT, ksT, vsT, kwT, vwT, gtT, w1d, peT, b1d, w2d, b2k, b2v = nsa_in
    NQ = L // 128
    NC = L // 16 - 1
    NCT = (NC + 1 + 127) // 128
    NCp = NCT * 128
    NJ = L // 64
    NJC = (NJ + 127) // 128
    JW = NJC * 128
    kc_bf = P.sb([64, NCp], F32, "n_kc"); vc = P.sb([128, NCT, 65], F32, "n_vc")
    P.memset(kc_bf, kc_bf.t[:], 0.0); P.memset(vc, vc.t[:], 1.0)
    with P.scope():
        psB = [P.ps([128, 512], F32, f"nB_ps{i}") for i in range(3)]
        w1 = P.sb([64, 32, 256], F32, "nB_w1"); hid = P.sb([128, 2, NCp], F32, "nB_hid")
        kraw = P.sb([64, 512 * 16 + 16], F32, "nB_kraw")
        pes = P.sb([64, 2, 32], F32, "nB_pe"); b1s = P.sb([128, 2, 2], F32, "nB_b1"); w2s = P.sb([128, 2, 2, 64], F32, "nB_w2")
        b2ks = P.sb([64, 1], F32, "nB_b2k"); b2vs = P.sb([128, 64], F32, "nB_b2v"); hidb = P.sb([128, 2], F32, "nB_hidb")
        P.dma(pes[:], peT[:]); P.dma(b1s[:], b1d[:]); P.dma(w2s[:], w2d[:]); P.dma(b2ks[:], b2k[:])
        P.dma(b2vs[:], V(b2v, b2v.t.partition_broadcast(128)))
        P.memset(hid, hid.t[:], 0.0)
        for kv in range(2):
            src = kcT if kv == 0 else vcT
            for jj in range(4):
                P.dma(V(w1, w1.t[:, jj * 8:(jj + 1) * 8, :]), V(w1d, w1d.t[kv, :, jj * 8:(jj + 1) * 8, :]))
            for hc in range(2):
                pp = psB[2]
                for j in range(32):
                    P.mm(pp, pp.t[:, 0:1], w1.t[:, j, hc * 128:(hc + 1) * 128], pes.t[:, kv, j:j + 1], j == 0, j == 31, [w1, pes])
                P.tt('dve', hidb, hidb.t[:, hc:hc + 1], pp.t[:, 0:1], b1s.t[:, kv, hc:hc + 1], ALU.add, [pp, b1s])
            for c0 in range(0, NC, 512):
                n = min(512, NC - c0)
                P.dma(V(kraw, kraw.t[:, 0:16 * n + 16]), V(src, src.t[:, 16 * c0:16 * c0 + 16 * n + 16]))
                for hc in range(2):
                    pp = psB[hc]
                    for j in range(32):
                        P.mm(pp, pp.t[:, 0:n], w1.t[:, j, hc * 128:(hc + 1) * 128], kraw.t[:, j:j + 16 * (n - 1) + 1:16], j == 0, j == 31, [w1, kraw])
                    P.act(hid, hid.t[:, hc, c0:c0 + n], pp.t[:, 0:n], AF.Gelu_apprx_tanh, [pp, hidb], bias=hidb.t[:, hc:hc + 1], scale=1.0)
            if kv == 0:
                for c0 in range(0, NC, 512):
                    n = min(512, NC - c0)
                    pp = psB[2]
                    for hc in range(2):
                        P.mm(pp, pp.t[0:64, 0:n], w2s.t[:, 0, hc, :], hid.t[:, hc, c0:c0 + n], hc == 0, hc == 1, [w2s, hid])
                    P.act(kc_bf, kc_bf.t[:, c0:c0 + n], pp.t[0:64, 0:n], AF.Identity, [pp, b2ks], bias=b2ks.t[:, 0:1], scale=1.0)
            else:
                for ct in range(NCT):
                    pp = psB[2]
                    for hc in range(2):
                        P.mm(pp, pp.t[:, 0:64], hid.t[:, hc, ct * 128:(ct + 1) * 128], w2s.t[:, 1, hc, :], hc == 0, hc == 1, [hid, w2s])
                    P.tt('dve', vc, vc.t[:, ct, 0:64], pp.t[:, 0:64], b2vs.t[:], ALU.add, [pp, b2vs])
    with P.scope():
        ks_bf = P.sb([64, L], BF16, "n_ks"); kw_bf = P.sb([64, L], BF16, "n_kw")
        vs = P.sb([128, NQ, 65], BF16, "n_vs"); vw = P.sb([128, NQ, 65], BF16, "n_vw")
        P.memset(vs, vs.t[:], 1.0); P.memset(vw, vw.t[:], 1.0)
        WEXP = min(64, NQ) * 128
        wexp = P.sb([128, WEXP], BF16, "n_wexp")
        S_ps = [P.ps([128, 512], F32, f"nC_S{i}") for i in range(2)]
        Oc_ps = P.ps([128, 512], F32, "nC_Oc"); Os_ps = P.ps([128, 512], F32, "nC_Os"); Ow_ps = P.ps([128, 512], F32, "nC_Ow")
        imp_ps = P.ps([128, 1024], F32, "nC_imp"); tp_ps = P.ps([128, 512], F32, "nC_tp")
        with P.scope():
            iw = P.sb([128, 2048], I32, "nc_iw"); wa = P.sb([128, 2048], F32, "nc_wa"); wb_ = P.sb([128, 2048], F32, "nc_wb")
            for pc in range(WEXP // 2048 if WEXP >= 2048 else 1):
                w = min(2048, WEXP)
                P.op('pool', lambda e, pc=pc, w=w: e.iota(iw.t[:, 0:w], pattern=[[1, w]], base=2048 * pc, channel_multiplier=-64), outs=[iw])
                P.ts('dve', wa, wa.t[:, 0:w], iw.t[:, 0:w], 0.0, ALU.is_ge, [iw])
                P.ts('dve', wb_, wb_.t[:, 0:w], iw.t[:, 0:w], 63.0, ALU.is_le, [iw])
                P.tt('dve', wexp, wexp.t[:, 2048 * pc:2048 * pc + w], wa.t[:, 0:w], wb_.t[:, 0:w], ALU.mult, [wa, wb_])
            for (src, dst) in ((ksT, ks_bf), (kwT, kw_bf)):
                for c0 in range(0, L, 8192):
                    w = min(8192, L - c0)
                    P.op('pool', lambda e, src=src, dst=dst, c0=c0, w=w: e.dma_start(out=dst.t[:, c0:c0 + w], in_=src.t[:, c0:c0 + w], max_dma_last_dim=4096),
                         outs=[dst], ins=[src], ctr='dpool')
            vraw = P.sb([64, VSC], F32, "nc_vraw")
            for (src, dst) in ((vsT, vs), (vwT, vw)):
                for s0 in range(0, L, VSC):
                    P.dma(vraw[:], V(src, src.t[:, s0:s0 + VSC]))
                    for c in range(VSC // 128):
                        kt = s0 // 128 + c
                        P.tr(tp_ps, tp_ps.t[:, 0:64], vraw.t[:, c * 128:(c + 1) * 128], ident.t[0:64, 0:64], [vraw, ident])
                        P.cp('act' if c % 2 else 'dve', dst, dst.t[:, kt, 0:64], tp_ps.t[:, 0:64], [tp_ps])
        ioA = P.sb([128, 2, 128], I32, "n_ioA")
        P.op('pool', lambda e: e.iota(ioA.t[:], pattern=[[0, 2], [1, 128]], base=0, channel_multiplier=-1), outs=[ioA])
        mcur = P.sb([128, 2, 128], BF16, "n_mcur"); mprev = P.sb([128, 2, 128], BF16, "n_mprev")
        P.ts('dve', mcur, mcur.t[:], ioA.t[:], 0.0, ALU.is_ge, [ioA])
        P.ts('dve', mprev, mprev.t[:], ioA.t[:], 0.0, ALU.is_lt, [ioA])
        ioR = P.sb([128, 128], I32, "n_ioR"); Rt = P.sb([128, 128], F32, "n_R")
        P.op('pool', lambda e: e.iota(ioR.t[:], pattern=[[1, 128]], base=0, channel_multiplier=-16), outs=[ioR])
        P.cp('dve', Rt, Rt.t[:], ioR.t[:], [ioR])
        ioO = P.sb([128, 33], I32, "n_ioO"); ova = P.sb([128, 33], F32, "n_ova"); OVt = P.sb([128, 33], F32, "n_OVt")
        P.op('pool', lambda e: e.iota(ioO.t[:], pattern=[[-4, 33]], base=0, channel_multiplier=1), outs=[ioO])
        P.ts('dve', ova, ova.t[:], ioO.t[:], -1.0, ALU.is_ge, [ioO])
        P.ts('dve', OVt, OVt.t[:], ioO.t[:], 3.0, ALU.is_le, [ioO])
        P.tt('dve', OVt, OVt.t[:], OVt.t[:], ova.t[:], ALU.mult, [OVt, ova])
        ioJ = P.sb([128, JW], I32, "n_ioJ"); ioP = P.sb([128, 1], I32, "n_ioP"); Jt = P.sb([128, JW], F32, "n_J"); pge = P.sb([128, 1], F32, "n_pge")
        P.op('pool', lambda e: e.iota(ioJ.t[:], pattern=[[1, JW]], base=0, channel_multiplier=0), outs=[ioJ])
        P.op('pool', lambda e: e.iota(ioP.t[:], pattern=[[0, 1]], base=0, channel_multiplier=1), outs=[ioP])
        P.ts('dve', pge, pge.t[:], ioP.t[:], 64.0, ALU.is_ge, [ioP])
        P.ts('dve', Jt, Jt.t[:], ioJ.t[:], pge.t[:, 0:1], ALU.subtract, [ioJ, pge])
        zl = P.sb([1, 128], F32, "n_zl"); zr = P.sb([1, 512], F32, "n_zr"); zlb = P.sb([1, 128], BF16, "n_zlb"); zrb = P.sb([1, 512], BF16, "n_zrb")
        for t_ in (zl, zr, zlb, zrb):
            P.memset(t_, t_.t[:], 0.0)
        q4f = [P.sb([64, 4, 128], F32, f"n_q4f{i}") for i in range(2)]; q4b = [P.sb([64, 4, 128], BF16, f"n_q4b{i}") for i in range(2)]
        gtf = P.sb([6, 128], F32, "n_gtf"); gs = P.sb([128, 6], F32, "n_gs")
        Ef = [P.sb([128, 512], F32, f"n_Ef{i}") for i in range(2)]; cmk = P.sb([128, 128], F32, "n_cmk")
        Eb = [P.sb([128, 256], BF16, f"n_Eb{i}") for i in range(3)]
        rl4 = P.sb([128, 4], F32, "n_rl4"); rl2 = P.sb([128, 2], F32, "n_rl2"); coef = P.sb([128, 2], F32, "n_coef")
        imp = P.sb([128, JW], F32, "n_imp"); imp2 = P.sb([128, JW], F32, "n_imp2"); fbA = P.sb([128, JW], F32, "n_fbA")
        m8a = P.sb([128, 8], F32, "n_m8a"); m8b = P.sb([128, 8], F32, "n_m8b")
        nb = P.sb([128, JW], F32, "n_nb"); nbT = [P.sb([128, 128], BF16, f"n_nbT{i}") for i in range(NJC)]
        acc = P.sb([128, 128], F32, "n_acc"); accT = [P.sb([128, 128], F32, f"n_accT{i}") for i in range(2)]
        ebi = [0]

        def next_eb():
            ebi[0] += 1
            return Eb[ebi[0] % 3]

        def combine(O_ps, br, first):
            ov = O_ps.t[:, 0:130].rearrange("q (h e) -> q h e", h=2)
            P.ts('dve', rl2, rl2.t[:], ov[:, :, 64], 1e-30, ALU.max, [O_ps])
            P.op('dve', lambda e: e.reciprocal(out=rl2.t[:], in_=rl2.t[:]), outs=[rl2], ins=[rl2])
            P.tt('dve', coef, coef.t[:], rl2.t[:], gs.t[:, :].rearrange("q (h b) -> q h b", h=2)[:, :, br], ALU.mult, [rl2, gs])
            for hh in range(2):
                if first:
                    P.ts('dve', acc, acc.t[:, hh * 64:(hh + 1) * 64], ov[:, hh, 0:64], coef.t[:, hh:hh + 1], ALU.mult, [O_ps, coef])
                else:
                    P.stt(acc, acc.t[:, hh * 64:(hh + 1) * 64], ov[:, hh, 0:64], coef.t[:, hh:hh + 1], acc.t[:, hh * 64:(hh + 1) * 64], ALU.mult, ALU.add,
                          [O_ps, coef, acc])

        for qt in range(NQ):
            tok = slice(qt * 128, (qt + 1) * 128)
            qf, qb = q4f[qt % 2], q4b[qt % 2]
            P.dma(qf[:], V(qT4, qT4.t[:, tok].rearrange("(h d) q -> d h q", h=4)))
            P.cp('pool', qb, qb.t[:], qf.t[:], [qf])
            P.dma(gtf[:], V(gtT, gtT.t[:, tok]))
            P.tr(tp_ps, tp_ps.t[:, 0:6], gtf.t[:], ident.t[0:6, 0:6], [gtf, ident])
            P.act(gs, gs.t[:], tp_ps.t[:, 0:6], AF.Sigmoid, [tp_ps])
            q4v = qb.t[:].rearrange("d h q -> d (h q)")
            qov = qb.t[:, 0:2, :].rearrange("d h q -> d (h q)")
            P.mm(Oc_ps, Oc_ps.t[:, 0:260], zl.t[:], zr.t[:, 0:260], True, False, [zl, zr])
            P.mm(imp_ps, imp_ps.t[:, 0:512], zl.t[:], zr.t[:, 0:512], True, False, [zl, zr])
            P.mm(imp_ps, imp_ps.t[:, 512:1024], zl.t[:], zr.t[:, 0:512], True, False, [zl, zr])
            nct = (8 * qt + 6) // 128 + 1
            for ct in range(nct):
                sp = S_ps[ct % 2]
                P.mm(sp, sp.t[:], kc_bf.t[:, ct * 128:(ct + 1) * 128], qf.t[:].rearrange("d h q -> d (h q)"), True, True, [kc_bf, qf])
                ef = Ef[ct % 2]
                P.act(ef, ef.t[:], sp.t[:], AF.Exp, [sp], scale=0.125)
                thr = 2048 * ct + 31 - 128 * qt
                if thr > -2032:
                    P.ts('dve', cmk, cmk.t[:], Rt.t[:], float(thr), ALU.is_ge, [Rt])
                    P.tt('dve', ef, ef.t[:].rearrange("c (h q) -> c h q", h=4), ef.t[:].rearrange("c (h q) -> c h q", h=4),
                         cmk.t[:, :].unsqueeze(1).to_broadcast([128, 4, 128]), ALU.mult, [ef, cmk])
                last = ct == nct - 1
                for h in range(4):
                    P.mm(Oc_ps, Oc_ps.t[:, h * 65:(h + 1) * 65], ef.t[:, h * 128:(h + 1) * 128], vc.t[:, ct, :], False, last and h == 3, [ef, vc])
                ncol = min(33, NJ - 32 * ct)
                for h in range(4):
                    P.mm(imp_ps, imp_ps.t[:, h * 256 + 32 * ct:h * 256 + 32 * ct + ncol], ef.t[:, h * 128:(h + 1) * 128], OVt.t[:, 0:ncol], False, last and h in (1, 3),
                         [ef, OVt])
            ocv = Oc_ps.t[:, 0:260].rearrange("q (h e) -> q h e", h=4)
            P.ts('dve', rl4, rl4.t[:], ocv[:, :, 64], 1e-30, ALU.max, [Oc_ps])
            P.op('dve', lambda e: e.reciprocal(out=rl4.t[:], in_=rl4.t[:]), outs=[rl4], ins=[rl4])
            P.ts('dve', imp, imp.t[:, 0:JW], imp_ps.t[:, 0:JW], rl4.t[:, 0:1], ALU.mult, [imp_ps, rl4])
            for h in range(1, 4):
                P.stt(imp, imp.t[:, 0:JW], imp_ps.t[:, h * 256:h * 256 + JW], rl4.t[:, h:h + 1], imp.t[:, 0:JW], ALU.mult, ALU.add, [imp_ps, rl4, imp])
            combine(Oc_ps, 0, True)
            P.ts('dve', fbA, fbA.t[:], Jt.t[:], float(2 * qt - 1), ALU.is_ge, [Jt], s2=1e9, op1=ALU.mult)
            P.tt('dve', imp2, imp2.t[:], imp.t[:], fbA.t[:], ALU.add, [imp, fbA])
            P.ts('dve', fbA, fbA.t[:], Jt.t[:], float(2 * qt), ALU.is_gt, [Jt], s2=-2e9, op1=ALU.mult)
            P.tt('dve', imp2, imp2.t[:], imp2.t[:], fbA.t[:], ALU.add, [imp2, fbA])
            P.memset(imp2, imp2.t[:, 0:1], 1e9, eng='dve')
            P.op('dve', lambda e: e.max(out=m8a.t[:], in_=imp2.t[:]), outs=[m8a], ins=[imp2])
            P.op('dve', lambda e: e.match_replace(out=imp.t[:], in_to_replace=m8a.t[:], in_values=imp2.t[:], imm_value=-3e9), outs=[imp], ins=[m8a, imp2])
            P.op('dve', lambda e: e.max(out=m8b.t[:], in_=imp.t[:]), outs=[m8b], ins=[imp])
            P.ts('dve', nb, nb.t[:], imp2.t[:], m8b.t[:, 7:8], ALU.is_ge, [imp2, m8b])
            P.ts('dve', nb, nb.t[:], nb.t[:], -NEGB, ALU.mult, [nb], s2=NEGB, op1=ALU.add)
            njc = (2 * qt + 1) // 128 + 1
            for jc in range(njc):
                P.tr(tp_ps, tp_ps.t[:, 0:128], nb.t[:, jc * 128:(jc + 1) * 128], ident.t[:], [nb, ident])
                P.cp('act', nbT[jc], nbT[jc].t[:], tp_ps.t[:, 0:128], [tp_ps])
            P.mm(Os_ps, Os_ps.t[:, 0:130], zlb.t[:], zrb.t[:, 0:130], True, False, [zlb, zrb])
            for kt in range(qt + 1):
                sp = S_ps[kt % 2]
                P.mm(sp, sp.t[:, 0:256], ks_bf.t[:, kt * 128:(kt + 1) * 128], qov, True, False, [ks_bf, qb])
                jc = kt // 64
                P.mm(sp, sp.t[:, 0:256], wexp.t[:, 128 * (kt % 64):128 * (kt % 64) + 128], nbT[jc].t[:, :].unsqueeze(1).to_broadcast([128, 2, 128]),
                     False, True, [wexp, nbT[jc]])
                eb = next_eb()
                P.act(eb, eb.t[:], sp.t[:, 0:256], AF.Exp, [sp], scale=0.125)
                if kt == qt:
                    P.tt('dve', eb, eb.t[:], eb.t[:], mcur.t[:].rearrange("k h q -> k (h q)"), ALU.mult, [eb, mcur])
                for hh in range(2):
                    P.mm(Os_ps, Os_ps.t[:, hh * 65:(hh + 1) * 65], eb.t[:, hh * 128:(hh + 1) * 128], vs.t[:, kt, :], False, kt == qt and hh == 1, [eb, vs])
            combine(Os_ps, 1, False)
            P.mm(Ow_ps, Ow_ps.t[:, 0:130], zlb.t[:], zrb.t[:, 0:130], True, False, [zlb, zrb])
            for kt in range(max(0, qt - 4), qt + 1):
                sp = S_ps[kt % 2]
                P.mm(sp, sp.t[:, 0:256], kw_bf.t[:, kt * 128:(kt + 1) * 128], qov, True, True, [kw_bf, qb])
                eb = next_eb()
                P.act(eb, eb.t[:], sp.t[:, 0:256], AF.Exp, [sp], scale=0.125)
                if kt == qt:
                    P.tt('dve', eb, eb.t[:], eb.t[:], mcur.t[:].rearrange("k h q -> k (h q)"), ALU.mult, [eb, mcur])
                if kt == qt - 4:
                    P.tt('dve', eb, eb.t[:], eb.t[:], mprev.t[:].rearrange("k h q -> k (h q)"), ALU.mult, [eb, mprev])
                for hh in range(2):
                    P.mm(Ow_ps, Ow_ps.t[:, hh * 65:(hh + 1) * 65], eb.t[:, hh * 128:(hh + 1) * 128], vw.t[:, kt, :], False, kt == qt and hh == 1, [eb, vw])
            combine(Ow_ps, 2, False)
            P.tr(tp_ps, tp_ps.t[:, 0:128], acc.t[:], ident.t[:], [acc, ident])
            at = accT[qt % 2]
            P.cp('act', at, at.t[:], tp_ps.t[:, 0:128], [tp_ps])
            P.dma(V(onT, onT.t[:, tok]), at[:])


SC = 2048


def ssd_phase(P, L, ps, ident, ssd_in, ygT):
    zT, xT, bT_, cT_, dtT, cw, cb, hp, dsk_d = ssd_in
    NSC = L // SC
    io = P.sb([128, 128], I32, "s_io")
    P.op('pool', lambda e: e.iota(io.t[:], pattern=[[1, 128]], base=0, channel_multiplier=-1), outs=[io])
    negm = P.sb([128, 128], F32, "s_negm")
    P.ts('dve', negm, negm.t[:], io.t[:], 0.0, ALU.is_ge, [io], s2=None)
    P.ts('dve', negm, negm.t[:], negm.t[:], -NEGB, ALU.mult, [negm], s2=NEGB, op1=ALU.add)
    io2 = P.sb([2, SC // 128, 128], I32, "s_io2")
    P.op('pool', lambda e: e.iota(io2.t[:], pattern=[[0, SC // 128], [1, 128]], base=0, channel_multiplier=0), outs=[io2])
    rmask = P.sb([2, SC], F32, "s_rmask")
    P.ts('dve', rmask, rmask.t[:], io2.t[:].rearrange("p a b -> p (a b)"), 0.0, ALU.is_gt, [io2])
    io3 = P.sb([2, 2, 128], I32, "s_io3")
    P.op('pool', lambda e: e.iota(io3.t[:], pattern=[[1, 2], [0, 128]], base=0, channel_multiplier=-1), outs=[io3])
    sel = P.sb([2, 2, 128], F32, "s_sel")
    P.ts('dve', sel, sel.t[:], io3.t[:], 0.0, ALU.is_equal, [io3])
    cws = P.sb([128, 3, 4], F32, "s_cw"); cbs = P.sb([128, 3], F32, "s_cb"); hps = P.sb([2, 3], F32, "s_hp"); dsk = P.sb([128, 1], F32, "s_dsk")
    P.dma(cws[:], cw[:]); P.dma(cbs[:], cb[:]); P.dma(hps[:], hp[:]); P.dma(dsk[:], dsk_d[:])
    na = P.sb([2, 1], F32, "s_na")
    P.act(na, na.t[:], hps.t[:, 1:2], AF.Exp, [hps])
    P.ts('dve', na, na.t[:], na.t[:], -1.0, ALU.mult, [na])
    hf = P.sb([128, 128], F32, "s_hf"); hb = P.sb([128, 128], BF16, "s_hb")
    P.memset(hf, hf.t[:], 0.0); P.memset(hb, hb.t[:], 0.0)
    raw = P.sb([128, SC + 3], F32, "s_raw"); acc = P.sb([128, SC], F32, "s_acc")
    xs = P.sb([128, SC], F32, "s_xs"); Bs = P.sb([128, SC], BF16, "s_Bs"); Cs = P.sb([128, SC], BF16, "s_Cs")
    zs = P.sb([128, SC], F32, "s_zs"); ysc = P.sb([128, SC], F32, "s_ysc")
    dtr = P.sb([2, SC], F32, "s_dtr"); dts = P.sb([2, SC], F32, "s_dt"); acum = P.sb([2, SC], F32, "s_acum")
    small = P.sb([128, 4], F32, "s_small"); Btok = P.sb([128, 128], BF16, "s_Btok")
    Dsb = [P.sb([128, 128], F32, f"s_D{r}") for r in range(2)]
    Esb = [P.sb([128, 128], F32, f"s_E{r}") for r in range(2)]
    Msb = [P.sb([128, 128], BF16, f"s_M{r}") for r in range(2)]
    EBs = [P.sb([128, 128], F32, f"s_EB{r}") for r in range(2)]
    Csr = [P.sb([128, 128], BF16, f"s_Csr{r}") for r in range(2)]
    xd = P.sb([128, 128], BF16, "s_xd"); xdd = P.sb([128, 128], BF16, "s_xdd")
    identb = P.sb([128, 128], BF16, "s_identb")
    P.cp('dve', identb, identb.t[:], ident.t[:], [ident])
    tp1, g_ps, bc_ps, y_ps, S_ps = ps[0], ps[1], ps[2], ps[3], ps[4]
    psb = P.ps([128, 128], BF16, "s_psb")

    def conv_silu(src, which, out_tt, s):
        P.dma(raw[:], V(src, src.t[:, s * SC:s * SC + SC + 3]))
        P.ts('dve', acc, acc.t[:], raw.t[:, 0:SC], cws.t[:, which, 0:1], ALU.mult, [raw, cws])
        for k in range(1, 4):
            P.stt(acc, acc.t[:], raw.t[:, k:SC + k], cws.t[:, which, k:k + 1], acc.t[:], ALU.mult, ALU.add, [raw, cws, acc])
        P.act(out_tt, out_tt.t[:], acc.t[:], AF.Silu, [acc, cbs], bias=cbs.t[:, which:which + 1], scale=1.0)

    for s in range(NSC):
        conv_silu(xT, 0, xs, s)
        conv_silu(bT_, 1, Bs, s)
        conv_silu(cT_, 2, Cs, s)
        P.dma(acc[:], V(zT, zT.t[:, s * SC:(s + 1) * SC]))
        P.act(zs, zs.t[:], acc.t[:], AF.Silu, [acc])
        P.dma(dtr[:], V(dtT, dtT.t[:, s * SC:(s + 1) * SC]))
        P.act(dtr, dtr.t[:], dtr.t[:], AF.Exp, [dtr, hps], bias=hps.t[:, 0:1], scale=1.0)
        P.act(dts, dts.t[:], dtr.t[:], AF.Ln, [dtr], bias=1.0, scale=1.0)
        P.ts('dve', dtr, dtr.t[:], dts.t[:], na.t[:, 0:1], ALU.mult, [dts, na])
        P.op('dve', lambda e: e.tensor_tensor_scan(out=acum.t[:], data0=rmask.t[:], data1=dtr.t[:], initial=0.0, op0=ALU.mult, op1=ALU.add),
             outs=[acum], ins=[rmask, dtr])
        for c in range(SC // 128):
            o = slice(c * 128, (c + 1) * 128)
            P.tr(tp1, tp1.t[:, 0:128], xs.t[:, o], ident.t[:], [xs, ident])
            P.tr(tp1, tp1.t[:, 128:130], dts.t[0:2, o], ident.t[0:2, 0:2], [dts, ident])
            P.tr(tp1, tp1.t[:, 130:132], acum.t[0:2, o], ident.t[0:2, 0:2], [acum, ident])
            P.cp('dve', small, small.t[:], tp1.t[:, 128:132], [tp1])
            P.tr(psb, psb.t[:], Bs.t[:, o], identb.t[:], [Bs, identb])
            P.cp('act', Btok, Btok.t[:], psb.t[:], [psb])
            P.mm(g_ps, g_ps.t[:, 0:128], Bs.t[:, o], Cs.t[:, o], True, True, [Bs, Cs])
            for r in range(2):
                P.mm(bc_ps, bc_ps.t[:, r * 128:(r + 1) * 128], sel.t[:, r, :], acum.t[0:2, o], True, True, [sel, acum])
            for r in range(2):
                bcr = bc_ps.t[:, r * 128:(r + 1) * 128]
                P.stt(Dsb[r], Dsb[r].t[:], bcr, small.t[:, 2 + r:3 + r], negm.t[:], ALU.subtract, ALU.add, [bc_ps, small, negm])
                P.act(Esb[r], Esb[r].t[:], Dsb[r].t[:], AF.Exp, [Dsb[r]])
                P.tt('dve', Msb[r], Msb[r].t[:], g_ps.t[:, 0:128], Esb[r].t[:], ALU.mult, [g_ps, Esb[r]])
                P.act(EBs[r], EBs[r].t[:], bcr, AF.Exp, [bc_ps])
                P.tt('pool', Csr[r], Csr[r].t[:], Cs.t[:, o], EBs[r].t[:], ALU.mult, [Cs, EBs[r]])
                P.ts('dve', xdd, xdd.t[:, r * 64:(r + 1) * 64], tp1.t[:, r * 64:(r + 1) * 64], small.t[:, r:r + 1], ALU.mult, [tp1, small, Esb[r]],
                     s2=Esb[r].t[:, 127:128], op1=ALU.mult)
                P.ts('dve', xd, xd.t[:, r * 64:(r + 1) * 64], tp1.t[:, r * 64:(r + 1) * 64], small.t[:, r:r + 1], ALU.mult, [tp1, small])
            for r in range(2):
                P.mm(y_ps, y_ps.t[64 * r:64 * r + 64, 0:128], xd.t[:, r * 64:(r + 1) * 64], Msb[r].t[:], True, False, [xd, Msb[r]])
                P.mm(y_ps, y_ps.t[64 * r:64 * r + 64, 0:128], hb.t[:, r * 64:(r + 1) * 64], Csr[r].t[:], False, True, [hb, Csr[r]])
            P.mm(S_ps, S_ps.t[:, 0:128], Btok.t[:], xdd.t[:], True, True, [Btok, xdd])
            for r in range(2):
                P.stt(hf, hf.t[:, r * 64:(r + 1) * 64], hf.t[:, r * 64:(r + 1) * 64], EBs[r].t[:, 127:128], S_ps.t[:, r * 64:(r + 1) * 64],
                      ALU.mult, ALU.add, [hf, EBs[r], S_ps])
            P.cp('pool', hb, hb.t[:], hf.t[:], [hf])
            P.stt(ysc, ysc.t[:, o], xs.t[:, o], dsk.t[:, 0:1], y_ps.t[:, 0:128], ALU.mult, ALU.add, [xs, dsk, y_ps])
        P.tt('pool', ysc, ysc.t[:], ysc.t[:], zs.t[:], ALU.mult, [ysc, zs])
        P.dma(V(ygT, ygT.t[:, s * SC:(s + 1) * SC]), ysc[:])


def build_even(L=16384, do_ssd=True, do_nsa=True):
    P = Prog()
    io = P.sb([128, 128], I32, "io0")
    P.op('pool', lambda e: e.iota(io.t[:], pattern=[[1, 128]], base=0, channel_multiplier=-1), outs=[io])
    ident = P.sb([128, 128], F32, "ident")
    P.ts('dve', ident, ident.t[:], io.t[:], 0.0, ALU.is_equal, [io])
    if do_ssd:
        zT = P.din("zT", [128, L]); xT = P.din("xT", [128, L + 3]); bT_ = P.din("bT_", [128, L + 3]); cT_ = P.din("cT_", [128, L + 3])
        dtT = P.din("dtT", [2, L]); cw = P.din("cw", [128, 3, 4]); cb = P.din("cb", [128, 3]); hp = P.din("hp", [2, 3]); dsk = P.din("dsk", [128, 1])
        ygT = P.dout("ygT", [128, L])
        with P.scope():
            ps = [P.ps([128, 512], F32, f"ps{i}") for i in range(5)]
            ssd_phase(P, L, ps, ident, (zT, xT, bT_, cT_, dtT, cw, cb, hp, dsk), ygT)
    if do_nsa:
        nsa_in = (P.din("qT4", [256, L]), P.din("kcT", [64, L]), P.din("vcT", [64, L]), P.din("ksT", [64, L]), P.din("vsT", [64, L]),
                  P.din("kwT", [64, L]), P.din("vwT", [64, L]), P.din("gtT", [6, L]), P.din("w1d", [2, 64, 32, 256]), P.din("peT", [64, 2, 32]),
                  P.din("b1d", [128, 2, 2]), P.din("w2d", [128, 2, 2, 64]), P.din("b2k", [64, 1]), P.din("b2v", [64]))
        onT = P.dout("onT", [128, L])
        nsa_phase(P, L, ident, nsa_in, onT)
    return P.finish()


def prep_odd(uT, part, i, inp, nq=32, nch=32):
    NQT = nq * 128
    s0 = part * NQT
    m = {}
    m["qT"] = np.ascontiguousarray(uT[0:512, s0:s0 + NQT])
    kv = np.zeros((256, NQT + 128), np.float32)
    lo = s0 - 128
    if lo >= 0:
        kv[:, :] = uT[512:768, lo:s0 + NQT]
    else:
        kv[:, 128:] = uT[512:768, s0:s0 + NQT]
    m["kT"] = np.ascontiguousarray(kv[0:128]); m["vT"] = np.ascontiguousarray(kv[128:256])
    k = np.arange(128)[:, None]; q = np.arange(128)[None, :]
    mp = np.tile((k > q).astype(np.float32), (1, 4))
    m["mprev0"] = mp if part > 0 else np.zeros_like(mp)
    m["sinks"] = np.ascontiguousarray(inp["swa_sinks"][i])
    L = nch * 512
    c0 = 768 + 128 * part
    m["usT"] = np.ascontiguousarray(uT[c0:c0 + 128, 0:L])
    g0 = 8 * part
    prm = np.zeros((128, 4, 3), np.float32)
    bT = np.zeros((4, 128, 2, 128), np.float32); cT = np.zeros((4, 128, 2, 128), np.float32)
    for t in range(4):
        for gg in range(2):
            gl = 2 * t + gg; g = g0 + gl
            sl = slice(gg * 64, gg * 64 + 64)
            prm[sl, t, 0] = inp["s5_a_re"][i, g]; prm[sl, t, 1] = inp["s5_a_im"][i, g]; prm[sl, t, 2] = inp["s5_log_dt"][i, g]
            ch = slice(gl * 16, gl * 16 + 16)
            bT[t, ch, 0, sl] = inp["s5_b_re"][i, g].T; bT[t, ch, 1, sl] = inp["s5_b_im"][i, g].T
            cT[t, sl, 0, ch] = inp["s5_c_re"][i, g].T; cT[t, sl, 1, ch] = inp["s5_c_im"][i, g].T
    m["s5p"] = prm; m["bT"] = bT; m["cT"] = cT
    m["s5d"] = np.ascontiguousarray(inp["s5_d"][i, c0 - 768:c0 - 768 + 128].reshape(128, 1))
    return m


def prep_ssd(uT, g, half, i, inp, L=16384):
    hh = 4 * g + 2 * half
    m = {}
    m["zT"] = np.ascontiguousarray(uT[64 * hh:64 * hh + 128, :L])
    def pad(rows):
        a = np.zeros((rows.shape[0], L + 3), np.float32); a[:, 3:] = rows[:, :L]; return a
    m["xT"] = pad(uT[512 + 64 * hh:512 + 64 * hh + 128])
    m["bT_"] = pad(uT[1024 + 128 * g:1024 + 128 * g + 128])
    m["cT_"] = pad(uT[1280 + 128 * g:1280 + 128 * g + 128])
    m["dtT"] = np.ascontiguousarray(uT[1536 + hh:1536 + hh + 2, :L])
    cwf = inp["ssd_conv_w"][i]; cbf = inp["ssd_conv_b"][i]
    chs = [slice(64 * hh, 64 * hh + 128), slice(512 + 128 * g, 512 + 128 * g + 128), slice(768 + 128 * g, 768 + 128 * g + 128)]
    cw = np.zeros((128, 3, 4), np.float32); cb = np.zeros((128, 3), np.float32)
    for w, sl in enumerate(chs):
        cw[:, w, :] = cwf[:, sl].T; cb[:, w] = cbf[sl]
    m["cw"] = cw; m["cb"] = cb
    hp = np.zeros((2, 3), np.float32)
    hp[:, 0] = inp["ssd_dt_bias"][i, hh:hh + 2]; hp[:, 1] = inp["ssd_a_log"][i, hh:hh + 2]
    m["hp"] = hp
    m["dsk"] = np.repeat(inp["ssd_d"][i, hh:hh + 2], 64).reshape(128, 1).astype(np.float32)
    return m

def prep_nsa(uT, g, half, i, inp, L=16384):
    m = {}
    base = SSD_IN
    order = [2 * half, 2 * half + 1, 2 * (1 - half), 2 * (1 - half) + 1]
    q = [uT[base + 64 * (4 * g + h):base + 64 * (4 * g + h) + 64, :L] for h in order]
    m["qT4"] = np.ascontiguousarray(np.concatenate(q, 0))
    names = ["kcT", "vcT", "ksT", "vsT", "kwT", "vwT"]
    for n_, nm in enumerate(names):
        r0 = base + 512 + 128 * n_ + 64 * g
        m[nm] = np.ascontiguousarray(uT[r0:r0 + 64, :L])
    g0 = base + 512 + 768 + 12 * g + 6 * half
    m["gtT"] = np.ascontiguousarray(uT[g0:g0 + 6, :L])
    w1 = inp["nsa_cmp_w1"][i]
    m["w1d"] = np.ascontiguousarray(w1.reshape(2, 32, 64, 256).transpose(0, 2, 1, 3))
    m["peT"] = np.ascontiguousarray(inp["nsa_pe"][i].transpose(2, 0, 1))
    m["b1d"] = np.ascontiguousarray(inp["nsa_cmp_b1"][i].reshape(2, 2, 128).transpose(2, 0, 1))
    m["w2d"] = np.ascontiguousarray(inp["nsa_cmp_w2"][i].reshape(2, 2, 128, 64).transpose(2, 0, 1, 3))
    m["b2k"] = np.ascontiguousarray(inp["nsa_cmp_b2"][i, 0].reshape(64, 1))
    m["b2v"] = np.ascontiguousarray(inp["nsa_cmp_b2"][i, 1])
    return m


_PROGS = {}


def _prog(key, fn):
    if key not in _PROGS:
        _PROGS[key] = fn()
    return _PROGS[key]


def _g2(g):
    return np.ascontiguousarray(np.asarray(g, np.float32).reshape(8, 128).T)


def _run(nc, maps):
    res = run_bass_kernel_spmd(nc, maps, core_ids=list(range(8)))
    return res.results


def kernel(**inp):
    inp = {k: np.asarray(v) for k, v in inp.items()}
    x = inp["x"].astype(np.float32)
    B, S, D = x.shape
    NTC = 4096
    hT = [np.ascontiguousarray(x[c // 4, (c % 4) * NTC:(c % 4 + 1) * NTC, :].T) for c in range(8)]

    def gather_u(res, n):
        uT = [np.empty((n, S), np.float32) for _ in range(B)]
        for c in range(8):
            uT[c // 4][:, (c % 4) * NTC:(c % 4 + 1) * NTC] = res[c]["uT"]
        return uT

    nc = _prog(("tok", False, False, 2848, False), lambda: build_tok(False, False, 2848, False))
    res = _run(nc, [{"hT": hT[c], "g_in": _g2(inp["norm_mix"][0]), "w_in": np.ascontiguousarray(inp["ev_w_in"][0])} for c in range(8)])
    uT = gather_u(res, 2848)
    out = None
    for layer in range(4):
        i = layer // 2
        odd = layer % 2 == 1
        ycT = [np.empty((1024, S), np.float32) for _ in range(B)]
        if not odd:
            nc = _prog(("even",), lambda: build_even(16384, True, True))
            maps = []
            for c in range(8):
                b, g, half = c // 4, (c % 4) // 2, c % 2
                m = prep_ssd(uT[b], g, half, i, inp)
                m.update(prep_nsa(uT[b], g, half, i, inp))
                maps.append(m)
            res = _run(nc, maps)
            for c in range(8):
                b, g, half = c // 4, (c % 4) // 2, c % 2
                hh = 4 * g + 2 * half
                ycT[b][64 * hh:64 * hh + 128, :] = res[c]["ygT"]
                ycT[b][512 + 64 * hh:512 + 64 * hh + 128, :] = res[c]["onT"]
        else:
            nc = _prog(("odd",), lambda: build_odd(32, 32))
            maps = [prep_odd(uT[c // 4], c % 4, i, inp) for c in range(8)]
            res = _run(nc, maps)
            for c in range(8):
                b, part = c // 4, c % 4
                ycT[b][0:512, part * NTC:(part + 1) * NTC] = res[c]["ocT"]
                ycT[b][512 + 128 * part:512 + 128 * part + 128, :] = res[c]["ydT"]
        del uT
        last = layer == 3
        n_in = 0 if last else (1280 if not odd else 2848)
        nc = _prog(("tok", True, odd, n_in, last), lambda: build_tok(True, odd, n_in, last))
        maps = []
        for c in range(8):
            b, part = c // 4, c % 4
            m = {"hT": hT[c], "ycT": np.ascontiguousarray(ycT[b][:, part * NTC:(part + 1) * NTC]),
                 "w_out": np.ascontiguousarray((inp["od_w_out"] if odd else inp["ev_w_out"])[i]),
                 "g_mlp": _g2(inp["norm_mlp"][layer]),
                 "w_up": np.ascontiguousarray(inp["mlp_w_up"][layer]), "w_down": np.ascontiguousarray(inp["mlp_w_down"][layer])}
            if odd:
                m["glu_w"] = np.ascontiguousarray(inp["s5_glu_w"][i])
                m["glu_b"] = np.ascontiguousarray(inp["s5_glu_b"][i].reshape(4, 128).T)
            else:
                m["ssdn"] = np.ascontiguousarray(inp["ssd_norm"][i].reshape(4, 128).T)
            if n_in:
                m["g_in"] = _g2(inp["norm_mix"][layer + 1])
                m["w_in"] = np.ascontiguousarray((inp["od_w_in"][i] if not odd else inp["ev_w_in"][i + 1]))
            if last:
                m["g_fin"] = _g2(inp["norm_final"])
            maps.append(m)
        res = _run(nc, maps)
        del ycT
        if last:
            out = np.empty((B, S, D), np.float32)
            for c in range(8):
                out[c // 4, (c % 4) * NTC:(c % 4 + 1) * NTC, :] = res[c]["hTo"].T
        else:
            hT = [res[c]["hTo"] for c in range(8)]
            uT = gather_u(res, n_in)
    return out
```

```python
import numpy as np
from contextlib import ExitStack
import concourse.bass as bass
import concourse.mybir as mybir
from concourse.bass_utils import run_bass_kernel_spmd

F32 = mybir.dt.float32
BF16 = mybir.dt.bfloat16
AF = mybir.ActivationFunctionType
ALU = mybir.AluOpType
AX = mybir.AxisListType

NS = 8
SAME_ENGINE_SYNC = True


class TT:
    def __init__(self, t, name):
        self.t = t
        self.name = name
        self.last_w = None
        self.readers = []

    def __getitem__(self, idx):
        return V(self, self.t[idx])

    def v(self, ap):
        return V(self, ap)


class V:
    def __init__(self, tt, ap):
        self.tt = tt
        self.ap = ap


class Prog:
    DMAC = ('dsp', 'dpool', 'dact')
    ENG = {'pe': 'tensor', 'dve': 'vector', 'act': 'scalar', 'pool': 'gpsimd', 'sp': 'sync'}

    def __init__(self):
        self.nc = bass.Bass("TRN2", target_bir_lowering=False)
        self.es = ExitStack()
        self.q = {e: [] for e in self.ENG}
        self.ctr = ['pe', 'dve', 'act', 'pool', 'dsp', 'dpool', 'dact']
        self.cnt = {c: 0 for c in self.ctr}
        self.sems = {c: [self.es.enter_context(self.nc.semaphore(f"s_{c}{i}")) for i in range(NS)]
                     for c in self.ctr}
        self.known = {e: {} for e in self.ENG}
        self.nbuf = 0
        self.dram = {}
        self.ninstr = 0
        self.stack = [self.es]

    def sb(self, shape, dtype=F32, name=None):
        self.nbuf += 1
        name = name or f"sb{self.nbuf}"
        t = self.stack[-1].enter_context(self.nc.sbuf_tensor(name, list(shape), dtype))
        return TT(t, name)

    def ps(self, shape, dtype=F32, name=None):
        self.nbuf += 1
        name = name or f"ps{self.nbuf}"
        t = self.stack[-1].enter_context(self.nc.psum_tensor(name, list(shape), dtype))
        return TT(t, name)

    def din(self, name, shape, dtype=F32):
        t = self.nc.dram_tensor(name, list(shape), dtype, kind="ExternalInput")
        tt = TT(t.ap(), name)
        self.dram[name] = tt
        return tt

    def dout(self, name, shape, dtype=F32):
        t = self.nc.dram_tensor(name, list(shape), dtype, kind="ExternalOutput")
        tt = TT(t.ap(), name)
        self.dram[name] = tt
        return tt

    def dint(self, name, shape, dtype=F32):
        t = self.nc.dram_tensor(name, list(shape), dtype, kind="Internal")
        tt = TT(t.ap(), name)
        self.dram[name] = tt
        return tt

    def _semval(self, c, k):
        mult = 16 if c.startswith('d') and c != 'dve' else 1
        return self.sems[c][(k - 1) % NS], mult * ((k - 1) // NS + 1)

    def op(self, e, fn, outs=(), ins=(), ctr=None):
        c = ctr or e
        deps = set()
        for v in ins:
            tt = v.tt if isinstance(v, V) else v
            if tt.last_w is not None:
                deps.add(tt.last_w)
        for v in outs:
            tt = v.tt if isinstance(v, V) else v
            if tt.last_w is not None:
                deps.add(tt.last_w)
            for r in tt.readers:
                deps.add(r)
        waits = []
        best = {}
        for (f, k) in deps:
            if f == c and (c == 'pe' or not SAME_ENGINE_SYNC):
                continue
            key = (f, (k - 1) % NS) if f in self.DMAC else f
            if k > best.get(key, (None, 0))[1]:
                best[key] = (f, k)
        for key, (f, k) in best.items():
            if self.known[e].get(key, 0) >= k:
                continue
            self.known[e][key] = k
            waits.append(self._semval(f, k))
        self.cnt[c] += 1
        k_me = self.cnt[c]
        is_dma = c in ('dsp', 'dpool', 'dact')
        if is_dma and k_me > NS:
            kk = k_me - NS
            key = (c, (kk - 1) % NS)
            if self.known[e].get(key, 0) < kk:
                self.known[e][key] = kk
                waits.append(self._semval(c, kk))
        sem = self.sems[c][(k_me - 1) % NS]
        inc = 16 if is_dma else 1

        eng = getattr(self.nc, self.ENG[e])
        for (s, val) in waits:
            eng.wait_ge(s, val)
        fn(eng).then_inc(sem, inc)
        self.ninstr += 1
        for v in outs:
            tt = v.tt if isinstance(v, V) else v
            tt.last_w = (c, k_me)
            tt.readers = []
        for v in ins:
            tt = v.tt if isinstance(v, V) else v
            tt.readers.append((c, k_me))
        return (c, k_me)

    def dma(self, out, in_, q='sp'):
        e, c = {'sp': ('sp', 'dsp'), 'pool': ('pool', 'dpool'), 'act': ('act', 'dact')}[q]
        return self.op(e, lambda eng: eng.dma_start(out=out.ap, in_=in_.ap), outs=[out], ins=[in_], ctr=c)

    def mm(self, o, out, lhsT, rhs, start, stop, ins):
        return self.op('pe', lambda e: e.matmul(out, lhsT=lhsT, rhs=rhs, start=start, stop=stop), outs=[o], ins=ins)

    def tr(self, o, out, in_, ident, ins):
        return self.op('pe', lambda e: e.transpose(out=out, in_=in_, identity=ident), outs=[o], ins=ins)

    def act(self, o, out, in_, func, ins, bias=None, scale=None):
        kw = {}
        if bias is not None:
            kw['bias'] = bias
        if scale is not None:
            kw['scale'] = scale
        return self.op('act', lambda e: e.activation(out=out, in_=in_, func=func, **kw), outs=[o], ins=ins)

    def tt(self, eng, o, out, a, b, op, ins):
        return self.op(eng, lambda e: e.tensor_tensor(out=out, in0=a, in1=b, op=op), outs=[o], ins=ins)

    def ts(self, eng, o, out, in_, s1, op0, ins, s2=None, op1=None):
        if op1 is None:
            return self.op(eng, lambda e: e.tensor_scalar(out=out, in0=in_, scalar1=s1, scalar2=None, op0=op0), outs=[o], ins=ins)
        return self.op(eng, lambda e: e.tensor_scalar(out=out, in0=in_, scalar1=s1, scalar2=s2, op0=op0, op1=op1), outs=[o], ins=ins)

    def stt(self, o, out, in0, scalar, in1, op0, op1, ins, eng='dve'):
        return self.op(eng, lambda e: e.scalar_tensor_tensor(out=out, in0=in0, scalar=scalar, in1=in1, op0=op0, op1=op1), outs=[o], ins=ins)

    def cp(self, eng, o, out, in_, ins):
        if eng == 'act':
            return self.op('act', lambda e: e.copy(out=out, in_=in_), outs=[o], ins=ins)
        return self.op(eng, lambda e: e.tensor_copy(out=out, in_=in_), outs=[o], ins=ins)

    def memset(self, o, ap, val, eng='pool'):
        return self.op(eng, lambda e: e.memset(ap, val), outs=[o])

    def barrier(self, engines=None):
        for e in (engines or self.ENG):
            eng = getattr(self.nc, self.ENG[e])
            for c in self.ctr:
                k = self.cnt[c]
                if k == 0:
                    continue
                if c in self.DMAC:
                    for kk in range(max(1, k - NS + 1), k + 1):
                        key = (c, (kk - 1) % NS)
                        if self.known[e].get(key, 0) < kk:
                            self.known[e][key] = kk
                            sm, val = self._semval(c, kk)
                            eng.wait_ge(sm, val)
                else:
                    if c == e and c == 'pe':
                        continue
                    if self.known[e].get(c, 0) < k:
                        self.known[e][c] = k
                        sm, val = self._semval(c, k)
                        eng.wait_ge(sm, val)

    from contextlib import contextmanager

    @contextmanager
    def scope(self):
        st = ExitStack()
        self.stack.append(st)
        try:
            yield
        finally:
            self.barrier()
            self.stack.pop()
            st.close()

    def finish(self):
        self.barrier(['sp'])
        self.es.close()
        return self.nc


I32 = mybir.dt.int32
NEGB = -30000.0
SSD_IN = 1544


EPS = 1e-6
TT_TOK = 512


def build_tok(post, odd, n_in, final, ntt=8):
    P = Prog()
    NT = ntt * TT_TOK
    hT = P.din("hT", [1024, NT])
    if post:
        ycT = P.din("ycT", [1024, NT])
        w_out = P.din("w_out", [1024, 1024])
        g_mlp = P.din("g_mlp", [128, 8])
        w_up = P.din("w_up", [1024, 4096])
        w_down = P.din("w_down", [4096, 1024])
        if odd:
            glu_w = P.din("glu_w", [512, 512])
            glu_b = P.din("glu_b", [128, 4])
        else:
            ssdn_d = P.din("ssdn", [128, 4])
    if n_in:
        g_in = P.din("g_in", [128, 8])
        w_in = P.din("w_in", [1024, n_in])
        uT = P.dout("uT", [n_in, NT])
    if final:
        g_fin = P.din("g_fin", [128, 8])
    if post or final:
        hTo = P.dout("hTo", [1024, NT])

    def cast_rows(dst, src, K, C):
        nd = (C + 1023) // 1024
        kg = 4 if nd == 1 else 1
        sv = src.t.rearrange("(k p) c -> p k c", p=128)
        for k0 in range(0, K, kg):
            k1 = min(K, k0 + kg)
            P.op('pool', lambda e, k0=k0, k1=k1: e.dma_start(out=dst.t[:, k0:k1, :], in_=sv[:, k0:k1, :], max_dma_last_dim=4096),
                 outs=[dst], ins=[src], ctr='dpool')
    if post:
        s_up = P.dint("s_up", [128, 8, 4096], BF16)
        s_down = P.dint("s_down", [128, 32, 1024], BF16)
        s_out = P.dint("s_out", [128, 8, 1024], BF16)
        cast_rows(s_up, w_up, 8, 4096)
        cast_rows(s_down, w_down, 32, 1024)
        cast_rows(s_out, w_out, 8, 1024)
        if odd:
            s_glu = P.dint("s_glu", [128, 4, 512], BF16)
            cast_rows(s_glu, glu_w, 4, 512)
    n_oc = (n_in + 127) // 128
    if n_in:
        s_in = P.dint("s_in", [128, 8, n_in], BF16)
        cast_rows(s_in, w_in, 8, n_in)

    ones = P.sb([128, 128], F32, "ones")
    P.op('pool', lambda e: e.memset(ones.t[:], 1.0), outs=[ones])
    h = [P.sb([128, TT_TOK], F32, f"h{k}") for k in range(8)]
    hn = [P.sb([128, TT_TOK], BF16, f"hn{k}") for k in range(8)]
    sq = [P.sb([128, TT_TOK], F32, f"sq{i}") for i in range(2)]
    rstd = P.sb([128, TT_TOK], F32, "rstd")
    pss = [P.ps([128, TT_TOK], F32, f"pp{i}") for i in range(4)]
    ps_ss = P.ps([128, TT_TOK], F32, "ps_ss")
    psi = [0]

    def next_ps():
        psi[0] = (psi[0] + 1) % 4
        return pss[psi[0]]

    if post:
        yc = [P.sb([128, TT_TOK], F32, f"yc{k}") for k in range(8)]
        ycb = [P.sb([128, TT_TOK], BF16, f"ycb{k}") for k in range(8)]
        wo = P.sb([128, 8, 1024], BF16, "wo")
        P.dma(wo[:], s_out[:])
        gm = P.sb([128, 8], F32, "gm")
        P.dma(gm[:], g_mlp[:])
        a = [P.sb([128, TT_TOK], BF16, f"a{f}") for f in range(32)]
        rl = [P.sb([128, TT_TOK], F32, f"rl{i}") for i in range(2)]
        wup = [P.sb([128, 8, 512], BF16, f"wup{i}") for i in range(2)]
        wdn = [P.sb([128, 32, 128], BF16, f"wdn{i}") for i in range(2)]
        if odd:
            wg = P.sb([128, 4, 512], BF16, "wg")
            P.dma(wg[:], s_glu[:])
            gb = P.sb([128, 4], F32, "gb")
            P.dma(gb[:], glu_b[:])
            gate = [P.sb([128, TT_TOK], F32, f"gate{i}") for i in range(2)]
        else:
            ssdn = P.sb([128, 4], F32, "ssdn_s")
            P.dma(ssdn[:], ssdn_d[:])
    if n_in:
        gi = P.sb([128, 8], F32, "gi")
        P.dma(gi[:], g_in[:])
        win = [P.sb([128, 8, 128], BF16, f"win{i}") for i in range(2)]
        uo = [P.sb([128, TT_TOK], F32, f"uo{i}") for i in range(2)]
    if final:
        gf = P.sb([128, 8], F32, "gf")
        P.dma(gf[:], g_fin[:])
        fo = [P.sb([128, TT_TOK], F32, f"fo{i}") for i in range(2)]

    def rmsnorm(g, outs, out_dtype_f32=False):
        for k in range(8):
            s = sq[k % 2]
            P.op('act', lambda e, s=s, k=k: e.activation(out=s.t[:], in_=h[k].t[:], func=AF.Square), outs=[s], ins=[h[k]])
            P.op('pe', lambda e, s=s, k=k: e.matmul(ps_ss.t[:], lhsT=ones.t[:], rhs=s.t[:], start=(k == 0), stop=(k == 7)),
                 outs=[ps_ss], ins=[ones, s])
        P.op('act', lambda e: e.activation(out=rstd.t[:], in_=ps_ss.t[:], func=AF.Sqrt, bias=EPS, scale=1.0 / 1024), outs=[rstd], ins=[ps_ss])
        P.op('dve', lambda e: e.reciprocal(out=rstd.t[:], in_=rstd.t[:]), outs=[rstd], ins=[rstd])
        for k in range(8):
            P.op('dve', lambda e, k=k: e.scalar_tensor_tensor(out=outs[k].t[:], in0=h[k].t[:], scalar=g.t[:, k:k + 1], in1=rstd.t[:],
                                                             op0=ALU.mult, op1=ALU.mult), outs=[outs[k]], ins=[h[k], g, rstd])

    for tt in range(ntt):
        tok = slice(tt * TT_TOK, (tt + 1) * TT_TOK)
        for k in range(8):
            P.dma(h[k][:], V(hT, hT.t[k * 128:(k + 1) * 128, tok]))
        if post:
            for k in range(8):
                P.dma(yc[k][:], V(ycT, ycT.t[k * 128:(k + 1) * 128, tok]))
            if not odd:
                for g in range(2):
                    for kk in range(2):
                        k = 2 * g + kk
                        sq_ = sq[kk]
                        P.op('act', lambda e, sq_=sq_, k=k: e.activation(out=sq_.t[:], in_=yc[k].t[:], func=AF.Square), outs=[sq_], ins=[yc[k]])
                        P.op('pe', lambda e, sq_=sq_, kk=kk: e.matmul(ps_ss.t[:], lhsT=ones.t[:], rhs=sq_.t[:], start=(kk == 0), stop=(kk == 1)),
                             outs=[ps_ss], ins=[ones, sq_])
                    P.op('act', lambda e: e.activation(out=rstd.t[:], in_=ps_ss.t[:], func=AF.Sqrt, bias=EPS, scale=1.0 / 256), outs=[rstd], ins=[ps_ss])
                    P.op('dve', lambda e: e.reciprocal(out=rstd.t[:], in_=rstd.t[:]), outs=[rstd], ins=[rstd])
                    for kk in range(2):
                        k = 2 * g + kk
                        P.op('dve', lambda e, k=k: e.scalar_tensor_tensor(out=ycb[k].t[:], in0=yc[k].t[:], scalar=ssdn.t[:, k:k + 1], in1=rstd.t[:],
                                                                         op0=ALU.mult, op1=ALU.mult), outs=[ycb[k]], ins=[yc[k], ssdn, rstd])
            for k in (range(4) if odd else range(4, 8)):
                eng = 'pool' if k % 2 else 'dve'
                P.op(eng, lambda e, k=k: e.tensor_copy(out=ycb[k].t[:], in_=yc[k].t[:]), outs=[ycb[k]], ins=[yc[k]])
            if odd:
                for k in range(4, 8):
                    P.op('pool', lambda e, k=k: e.tensor_copy(out=hn[k].t[:], in_=yc[k].t[:]), outs=[hn[k]], ins=[yc[k]])
                for j in range(4):
                    pp = next_ps()
                    for k in range(4):
                        P.op('pe', lambda e, pp=pp, j=j, k=k: e.matmul(pp.t[:], lhsT=wg.t[:, k, j * 128:(j + 1) * 128], rhs=hn[4 + k].t[:],
                                                                      start=(k == 0), stop=(k == 3)), outs=[pp], ins=[wg, hn[4 + k]])
                    gt = gate[j % 2]
                    P.op('act', lambda e, pp=pp, gt=gt, j=j: e.activation(out=gt.t[:], in_=pp.t[:], func=AF.Sigmoid, bias=gb.t[:, j:j + 1], scale=1.0),
                         outs=[gt], ins=[pp, gb])
                    P.op('dve', lambda e, gt=gt, j=j: e.tensor_tensor(out=ycb[4 + j].t[:], in0=yc[4 + j].t[:], in1=gt.t[:], op=ALU.mult),
                         outs=[ycb[4 + j]], ins=[yc[4 + j], gt])
            for j in range(8):
                pp = next_ps()
                for k in range(8):
                    P.op('pe', lambda e, pp=pp, j=j, k=k: e.matmul(pp.t[:], lhsT=wo.t[:, k, j * 128:(j + 1) * 128], rhs=ycb[k].t[:],
                                                                  start=(k == 0), stop=(k == 7)), outs=[pp], ins=[wo, ycb[k]])
                P.op('dve', lambda e, pp=pp, j=j: e.tensor_tensor(out=h[j].t[:], in0=h[j].t[:], in1=pp.t[:], op=ALU.add), outs=[h[j]], ins=[h[j], pp])
            rmsnorm(gm, hn)
            for fg in range(8):
                wb = wup[fg % 2]
                P.dma(wb[:], V(s_up, s_up.t[:, :, fg * 512:(fg + 1) * 512]))
                for fi in range(4):
                    f = fg * 4 + fi
                    pp = next_ps()
                    for k in range(8):
                        P.op('pe', lambda e, pp=pp, wb=wb, fi=fi, k=k: e.matmul(pp.t[:], lhsT=wb.t[:, k, fi * 128:(fi + 1) * 128], rhs=hn[k].t[:],
                                                                               start=(k == 0), stop=(k == 7)), outs=[pp], ins=[wb, hn[k]])
                    r = rl[f % 2]
                    P.op('act', lambda e, pp=pp, r=r: e.activation(out=r.t[:], in_=pp.t[:], func=AF.Relu), outs=[r], ins=[pp])
                    P.op('pool', lambda e, r=r, f=f: e.tensor_tensor(out=a[f].t[:], in0=r.t[:], in1=r.t[:], op=ALU.mult), outs=[a[f]], ins=[r])
            for j in range(8):
                wb = wdn[j % 2]
                P.dma(wb[:], V(s_down, s_down.t[:, :, j * 128:(j + 1) * 128]))
                pp = next_ps()
                for f in range(32):
                    P.op('pe', lambda e, pp=pp, wb=wb, f=f: e.matmul(pp.t[:], lhsT=wb.t[:, f, :], rhs=a[f].t[:], start=(f == 0), stop=(f == 31)),
                         outs=[pp], ins=[wb, a[f]])
                P.op('dve', lambda e, pp=pp, j=j: e.tensor_tensor(out=h[j].t[:], in0=h[j].t[:], in1=pp.t[:], op=ALU.add), outs=[h[j]], ins=[h[j], pp])
        if post and not final:
            for k in range(8):
                P.dma(V(hTo, hTo.t[k * 128:(k + 1) * 128, tok]), h[k][:])
        if n_in:
            rmsnorm(gi, hn)
            for o in range(n_oc):
                cw = min(128, n_in - o * 128)
                wb = win[o % 2]
                P.dma(V(wb, wb.t[:, :, 0:cw]), V(s_in, s_in.t[:, :, o * 128:o * 128 + cw]))
                pp = next_ps()
                for k in range(8):
                    P.op('pe', lambda e, pp=pp, wb=wb, k=k, cw=cw: e.matmul(pp.t[0:cw, :], lhsT=wb.t[:, k, 0:cw], rhs=hn[k].t[:],
                                                                           start=(k == 0), stop=(k == 7)), outs=[pp], ins=[wb, hn[k]])
                u = uo[o % 2]
                eng = 'act' if o % 2 else 'dve'
                if eng == 'act':
                    P.op('act', lambda e, pp=pp, u=u, cw=cw: e.copy(out=u.t[0:cw, :], in_=pp.t[0:cw, :]), outs=[u], ins=[pp])
                else:
                    P.op('dve', lambda e, pp=pp, u=u, cw=cw: e.tensor_copy(out=u.t[0:cw, :], in_=pp.t[0:cw, :]), outs=[u], ins=[pp])
                P.dma(V(uT, uT.t[o * 128:o * 128 + cw, tok]), V(u, u.t[0:cw, :]))
        if final:
            for k in range(8):
                s = sq[k % 2]
                P.op('act', lambda e, s=s, k=k: e.activation(out=s.t[:], in_=h[k].t[:], func=AF.Square), outs=[s], ins=[h[k]])
                P.op('pe', lambda e, s=s, k=k: e.matmul(ps_ss.t[:], lhsT=ones.t[:], rhs=s.t[:], start=(k == 0), stop=(k == 7)),
                     outs=[ps_ss], ins=[ones, s])
            P.op('act', lambda e: e.activation(out=rstd.t[:], in_=ps_ss.t[:], func=AF.Sqrt, bias=EPS, scale=1.0 / 1024), outs=[rstd], ins=[ps_ss])
            P.op('dve', lambda e: e.reciprocal(out=rstd.t[:], in_=rstd.t[:]), outs=[rstd], ins=[rstd])
            for k in range(8):
                o_ = fo[k % 2]
                P.op('dve', lambda e, k=k, o_=o_: e.scalar_tensor_tensor(out=o_.t[:], in0=h[k].t[:], scalar=gf.t[:, k:k + 1], in1=rstd.t[:],
                                                                        op0=ALU.mult, op1=ALU.mult), outs=[o_], ins=[h[k], gf, rstd])
                P.dma(V(hTo, hTo.t[k * 128:(k + 1) * 128, tok]), o_[:])
    return P.finish()


TWO_PI = 2 * np.pi
T5 = 512


def range_reduce(P, x, n, tmp_i, tmp_f, tmp_c):
    xs, ni, nf, c1 = x.t[:, :n], tmp_i.t[:, :n], tmp_f.t[:, :n], tmp_c.t[:, :n]
    P.op('dve', lambda e: e.tensor_scalar(out=ni, in0=xs, scalar1=1.0 / TWO_PI, scalar2=None, op0=ALU.mult), outs=[tmp_i], ins=[x])
    P.op('dve', lambda e: e.tensor_copy(out=nf, in_=ni), outs=[tmp_f], ins=[tmp_i])
    P.op('dve', lambda e: e.scalar_tensor_tensor(out=xs, in0=nf, scalar=-6.28125, in1=xs, op0=ALU.mult, op1=ALU.add), outs=[x], ins=[tmp_f, x])
    P.op('dve', lambda e: e.scalar_tensor_tensor(out=xs, in0=nf, scalar=-(TWO_PI - 6.28125), in1=xs, op0=ALU.mult, op1=ALU.add), outs=[x], ins=[tmp_f, x])
    P.op('dve', lambda e: e.tensor_single_scalar(out=c1, in_=xs, scalar=np.pi, op=ALU.is_gt), outs=[tmp_c], ins=[x])
    P.op('dve', lambda e: e.scalar_tensor_tensor(out=xs, in0=c1, scalar=-TWO_PI, in1=xs, op0=ALU.mult, op1=ALU.add), outs=[x], ins=[tmp_c, x])
    P.op('dve', lambda e: e.tensor_single_scalar(out=c1, in_=xs, scalar=-np.pi, op=ALU.is_lt), outs=[tmp_c], ins=[x])
    P.op('dve', lambda e: e.scalar_tensor_tensor(out=xs, in0=c1, scalar=TWO_PI, in1=xs, op0=ALU.mult, op1=ALU.add), outs=[x], ins=[tmp_c, x])
    P.op('dve', lambda e: e.tensor_scalar(out=xs, in0=xs, scalar1=np.pi, scalar2=-np.pi, op0=ALU.min, op1=ALU.max), outs=[x], ins=[x])


def build_odd(nq=32, nch=32):
    P = Prog()
    NQT = nq * 128
    L = nch * T5
    qT = P.din("qT", [512, NQT]); kT = P.din("kT", [128, NQT + 128]); vT = P.din("vT", [128, NQT + 128])
    mprev0 = P.din("mprev0", [128, 512]); sinks = P.din("sinks", [8])
    usT = P.din("usT", [128, L]); s5p = P.din("s5p", [128, 4, 3])
    bT = P.din("bT", [4, 128, 2, 128]); cT = P.din("cT", [4, 128, 2, 128]); s5d = P.din("s5d", [128, 1])
    ocT = P.dout("ocT", [512, NQT]); ydT = P.dout("ydT", [128, L])

    ps = [P.ps([128, 512], F32, f"ps{i}") for i in range(8)]
    io = P.sb([128, 4, 128], I32, "io")
    P.op('pool', lambda e: e.iota(io.t[:], pattern=[[0, 4], [1, 128]], base=0, channel_multiplier=-1), outs=[io])
    mcur = P.sb([128, 512], BF16, "mcur"); mprev = P.sb([128, 512], BF16, "mprev"); mp0 = P.sb([128, 512], BF16, "mp0")
    iov = io.t[:].rearrange("p h q -> p (h q)")
    P.op('dve', lambda e: e.tensor_single_scalar(out=mcur.t[:], in_=iov, scalar=0.0, op=ALU.is_ge), outs=[mcur], ins=[io])
    P.op('dve', lambda e: e.tensor_single_scalar(out=mprev.t[:], in_=iov, scalar=0.0, op=ALU.is_lt), outs=[mprev], ins=[io])
    mp0f = P.sb([128, 512], F32, "mp0f")
    P.dma(mp0f[:], mprev0[:])
    P.op('dve', lambda e: e.tensor_copy(out=mp0.t[:], in_=mp0f.t[:]), outs=[mp0], ins=[mp0f])
    ident = P.sb([128, 128], F32, "ident")
    P.op('dve', lambda e: e.tensor_single_scalar(out=ident.t[:], in_=io.t[:, 0, :], scalar=0.0, op=ALU.is_equal), outs=[ident], ins=[io])
    esink = P.sb([128, 8], F32, "esink")
    P.dma(esink[:], V(sinks, sinks.t.partition_broadcast(128)))
    P.op('act', lambda e: e.activation(out=esink.t[:], in_=esink.t[:], func=AF.Exp), outs=[esink], ins=[esink])
    zl = P.sb([1, 128], BF16, "zl"); zr_ = P.sb([1, 512], BF16, "zr_")
    P.op('pool', lambda e: e.memset(zl.t[:], 0.0), outs=[zl])
    P.op('pool', lambda e: e.memset(zr_.t[:], 0.0), outs=[zr_])

    prm = P.sb([128, 4, 3], F32, "prm")
    P.dma(prm[:], s5p[:])
    dsk = P.sb([128, 1], F32, "dsk")
    P.dma(dsk[:], s5d[:])
    bts = P.sb([128, 4, 2, 128], F32, "bts"); cts = P.sb([128, 4, 2, 128], F32, "cts")
    for i in range(4):
        P.dma(V(bts, bts.t[:, i]), bT[i])
        P.dma(V(cts, cts.t[:, i]), cT[i])
    P.op('dve', lambda e: e.tensor_scalar(out=cts.t[:, :, 1, :], in0=cts.t[:, :, 1, :], scalar1=-1.0, scalar2=None, op0=ALU.mult), outs=[cts], ins=[cts])
    sm = {n: P.sb([128, 4], F32, "sm_" + n) for n in ["step", "ars", "th", "rho", "c", "s", "xr", "xi", "den", "fr", "fi", "t1", "t2", "thc"]}
    tmp_i = P.sb([128, T5], I32, "tmp_i"); tmp_f = P.sb([128, T5], F32, "tmp_f"); tmp_c = P.sb([128, T5], F32, "tmp_c")
    are, aim, ldt = prm.t[:, :, 0], prm.t[:, :, 1], prm.t[:, :, 2]

    def tt_(eng, out, a, b, op, o_tt, ins):
        P.op(eng, lambda e: e.tensor_tensor(out=out, in0=a, in1=b, op=op), outs=[o_tt], ins=ins)
    P.op('act', lambda e: e.activation(out=sm["step"].t[:], in_=ldt, func=AF.Exp), outs=[sm["step"]], ins=[prm])
    tt_('dve', sm["ars"].t[:], are, sm["step"].t[:], ALU.mult, sm["ars"], [prm, sm["step"]])
    tt_('dve', sm["th"].t[:], aim, sm["step"].t[:], ALU.mult, sm["th"], [prm, sm["step"]])
    P.op('act', lambda e: e.activation(out=sm["rho"].t[:], in_=sm["ars"].t[:], func=AF.Exp), outs=[sm["rho"]], ins=[sm["ars"]])
    P.op('dve', lambda e: e.tensor_copy(out=sm["s"].t[:], in_=sm["th"].t[:]), outs=[sm["s"]], ins=[sm["th"]])
    range_reduce(P, sm["s"], 4, tmp_i, tmp_f, tmp_c)
    P.op('act', lambda e: e.activation(out=sm["s"].t[:], in_=sm["s"].t[:], func=AF.Sin), outs=[sm["s"]], ins=[sm["s"]])
    P.op('dve', lambda e: e.tensor_scalar(out=sm["c"].t[:], in0=sm["th"].t[:], scalar1=np.pi / 2, scalar2=None, op0=ALU.add), outs=[sm["c"]], ins=[sm["th"]])
    range_reduce(P, sm["c"], 4, tmp_i, tmp_f, tmp_c)
    P.op('act', lambda e: e.activation(out=sm["c"].t[:], in_=sm["c"].t[:], func=AF.Sin), outs=[sm["c"]], ins=[sm["c"]])
    tt_('dve', sm["xr"].t[:], sm["rho"].t[:], sm["c"].t[:], ALU.mult, sm["xr"], [sm["rho"], sm["c"]])
    P.op('dve', lambda e: e.tensor_scalar(out=sm["xr"].t[:], in0=sm["xr"].t[:], scalar1=-1.0, scalar2=None, op0=ALU.add), outs=[sm["xr"]], ins=[sm["xr"]])
    tt_('dve', sm["xi"].t[:], sm["rho"].t[:], sm["s"].t[:], ALU.mult, sm["xi"], [sm["rho"], sm["s"]])
    tt_('dve', sm["t1"].t[:], are, are, ALU.mult, sm["t1"], [prm])
    tt_('dve', sm["t2"].t[:], aim, aim, ALU.mult, sm["t2"], [prm])
    tt_('dve', sm["den"].t[:], sm["t1"].t[:], sm["t2"].t[:], ALU.add, sm["den"], [sm["t1"], sm["t2"]])
    P.op('dve', lambda e: e.reciprocal(out=sm["den"].t[:], in_=sm["den"].t[:]), outs=[sm["den"]], ins=[sm["den"]])
    tt_('dve', sm["t1"].t[:], sm["xr"].t[:], are, ALU.mult, sm["t1"], [sm["xr"], prm])
    tt_('dve', sm["t2"].t[:], sm["xi"].t[:], aim, ALU.mult, sm["t2"], [sm["xi"], prm])
    tt_('dve', sm["fr"].t[:], sm["t1"].t[:], sm["t2"].t[:], ALU.add, sm["fr"], [sm["t1"], sm["t2"]])
    tt_('dve', sm["fr"].t[:], sm["fr"].t[:], sm["den"].t[:], ALU.mult, sm["fr"], [sm["fr"], sm["den"]])
    tt_('dve', sm["t1"].t[:], sm["xi"].t[:], are, ALU.mult, sm["t1"], [sm["xi"], prm])
    tt_('dve', sm["t2"].t[:], sm["xr"].t[:], aim, ALU.mult, sm["t2"], [sm["xr"], prm])
    tt_('dve', sm["fi"].t[:], sm["t1"].t[:], sm["t2"].t[:], ALU.subtract, sm["fi"], [sm["t1"], sm["t2"]])
    tt_('dve', sm["fi"].t[:], sm["fi"].t[:], sm["den"].t[:], ALU.mult, sm["fi"], [sm["fi"], sm["den"]])
    jt_i = P.sb([128, T5], I32, "jt_i"); jt = P.sb([128, T5], F32, "jt")
    P.op('pool', lambda e: e.iota(jt_i.t[:], pattern=[[1, T5]], base=1, channel_multiplier=0), outs=[jt_i])
    P.op('dve', lambda e: e.tensor_copy(out=jt.t[:], in_=jt_i.t[:]), outs=[jt], ins=[jt_i])
    Cn = [P.sb([128, T5], F32, f"Cn{i}") for i in range(4)]; Sn = [P.sb([128, T5], F32, f"Sn{i}") for i in range(4)]
    Fr = [P.sb([128, T5], F32, f"Fr{i}") for i in range(4)]; Fi = [P.sb([128, T5], F32, f"Fi{i}") for i in range(4)]
    for i in range(4):
        th_i = sm["th"].t[:, i:i + 1]
        P.op('dve', lambda e, i=i, th_i=th_i: e.tensor_scalar(out=Sn[i].t[:], in0=jt.t[:], scalar1=th_i, scalar2=None, op0=ALU.mult), outs=[Sn[i]], ins=[jt, sm["th"]])
        P.op('dve', lambda e, i=i: e.tensor_scalar(out=Cn[i].t[:], in0=Sn[i].t[:], scalar1=np.pi / 2, scalar2=None, op0=ALU.add), outs=[Cn[i]], ins=[Sn[i]])
        range_reduce(P, Sn[i], T5, tmp_i, tmp_f, tmp_c)
        range_reduce(P, Cn[i], T5, tmp_i, tmp_f, tmp_c)
        P.op('act', lambda e, i=i: e.activation(out=Sn[i].t[:], in_=Sn[i].t[:], func=AF.Sin), outs=[Sn[i]], ins=[Sn[i]])
        P.op('act', lambda e, i=i: e.activation(out=Cn[i].t[:], in_=Cn[i].t[:], func=AF.Sin), outs=[Cn[i]], ins=[Cn[i]])
        fr_i, fi_i = sm["fr"].t[:, i:i + 1], sm["fi"].t[:, i:i + 1]
        P.op('dve', lambda e, i=i, fi_i=fi_i: e.tensor_scalar(out=tmp_f.t[:], in0=Sn[i].t[:], scalar1=fi_i, scalar2=None, op0=ALU.mult), outs=[tmp_f], ins=[Sn[i], sm["fi"]])
        P.op('dve', lambda e, i=i, fr_i=fr_i: e.scalar_tensor_tensor(out=Fr[i].t[:], in0=Cn[i].t[:], scalar=fr_i, in1=tmp_f.t[:], op0=ALU.mult, op1=ALU.add),
             outs=[Fr[i]], ins=[Cn[i], sm["fr"], tmp_f])
        P.op('dve', lambda e, i=i, fr_i=fr_i: e.tensor_scalar(out=tmp_f.t[:], in0=Sn[i].t[:], scalar1=fr_i, scalar2=None, op0=ALU.mult), outs=[tmp_f], ins=[Sn[i], sm["fr"]])
        P.op('dve', lambda e, i=i, fi_i=fi_i: e.scalar_tensor_tensor(out=Fi[i].t[:], in0=Cn[i].t[:], scalar=fi_i, in1=tmp_f.t[:], op0=ALU.mult, op1=ALU.subtract),
             outs=[Fi[i]], ins=[Cn[i], sm["fi"], tmp_f])
    us = [P.sb([128, T5], F32, f"us{i}") for i in range(2)]
    m = [[P.sb([128, T5], F32, f"m{s}_{j}") for j in range(4)] for s in range(2)]
    zin = [[P.sb([128, T5], F32, f"zin{s}_{j}") for j in range(2)] for s in range(2)]
    zz = [[P.sb([128, T5], F32, f"zz{s}_{j}") for j in range(2)] for s in range(2)]
    nn = [[P.sb([128, T5], F32, f"nn{s}_{j}") for j in range(4)] for s in range(2)]
    xst = [[P.sb([128, T5], F32, f"xst{i}_{j}") for j in range(2)] for i in range(4)]
    yv = [P.sb([128, T5], F32, f"yv{i}") for i in range(2)]
    for c in range(nch):
        u = us[c % 2]
        P.dma(u[:], V(usT, usT.t[:, c * T5:(c + 1) * T5]))
        yps = ps[4 + c % 2]
        for i in range(4):
            s = i % 2
            brp, bip = ps[2 * s], ps[2 * s + 1]
            P.op('pe', lambda e, brp=brp, i=i, u=u: e.matmul(brp.t[:], lhsT=bts.t[:, i, 0, :], rhs=u.t[:], start=True, stop=True), outs=[brp], ins=[bts, u])
            P.op('pe', lambda e, bip=bip, i=i, u=u: e.matmul(bip.t[:], lhsT=bts.t[:, i, 1, :], rhs=u.t[:], start=True, stop=True), outs=[bip], ins=[bts, u])
            mm_ = m[s]
            tt_('dve', mm_[0].t[:], brp.t[:], Fr[i].t[:], ALU.mult, mm_[0], [brp, Fr[i]])
            tt_('dve', mm_[1].t[:], bip.t[:], Fi[i].t[:], ALU.mult, mm_[1], [bip, Fi[i]])
            tt_('dve', mm_[2].t[:], brp.t[:], Fi[i].t[:], ALU.mult, mm_[2], [brp, Fi[i]])
            tt_('dve', mm_[3].t[:], bip.t[:], Fr[i].t[:], ALU.mult, mm_[3], [bip, Fr[i]])
            tt_('pool', zin[s][0].t[:], mm_[0].t[:], mm_[1].t[:], ALU.subtract, zin[s][0], [mm_[0], mm_[1]])
            tt_('pool', zin[s][1].t[:], mm_[2].t[:], mm_[3].t[:], ALU.add, zin[s][1], [mm_[2], mm_[3]])
            rho_b = sm["rho"].t[:, i:i + 1].to_broadcast([128, T5])
            for j in range(2):
                init = 0.0 if c == 0 else xst[i][j].t[:, T5 - 1:T5]
                ins_ = [sm["rho"], zin[s][j]] + ([] if c == 0 else [xst[i][j]])
                P.op('dve', lambda e, s=s, j=j, rho_b=rho_b, init=init: e.tensor_tensor_scan(out=zz[s][j].t[:], data0=rho_b, data1=zin[s][j].t[:], initial=init,
                                                                                              op0=ALU.mult, op1=ALU.add), outs=[zz[s][j]], ins=ins_)
            n_ = nn[s]
            tt_('pool', n_[0].t[:], zz[s][0].t[:], Cn[i].t[:], ALU.mult, n_[0], [zz[s][0], Cn[i]])
            tt_('pool', n_[1].t[:], zz[s][1].t[:], Sn[i].t[:], ALU.mult, n_[1], [zz[s][1], Sn[i]])
            tt_('pool', n_[2].t[:], zz[s][0].t[:], Sn[i].t[:], ALU.mult, n_[2], [zz[s][0], Sn[i]])
            tt_('pool', n_[3].t[:], zz[s][1].t[:], Cn[i].t[:], ALU.mult, n_[3], [zz[s][1], Cn[i]])
            tt_('pool', xst[i][0].t[:], n_[0].t[:], n_[1].t[:], ALU.subtract, xst[i][0], [n_[0], n_[1]])
            tt_('pool', xst[i][1].t[:], n_[2].t[:], n_[3].t[:], ALU.add, xst[i][1], [n_[2], n_[3]])
            P.op('pe', lambda e, yps=yps, i=i: e.matmul(yps.t[:], lhsT=cts.t[:, i, 0, :], rhs=xst[i][0].t[:], start=(i == 0), stop=False), outs=[yps], ins=[cts, xst[i][0]])
            P.op('pe', lambda e, yps=yps, i=i: e.matmul(yps.t[:], lhsT=cts.t[:, i, 1, :], rhs=xst[i][1].t[:], start=False, stop=(i == 3)), outs=[yps], ins=[cts, xst[i][1]])
        y_ = yv[c % 2]
        P.op('dve', lambda e, y_=y_, u=u, yps=yps: e.scalar_tensor_tensor(out=y_.t[:], in0=u.t[:], scalar=dsk.t[:, 0:1], in1=yps.t[:], op0=ALU.mult, op1=ALU.add),
             outs=[y_], ins=[u, dsk, yps])
        P.op('act', lambda e, y_=y_: e.activation(out=y_.t[:], in_=y_.t[:], func=AF.Gelu_apprx_tanh), outs=[y_], ins=[y_])
        P.dma(V(ydT, ydT.t[:, c * T5:(c + 1) * T5]), y_[:])

    NR = 3
    kf = [P.sb([64, 2, 128], F32, f"kf{i}") for i in range(NR)]
    kb = [P.sb([64, 2, 128], BF16, f"kb{i}") for i in range(NR)]
    vf = [P.sb([128, 128], F32, f"vf{i}") for i in range(NR)]
    vb = [P.sb([128, 2, 65], BF16, f"vb{i}") for i in range(NR)]
    for i in range(NR):
        P.op('pool', lambda e, i=i: e.memset(vb[i].t[:], 1.0), outs=[vb[i]])
    qf = [P.sb([64, 8, 128], F32, f"qf{i}") for i in range(2)]
    qb = [P.sb([64, 8, 128], BF16, f"qb{i}") for i in range(2)]
    ex = [P.sb([128, 512], BF16, f"ex{i}") for i in range(2)]
    pm = [P.sb([128, 512], BF16, f"pm{i}") for i in range(4)]
    lsum = P.sb([128, 4], F32, "lsum")
    osb = [P.sb([128, 512], F32, f"osb{i}") for i in range(2)]
    otr = [P.sb([128, 128], F32, f"otr{i}") for i in range(2)]
    pmi = [0]

    def load_kv(j):
        r = j % NR
        P.dma(kf[r][:], V(kT, kT.t[:, j * 128:(j + 1) * 128].rearrange("(g d) k -> d g k", g=2)))
        P.op('pool', lambda e: e.tensor_copy(out=kb[r].t[:], in_=kf[r].t[:]), outs=[kb[r]], ins=[kf[r]])
        P.dma(vf[r][:], V(vT, vT.t[:, j * 128:(j + 1) * 128]))
        tp = ps[6]
        P.op('pe', lambda e: e.transpose(out=tp.t[:, 0:128], in_=vf[r].t[:], identity=ident.t[:]), outs=[tp], ins=[vf[r], ident])
        P.op('dve', lambda e: e.tensor_copy(out=vb[r].t[:, :, 0:64], in_=tp.t[:, 0:128].rearrange("k (g d) -> k g d", g=2)), outs=[vb[r]], ins=[tp])

    load_kv(0)
    for t in range(nq):
        load_kv(t + 1)
        qf_, qb_ = qf[t % 2], qb[t % 2]
        P.dma(qf_[:], V(qT, qT.t[:, t * 128:(t + 1) * 128].rearrange("(h d) q -> d h q", h=8)))
        P.op('pool', lambda e, qf_=qf_, qb_=qb_: e.tensor_copy(out=qb_.t[:], in_=qf_.t[:]), outs=[qb_], ins=[qf_])
        o_ = osb[t % 2]
        for g in range(2):
            ops_ = ps[7]
            P.op('pe', lambda e, ops_=ops_: e.matmul(ops_.t[:, 0:260], lhsT=zl.t[:], rhs=zr_.t[:, 0:260], start=True, stop=False), outs=[ops_], ins=[zl, zr_])
            for which in range(2):
                r = (t + which) % NR
                sp = ps[which]
                P.op('pe', lambda e, sp=sp, r=r, g=g, qb_=qb_: e.matmul(sp.t[:], lhsT=kb[r].t[:, g, :], rhs=qb_.t[:, 4 * g:4 * g + 4, :].rearrange("d h q -> d (h q)"),
                                                                      start=True, stop=True), outs=[sp], ins=[kb[r], qb_])
                e_ = ex[which]
                P.op('act', lambda e, sp=sp, e_=e_: e.activation(out=e_.t[:], in_=sp.t[:], func=AF.Exp, scale=0.125), outs=[e_], ins=[sp])
                mk = mcur if which == 1 else (mp0 if t == 0 else mprev)
                p_ = pm[pmi[0] % 4]; pmi[0] += 1
                P.op('dve', lambda e, p_=p_, e_=e_, mk=mk: e.tensor_tensor(out=p_.t[:], in0=e_.t[:], in1=mk.t[:], op=ALU.mult), outs=[p_], ins=[e_, mk])
                for h in range(4):
                    last = (which == 1 and h == 3)
                    P.op('pe', lambda e, ops_=ops_, p_=p_, r=r, g=g, h=h, last=last: e.matmul(ops_.t[:, h * 65:(h + 1) * 65], lhsT=p_.t[:, h * 128:(h + 1) * 128], rhs=vb[r].t[:, g, :],
                                                                                            start=False, stop=last), outs=[ops_], ins=[p_, vb[r]])
            ov = ops_.t[:, 0:260].rearrange("q (h e) -> q h e", h=4)
            P.op('dve', lambda e, ov=ov, g=g: e.tensor_tensor(out=lsum.t[:], in0=ov[:, :, 64], in1=esink.t[:, 4 * g:4 * g + 4], op=ALU.add), outs=[lsum], ins=[ops_, esink])
            P.op('dve', lambda e: e.reciprocal(out=lsum.t[:], in_=lsum.t[:]), outs=[lsum], ins=[lsum])
            for h in range(4):
                P.op('dve', lambda e, ov=ov, o_=o_, g=g, h=h: e.tensor_scalar(out=o_.t[:, (4 * g + h) * 64:(4 * g + h + 1) * 64], in0=ov[:, h, 0:64], scalar1=lsum.t[:, h:h + 1],
                                                                           scalar2=None, op0=ALU.mult), outs=[o_], ins=[ops_, lsum])
        for cc in range(4):
            tp = ps[6]
            P.op('pe', lambda e, tp=tp, o_=o_, cc=cc: e.transpose(out=tp.t[:, 0:128], in_=o_.t[:, cc * 128:(cc + 1) * 128], identity=ident.t[:]), outs=[tp], ins=[o_, ident])
            ot = otr[cc % 2]
            P.op('act', lambda e, tp=tp, ot=ot: e.copy(out=ot.t[:], in_=tp.t[:, 0:128]), outs=[ot], ins=[tp])
            P.dma(V(ocT, ocT.t[cc * 128:(cc + 1) * 128, t * 128:(t + 1) * 128]), ot[:])
    return P.finish()


VSC = 2048


def nsa_phase(P, L, ident, nsa_in, onT):
    qT4, kcT, vcT, ksT, vsT, kwT, vwT, gtT, w1d, peT, b1d, w2d, b2k, b2v = nsa_in
    NQ = L // 128
    NC = L // 16 - 1
    NCT = (NC + 1 + 127) // 128
    NCp = NCT * 128
    NJ = L // 64
    NJC = (NJ + 127) // 128
    JW = NJC * 128
    kc_bf = P.sb([64, NCp], F32, "n_kc"); vc = P.sb([128, NCT, 65], F32, "n_vc")
    P.memset(kc_bf, kc_bf.t[:], 0.0); P.memset(vc, vc.t[:], 1.0)
    with P.scope():
        psB = [P.ps([128, 512], F32, f"nB_ps{i}") for i in range(3)]
        w1 = P.sb([64, 32, 256], F32, "nB_w1"); hid = P.sb([128, 2, NCp], F32, "nB_hid")
        kraw = P.sb([64, 512 * 16 + 16], F32, "nB_kraw")
        pes = P.sb([64, 2, 32], F32, "nB_pe"); b1s = P.sb([128, 2, 2], F32, "nB_b1"); w2s = P.sb([128, 2, 2, 64], F32, "nB_w2")
        b2ks = P.sb([64, 1], F32, "nB_b2k"); b2vs = P.sb([128, 64], F32, "nB_b2v"); hidb = P.sb([128, 2], F32, "nB_hidb")
        P.dma(pes[:], peT[:]); P.dma(b1s[:], b1d[:]); P.dma(w2s[:], w2d[:]); P.dma(b2ks[:], b2k[:])
        P.dma(b2vs[:], V(b2v, b2v.t.partition_broadcast(128)))
        P.memset(hid, hid.t[:], 0.0)
        for kv in range(2):
            src = kcT if kv == 0 else vcT
            for jj in range(4):
                P.dma(V(w1, w1.t[:, jj * 8:(jj + 1) * 8, :]), V(w1d, w1d.t[kv, :, jj * 8:(jj + 1) * 8, :]))
            for hc in range(2):
                pp = psB[2]
                for j in range(32):
                    P.mm(pp, pp.t[:, 0:1], w1.t[:, j, hc * 128:(hc + 1) * 128], pes.t[:, kv, j:j + 1], j == 0, j == 31, [w1, pes])
                P.tt('dve', hidb, hidb.t[:, hc:hc + 1], pp.t[:, 0:1], b1s.t[:, kv, hc:hc + 1], ALU.add, [pp, b1s])
            for c0 in range(0, NC, 512):
                n = min(512, NC - c0)
                P.dma(V(kraw, kraw.t[:, 0:16 * n + 16]), V(src, src.t[:, 16 * c0:16 * c0 + 16 * n + 16]))
                for hc in range(2):
                    pp = psB[hc]
                    for j in range(32):
                        P.mm(pp, pp.t[:, 0:n], w1.t[:, j, hc * 128:(hc + 1) * 128], kraw.t[:, j:j + 16 * (n - 1) + 1:16], j == 0, j == 31, [w1, kraw])
                    P.act(hid, hid.t[:, hc, c0:c0 + n], pp.t[:, 0:n], AF.Gelu_apprx_tanh, [pp, hidb], bias=hidb.t[:, hc:hc + 1], scale=1.0)
            if kv == 0:
                for c0 in range(0, NC, 512):
                    n = min(512, NC - c0)
                    pp = psB[2]
                    for hc in range(2):
                        P.mm(pp, pp.t[0:64, 0:n], w2s.t[:, 0, hc, :], hid.t[:, hc, c0:c0 + n], hc == 0, hc == 1, [w2s, hid])
                    P.act(kc_bf, kc_bf.t[:, c0:c0 + n], pp.t[0:64, 0:n], AF.Identity, [pp, b2ks], bias=b2ks.t[:, 0:1], scale=1.0)
            else:
                for ct in range(NCT):
                    pp = psB[2]
                    for hc in range(2):
                        P.mm(pp, pp.t[:, 0:64], hid.t[:, hc, ct * 128:(ct + 1) * 128], w2s.t[:, 1, hc, :], hc == 0, hc == 1, [hid, w2s])
                    P.tt('dve', vc, vc.t[:, ct, 0:64], pp.t[:, 0:64], b2vs.t[:], ALU.add, [pp, b2vs])
    with P.scope():
        ks_bf = P.sb([64, L], BF16, "n_ks"); kw_bf = P.sb([64, L], BF16, "n_kw")
        vs = P.sb([128, NQ, 65], BF16, "n_vs"); vw = P.sb([128, NQ, 65], BF16, "n_vw")
        P.memset(vs, vs.t[:], 1.0); P.memset(vw, vw.t[:], 1.0)
        WEXP = min(64, NQ) * 128
        wexp = P.sb([128, WEXP], BF16, "n_wexp")
        S_ps = [P.ps([128, 512], F32, f"nC_S{i}") for i in range(3)]
        Oc_ps = P.ps([128, 512], F32, "nC_Oc"); Os_ps = P.ps([128, 512], F32, "nC_Osw")
        Ow_ps = Os_ps
        imp_ps = P.ps([128, 1024], F32, "nC_imp"); tp_ps = P.ps([128, 512], F32, "nC_tp")
        with P.scope():
            iw = P.sb([128, 2048], I32, "nc_iw"); wa = P.sb([128, 2048], F32, "nc_wa"); wb_ = P.sb([128, 2048], F32, "nc_wb")
            for pc in range(WEXP // 2048 if WEXP >= 2048 else 1):
                w = min(2048, WEXP)
                P.op('pool', lambda e, pc=pc, w=w: e.iota(iw.t[:, 0:w], pattern=[[1, w]], base=2048 * pc, channel_multiplier=-64), outs=[iw])
                P.ts('dve', wa, wa.t[:, 0:w], iw.t[:, 0:w], 0.0, ALU.is_ge, [iw])
                P.ts('dve', wb_, wb_.t[:, 0:w], iw.t[:, 0:w], 63.0, ALU.is_le, [iw])
                P.tt('dve', wexp, wexp.t[:, 2048 * pc:2048 * pc + w], wa.t[:, 0:w], wb_.t[:, 0:w], ALU.mult, [wa, wb_])
            for (src, dst) in ((ksT, ks_bf), (kwT, kw_bf)):
                for c0 in range(0, L, 8192):
                    w = min(8192, L - c0)
                    P.op('pool', lambda e, src=src, dst=dst, c0=c0, w=w: e.dma_start(out=dst.t[:, c0:c0 + w], in_=src.t[:, c0:c0 + w], max_dma_last_dim=4096),
                         outs=[dst], ins=[src], ctr='dpool')
            vraw = P.sb([64, VSC], F32, "nc_vraw")
            for (src, dst) in ((vsT, vs), (vwT, vw)):
                for s0 in range(0, L, VSC):
                    P.dma(vraw[:], V(src, src.t[:, s0:s0 + VSC]))
                    for c in range(VSC // 128):
                        kt = s0 // 128 + c
                        P.tr(tp_ps, tp_ps.t[:, 0:64], vraw.t[:, c * 128:(c + 1) * 128], ident.t[0:64, 0:64], [vraw, ident])
                        P.cp('act' if c % 2 else 'dve', dst, dst.t[:, kt, 0:64], tp_ps.t[:, 0:64], [tp_ps])
        ioA = P.sb([128, 2, 128], I32, "n_ioA")
        P.op('pool', lambda e: e.iota(ioA.t[:], pattern=[[0, 2], [1, 128]], base=0, channel_multiplier=-1), outs=[ioA])
        mcur = P.sb([128, 2, 128], BF16, "n_mcur"); mprev = P.sb([128, 2, 128], BF16, "n_mprev")
        P.ts('dve', mcur, mcur.t[:], ioA.t[:], 0.0, ALU.is_ge, [ioA])
        P.ts('dve', mprev, mprev.t[:], ioA.t[:], 0.0, ALU.is_lt, [ioA])
        ioR = P.sb([128, 128], I32, "n_ioR"); Rt = P.sb([128, 128], F32, "n_R")
        P.op('pool', lambda e: e.iota(ioR.t[:], pattern=[[1, 128]], base=0, channel_multiplier=-16), outs=[ioR])
        P.cp('dve', Rt, Rt.t[:], ioR.t[:], [ioR])
        ioO = P.sb([128, 33], I32, "n_ioO"); ova = P.sb([128, 33], F32, "n_ova"); OVt = P.sb([128, 33], F32, "n_OVt")
        P.op('pool', lambda e: e.iota(ioO.t[:], pattern=[[-4, 33]], base=0, channel_multiplier=1), outs=[ioO])
        P.ts('dve', ova, ova.t[:], ioO.t[:], -1.0, ALU.is_ge, [ioO])
        P.ts('dve', OVt, OVt.t[:], ioO.t[:], 3.0, ALU.is_le, [ioO])
        P.tt('dve', OVt, OVt.t[:], OVt.t[:], ova.t[:], ALU.mult, [OVt, ova])
        ioJ = P.sb([128, JW], I32, "n_ioJ"); ioP = P.sb([128, 1], I32, "n_ioP"); Jt = P.sb([128, JW], F32, "n_J"); pge = P.sb([128, 1], F32, "n_pge")
        P.op('pool', lambda e: e.iota(ioJ.t[:], pattern=[[1, JW]], base=0, channel_multiplier=0), outs=[ioJ])
        P.op('pool', lambda e: e.iota(ioP.t[:], pattern=[[0, 1]], base=0, channel_multiplier=1), outs=[ioP])
        P.ts('dve', pge, pge.t[:], ioP.t[:], 64.0, ALU.is_ge, [ioP])
        P.ts('dve', Jt, Jt.t[:], ioJ.t[:], pge.t[:, 0:1], ALU.subtract, [ioJ, pge])
        zl = P.sb([1, 128], F32, "n_zl"); zr = P.sb([1, 512], F32, "n_zr"); zlb = P.sb([1, 128], BF16, "n_zlb"); zrb = P.sb([1, 512], BF16, "n_zrb")
        for t_ in (zl, zr, zlb, zrb):
            P.memset(t_, t_.t[:], 0.0)
        q4f = [P.sb([64, 4, 128], F32, f"n_q4f{i}") for i in range(2)]; q4b = [P.sb([64, 4, 128], BF16, f"n_q4b{i}") for i in range(2)]
        gtf = P.sb([6, 128], F32, "n_gtf"); gs = P.sb([128, 6], F32, "n_gs")
        Ef = [P.sb([128, 512], F32, f"n_Ef{i}") for i in range(2)]; cmk = P.sb([128, 128], F32, "n_cmk")
        Eb = [P.sb([128, 256], BF16, f"n_Eb{i}") for i in range(4)]
        rl4 = P.sb([128, 4], F32, "n_rl4"); rl2 = P.sb([128, 2], F32, "n_rl2"); coef = P.sb([128, 2], F32, "n_coef")
        imp = P.sb([128, JW], F32, "n_imp"); imp2 = P.sb([128, JW], F32, "n_imp2"); fbA = P.sb([128, JW], F32, "n_fbA")
        m8a = P.sb([128, 8], F32, "n_m8a"); m8b = P.sb([128, 8], F32, "n_m8b")
        nb = P.sb([128, JW], F32, "n_nb"); nbT = [P.sb([128, 128], BF16, f"n_nbT{i}") for i in range(NJC)]
        acc = P.sb([128, 128], F32, "n_acc"); accT = [P.sb([128, 128], F32, f"n_accT{i}") for i in range(2)]
        ebi = [0]

        def next_eb():
            ebi[0] += 1
            return Eb[ebi[0] % 4]

        def combine(O_ps, br, first):
            ov = O_ps.t[:, 0:130].rearrange("q (h e) -> q h e", h=2)
            P.ts('dve', rl2, rl2.t[:], ov[:, :, 64], 1e-30, ALU.max, [O_ps])
            P.op('dve', lambda e: e.reciprocal(out=rl2.t[:], in_=rl2.t[:]), outs=[rl2], ins=[rl2])
            P.tt('dve', coef, coef.t[:], rl2.t[:], gs.t[:, :].rearrange("q (h b) -> q h b", h=2)[:, :, br], ALU.mult, [rl2, gs])
            for hh in range(2):
                if first:
                    P.ts('dve', acc, acc.t[:, hh * 64:(hh + 1) * 64], ov[:, hh, 0:64], coef.t[:, hh:hh + 1], ALU.mult, [O_ps, coef])
                else:
                    P.stt(acc, acc.t[:, hh * 64:(hh + 1) * 64], ov[:, hh, 0:64], coef.t[:, hh:hh + 1], acc.t[:, hh * 64:(hh + 1) * 64], ALU.mult, ALU.add,
                          [O_ps, coef, acc])

        for qt in range(NQ):
            tok = slice(qt * 128, (qt + 1) * 128)
            qf, qb = q4f[qt % 2], q4b[qt % 2]
            P.dma(qf[:], V(qT4, qT4.t[:, tok].rearrange("(h d) q -> d h q", h=4)))
            P.cp('pool', qb, qb.t[:], qf.t[:], [qf])
            P.dma(gtf[:], V(gtT, gtT.t[:, tok]))
            P.tr(tp_ps, tp_ps.t[:, 0:6], gtf.t[:], ident.t[0:6, 0:6], [gtf, ident])
            P.act(gs, gs.t[:], tp_ps.t[:, 0:6], AF.Sigmoid, [tp_ps])
            q4v = qb.t[:].rearrange("d h q -> d (h q)")
            qov = qb.t[:, 0:2, :].rearrange("d h q -> d (h q)")
            P.mm(Oc_ps, Oc_ps.t[:, 0:260], zl.t[:], zr.t[:, 0:260], True, False, [zl, zr])
            P.mm(imp_ps, imp_ps.t[:, 0:512], zl.t[:], zr.t[:, 0:512], True, False, [zl, zr])
            P.mm(imp_ps, imp_ps.t[:, 512:1024], zl.t[:], zr.t[:, 0:512], True, False, [zl, zr])
            nct = (8 * qt + 6) // 128 + 1
            def cmp_S(ct):
                sp = S_ps[ct % 3]
                P.mm(sp, sp.t[:], kc_bf.t[:, ct * 128:(ct + 1) * 128], qf.t[:].rearrange("d h q -> d (h q)"), True, True, [kc_bf, qf])
            cmp_S(0)
            for ct in range(nct):
                if ct + 1 < nct:
                    cmp_S(ct + 1)
                sp = S_ps[ct % 3]
                ef = Ef[ct % 2]
                P.act(ef, ef.t[:], sp.t[:], AF.Exp, [sp], scale=0.125)
                thr = 2048 * ct + 31 - 128 * qt
                if thr > -2032:
                    P.ts('dve', cmk, cmk.t[:], Rt.t[:], float(thr), ALU.is_ge, [Rt])
                    P.tt('dve', ef, ef.t[:].rearrange("c (h q) -> c h q", h=4), ef.t[:].rearrange("c (h q) -> c h q", h=4),
                         cmk.t[:, :].unsqueeze(1).to_broadcast([128, 4, 128]), ALU.mult, [ef, cmk])
                last = ct == nct - 1
                for h in range(4):
                    P.mm(Oc_ps, Oc_ps.t[:, h * 65:(h + 1) * 65], ef.t[:, h * 128:(h + 1) * 128], vc.t[:, ct, :], False, last and h == 3, [ef, vc])
                ncol = min(33, NJ - 32 * ct)
                for h in range(4):
                    P.mm(imp_ps, imp_ps.t[:, h * 256 + 32 * ct:h * 256 + 32 * ct + ncol], ef.t[:, h * 128:(h + 1) * 128], OVt.t[:, 0:ncol], False, last and h in (1, 3),
                         [ef, OVt])
            ocv = Oc_ps.t[:, 0:260].rearrange("q (h e) -> q h e", h=4)
            P.ts('dve', rl4, rl4.t[:], ocv[:, :, 64], 1e-30, ALU.max, [Oc_ps])
            P.op('dve', lambda e: e.reciprocal(out=rl4.t[:], in_=rl4.t[:]), outs=[rl4], ins=[rl4])
            P.ts('dve', imp, imp.t[:, 0:JW], imp_ps.t[:, 0:JW], rl4.t[:, 0:1], ALU.mult, [imp_ps, rl4])
            for h in range(1, 4):
                P.stt(imp, imp.t[:, 0:JW], imp_ps.t[:, h * 256:h * 256 + JW], rl4.t[:, h:h + 1], imp.t[:, 0:JW], ALU.mult, ALU.add, [imp_ps, rl4, imp])
            combine(Oc_ps, 0, True)
            P.ts('dve', fbA, fbA.t[:], Jt.t[:], float(2 * qt - 1), ALU.is_ge, [Jt], s2=1e9, op1=ALU.mult)
            P.tt('dve', imp2, imp2.t[:], imp.t[:], fbA.t[:], ALU.add, [imp, fbA])
            P.ts('dve', fbA, fbA.t[:], Jt.t[:], float(2 * qt), ALU.is_gt, [Jt], s2=-2e9, op1=ALU.mult)
            P.tt('dve', imp2, imp2.t[:], imp2.t[:], fbA.t[:], ALU.add, [imp2, fbA])
            P.memset(imp2, imp2.t[:, 0:1], 1e9, eng='dve')
            P.op('dve', lambda e: e.max(out=m8a.t[:], in_=imp2.t[:]), outs=[m8a], ins=[imp2])
            P.op('dve', lambda e: e.match_replace(out=imp.t[:], in_to_replace=m8a.t[:], in_values=imp2.t[:], imm_value=-3e9), outs=[imp], ins=[m8a, imp2])
            P.op('dve', lambda e: e.max(out=m8b.t[:], in_=imp.t[:]), outs=[m8b], ins=[imp])
            P.ts('dve', nb, nb.t[:], imp2.t[:], m8b.t[:, 7:8], ALU.is_ge, [imp2, m8b])
            P.ts('dve', nb, nb.t[:], nb.t[:], -NEGB, ALU.mult, [nb], s2=NEGB, op1=ALU.add)
            njc = (2 * qt + 1) // 128 + 1
            for jc in range(njc):
                P.tr(tp_ps, tp_ps.t[:, 0:128], nb.t[:, jc * 128:(jc + 1) * 128], ident.t[:], [nb, ident])
                P.cp('act', nbT[jc], nbT[jc].t[:], tp_ps.t[:, 0:128], [tp_ps])
            P.mm(Os_ps, Os_ps.t[:, 0:130], zlb.t[:], zrb.t[:, 0:130], True, False, [zlb, zrb])
            def slc_S(kt):
                sp = S_ps[kt % 3]
                P.mm(sp, sp.t[:, 0:256], ks_bf.t[:, kt * 128:(kt + 1) * 128], qov, True, False, [ks_bf, qb])
                jc = kt // 64
                P.mm(sp, sp.t[:, 0:256], wexp.t[:, 128 * (kt % 64):128 * (kt % 64) + 128], nbT[jc].t[:, :].unsqueeze(1).to_broadcast([128, 2, 128]),
                     False, True, [wexp, nbT[jc]])
            slc_S(0)
            if qt >= 1:
                slc_S(1)
            for kt in range(qt + 1):
                if kt + 2 <= qt:
                    slc_S(kt + 2)
                sp = S_ps[kt % 3]
                eb = next_eb()
                P.act(eb, eb.t[:], sp.t[:, 0:256], AF.Exp, [sp], scale=0.125)
                if kt == qt:
                    P.tt('dve', eb, eb.t[:], eb.t[:], mcur.t[:].rearrange("k h q -> k (h q)"), ALU.mult, [eb, mcur])
                for hh in range(2):
                    P.mm(Os_ps, Os_ps.t[:, hh * 65:(hh + 1) * 65], eb.t[:, hh * 128:(hh + 1) * 128], vs.t[:, kt, :], False, kt == qt and hh == 1, [eb, vs])
            combine(Os_ps, 1, False)
            P.mm(Ow_ps, Ow_ps.t[:, 0:130], zlb.t[:], zrb.t[:, 0:130], True, False, [zlb, zrb])
            k0 = max(0, qt - 4)

            def win_S(kt):
                sp = S_ps[kt % 3]
                P.mm(sp, sp.t[:, 0:256], kw_bf.t[:, kt * 128:(kt + 1) * 128], qov, True, True, [kw_bf, qb])
            win_S(k0)
            for kt in range(k0, qt + 1):
                if kt + 1 <= qt:
                    win_S(kt + 1)
                sp = S_ps[kt % 3]
                eb = next_eb()
                P.act(eb, eb.t[:], sp.t[:, 0:256], AF.Exp, [sp], scale=0.125)
                if kt == qt:
                    P.tt('dve', eb, eb.t[:], eb.t[:], mcur.t[:].rearrange("k h q -> k (h q)"), ALU.mult, [eb, mcur])
                if kt == qt - 4:
                    P.tt('dve', eb, eb.t[:], eb.t[:], mprev.t[:].rearrange("k h q -> k (h q)"), ALU.mult, [eb, mprev])
                for hh in range(2):
                    P.mm(Ow_ps, Ow_ps.t[:, hh * 65:(hh + 1) * 65], eb.t[:, hh * 128:(hh + 1) * 128], vw.t[:, kt, :], False, kt == qt and hh == 1, [eb, vw])
            combine(Ow_ps, 2, False)
            P.tr(tp_ps, tp_ps.t[:, 0:128], acc.t[:], ident.t[:], [acc, ident])
            at = accT[qt % 2]
            P.cp('act', at, at.t[:], tp_ps.t[:, 0:128], [tp_ps])
            P.dma(V(onT, onT.t[:, tok]), at[:])


SC = 2048


def ssd_phase(P, L, ps, ident, ssd_in, ygT):
    zT, xT, bT_, cT_, dtT, cw, cb, hp, dsk_d = ssd_in
    NSC = L // SC
    io = P.sb([128, 128], I32, "s_io")
    P.op('pool', lambda e: e.iota(io.t[:], pattern=[[1, 128]], base=0, channel_multiplier=-1), outs=[io])
    negm = P.sb([128, 128], F32, "s_negm")
    P.ts('dve', negm, negm.t[:], io.t[:], 0.0, ALU.is_ge, [io], s2=None)
    P.ts('dve', negm, negm.t[:], negm.t[:], -NEGB, ALU.mult, [negm], s2=NEGB, op1=ALU.add)
    io2 = P.sb([2, SC // 128, 128], I32, "s_io2")
    P.op('pool', lambda e: e.iota(io2.t[:], pattern=[[0, SC // 128], [1, 128]], base=0, channel_multiplier=0), outs=[io2])
    rmask = P.sb([2, SC], F32, "s_rmask")
    P.ts('dve', rmask, rmask.t[:], io2.t[:].rearrange("p a b -> p (a b)"), 0.0, ALU.is_gt, [io2])
    io3 = P.sb([2, 2, 128], I32, "s_io3")
    P.op('pool', lambda e: e.iota(io3.t[:], pattern=[[1, 2], [0, 128]], base=0, channel_multiplier=-1), outs=[io3])
    sel = P.sb([2, 2, 128], F32, "s_sel")
    P.ts('dve', sel, sel.t[:], io3.t[:], 0.0, ALU.is_equal, [io3])
    cws = P.sb([128, 3, 4], F32, "s_cw"); cbs = P.sb([128, 3], F32, "s_cb"); hps = P.sb([2, 3], F32, "s_hp"); dsk = P.sb([128, 1], F32, "s_dsk")
    P.dma(cws[:], cw[:]); P.dma(cbs[:], cb[:]); P.dma(hps[:], hp[:]); P.dma(dsk[:], dsk_d[:])
    na = P.sb([2, 1], F32, "s_na")
    P.act(na, na.t[:], hps.t[:, 1:2], AF.Exp, [hps])
    P.ts('dve', na, na.t[:], na.t[:], -1.0, ALU.mult, [na])
    hf = P.sb([128, 128], F32, "s_hf"); hb = P.sb([128, 128], BF16, "s_hb")
    P.memset(hf, hf.t[:], 0.0); P.memset(hb, hb.t[:], 0.0)
    raw = P.sb([128, SC + 3], F32, "s_raw"); acc = P.sb([128, SC], F32, "s_acc")
    xs = P.sb([128, SC], F32, "s_xs"); Bs = P.sb([128, SC], BF16, "s_Bs"); Cs = P.sb([128, SC], BF16, "s_Cs")
    zs = P.sb([128, SC], F32, "s_zs"); ysc = P.sb([128, SC], F32, "s_ysc")
    dtr = P.sb([2, SC], F32, "s_dtr"); dts = P.sb([2, SC], F32, "s_dt"); acum = P.sb([2, SC], F32, "s_acum")
    small = P.sb([128, 4], F32, "s_small"); Btok = P.sb([128, 128], BF16, "s_Btok")
    Dsb = [P.sb([128, 128], F32, f"s_D{r}") for r in range(2)]
    Esb = [P.sb([128, 128], F32, f"s_E{r}") for r in range(2)]
    Msb = [P.sb([128, 128], BF16, f"s_M{r}") for r in range(2)]
    EBs = [P.sb([128, 128], F32, f"s_EB{r}") for r in range(2)]
    Csr = [P.sb([128, 128], BF16, f"s_Csr{r}") for r in range(2)]
    xd = P.sb([128, 128], BF16, "s_xd"); xdd = P.sb([128, 128], BF16, "s_xdd")
    identb = P.sb([128, 128], BF16, "s_identb")
    P.cp('dve', identb, identb.t[:], ident.t[:], [ident])
    tp1, g_ps, bc_ps, y_ps, S_ps = ps[0], ps[1], ps[2], ps[3], ps[4]
    psb = P.ps([128, 128], BF16, "s_psb")

    def conv_silu(src, which, out_tt, s):
        P.dma(raw[:], V(src, src.t[:, s * SC:s * SC + SC + 3]))
        P.ts('dve', acc, acc.t[:], raw.t[:, 0:SC], cws.t[:, which, 0:1], ALU.mult, [raw, cws])
        for k in range(1, 4):
            P.stt(acc, acc.t[:], raw.t[:, k:SC + k], cws.t[:, which, k:k + 1], acc.t[:], ALU.mult, ALU.add, [raw, cws, acc])
        P.act(out_tt, out_tt.t[:], acc.t[:], AF.Silu, [acc, cbs], bias=cbs.t[:, which:which + 1], scale=1.0)

    for s in range(NSC):
        conv_silu(xT, 0, xs, s)
        conv_silu(bT_, 1, Bs, s)
        conv_silu(cT_, 2, Cs, s)
        P.dma(acc[:], V(zT, zT.t[:, s * SC:(s + 1) * SC]))
        P.act(zs, zs.t[:], acc.t[:], AF.Silu, [acc])
        P.dma(dtr[:], V(dtT, dtT.t[:, s * SC:(s + 1) * SC]))
        P.act(dtr, dtr.t[:], dtr.t[:], AF.Exp, [dtr, hps], bias=hps.t[:, 0:1], scale=1.0)
        P.act(dts, dts.t[:], dtr.t[:], AF.Ln, [dtr], bias=1.0, scale=1.0)
        P.ts('dve', dtr, dtr.t[:], dts.t[:], na.t[:, 0:1], ALU.mult, [dts, na])
        P.op('dve', lambda e: e.tensor_tensor_scan(out=acum.t[:], data0=rmask.t[:], data1=dtr.t[:], initial=0.0, op0=ALU.mult, op1=ALU.add),
             outs=[acum], ins=[rmask, dtr])
        for c in range(SC // 128):
            o = slice(c * 128, (c + 1) * 128)
            P.tr(tp1, tp1.t[:, 0:128], xs.t[:, o], ident.t[:], [xs, ident])
            P.tr(tp1, tp1.t[:, 128:130], dts.t[0:2, o], ident.t[0:2, 0:2], [dts, ident])
            P.tr(tp1, tp1.t[:, 130:132], acum.t[0:2, o], ident.t[0:2, 0:2], [acum, ident])
            P.cp('dve', small, small.t[:], tp1.t[:, 128:132], [tp1])
            P.tr(psb, psb.t[:], Bs.t[:, o], identb.t[:], [Bs, identb])
            P.cp('act', Btok, Btok.t[:], psb.t[:], [psb])
            P.mm(g_ps, g_ps.t[:, 0:128], Bs.t[:, o], Cs.t[:, o], True, True, [Bs, Cs])
            for r in range(2):
                P.mm(bc_ps, bc_ps.t[:, r * 128:(r + 1) * 128], sel.t[:, r, :], acum.t[0:2, o], True, True, [sel, acum])
            for r in range(2):
                bcr = bc_ps.t[:, r * 128:(r + 1) * 128]
                P.stt(Dsb[r], Dsb[r].t[:], bcr, small.t[:, 2 + r:3 + r], negm.t[:], ALU.subtract, ALU.add, [bc_ps, small, negm])
                P.act(Esb[r], Esb[r].t[:], Dsb[r].t[:], AF.Exp, [Dsb[r]])
                P.tt('dve', Msb[r], Msb[r].t[:], g_ps.t[:, 0:128], Esb[r].t[:], ALU.mult, [g_ps, Esb[r]])
                P.act(EBs[r], EBs[r].t[:], bcr, AF.Exp, [bc_ps])
                P.tt('pool', Csr[r], Csr[r].t[:], Cs.t[:, o], EBs[r].t[:], ALU.mult, [Cs, EBs[r]])
                P.ts('dve', xdd, xdd.t[:, r * 64:(r + 1) * 64], tp1.t[:, r * 64:(r + 1) * 64], small.t[:, r:r + 1], ALU.mult, [tp1, small, Esb[r]],
                     s2=Esb[r].t[:, 127:128], op1=ALU.mult)
                P.ts('dve', xd, xd.t[:, r * 64:(r + 1) * 64], tp1.t[:, r * 64:(r + 1) * 64], small.t[:, r:r + 1], ALU.mult, [tp1, small])
            for r in range(2):
                P.mm(y_ps, y_ps.t[64 * r:64 * r + 64, 0:128], xd.t[:, r * 64:(r + 1) * 64], Msb[r].t[:], True, False, [xd, Msb[r]])
                P.mm(y_ps, y_ps.t[64 * r:64 * r + 64, 0:128], hb.t[:, r * 64:(r + 1) * 64], Csr[r].t[:], False, True, [hb, Csr[r]])
            P.mm(S_ps, S_ps.t[:, 0:128], Btok.t[:], xdd.t[:], True, True, [Btok, xdd])
            for r in range(2):
                P.stt(hf, hf.t[:, r * 64:(r + 1) * 64], hf.t[:, r * 64:(r + 1) * 64], EBs[r].t[:, 127:128], S_ps.t[:, r * 64:(r + 1) * 64],
                      ALU.mult, ALU.add, [hf, EBs[r], S_ps])
            P.cp('pool', hb, hb.t[:], hf.t[:], [hf])
            P.stt(ysc, ysc.t[:, o], xs.t[:, o], dsk.t[:, 0:1], y_ps.t[:, 0:128], ALU.mult, ALU.add, [xs, dsk, y_ps])
        P.tt('pool', ysc, ysc.t[:], ysc.t[:], zs.t[:], ALU.mult, [ysc, zs])
        P.dma(V(ygT, ygT.t[:, s * SC:(s + 1) * SC]), ysc[:])


def build_even(L=16384, do_ssd=True, do_nsa=True):
    P = Prog()
    io = P.sb([128, 128], I32, "io0")
    P.op('pool', lambda e: e.iota(io.t[:], pattern=[[1, 128]], base=0, channel_multiplier=-1), outs=[io])
    ident = P.sb([128, 128], F32, "ident")
    P.ts('dve', ident, ident.t[:], io.t[:], 0.0, ALU.is_equal, [io])
    if do_ssd:
        zT = P.din("zT", [128, L]); xT = P.din("xT", [128, L + 3]); bT_ = P.din("bT_", [128, L + 3]); cT_ = P.din("cT_", [128, L + 3])
        dtT = P.din("dtT", [2, L]); cw = P.din("cw", [128, 3, 4]); cb = P.din("cb", [128, 3]); hp = P.din("hp", [2, 3]); dsk = P.din("dsk", [128, 1])
        ygT = P.dout("ygT", [128, L])
        with P.scope():
            ps = [P.ps([128, 512], F32, f"ps{i}") for i in range(5)]
            ssd_phase(P, L, ps, ident, (zT, xT, bT_, cT_, dtT, cw, cb, hp, dsk), ygT)
    if do_nsa:
        nsa_in = (P.din("qT4", [256, L]), P.din("kcT", [64, L]), P.din("vcT", [64, L]), P.din("ksT", [64, L]), P.din("vsT", [64, L]),
                  P.din("kwT", [64, L]), P.din("vwT", [64, L]), P.din("gtT", [6, L]), P.din("w1d", [2, 64, 32, 256]), P.din("peT", [64, 2, 32]),
                  P.din("b1d", [128, 2, 2]), P.din("w2d", [128, 2, 2, 64]), P.din("b2k", [64, 1]), P.din("b2v", [64]))
        onT = P.dout("onT", [128, L])
        nsa_phase(P, L, ident, nsa_in, onT)
    return P.finish()


def prep_odd(uT, part, i, inp, nq=32, nch=32):
    NQT = nq * 128
    s0 = part * NQT
    m = {}
    m["qT"] = np.ascontiguousarray(uT[0:512, s0:s0 + NQT])
    kv = np.zeros((256, NQT + 128), np.float32)
    lo = s0 - 128
    if lo >= 0:
        kv[:, :] = uT[512:768, lo:s0 + NQT]
    else:
        kv[:, 128:] = uT[512:768, s0:s0 + NQT]
    m["kT"] = np.ascontiguousarray(kv[0:128]); m["vT"] = np.ascontiguousarray(kv[128:256])
    k = np.arange(128)[:, None]; q = np.arange(128)[None, :]
    mp = np.tile((k > q).astype(np.float32), (1, 4))
    m["mprev0"] = mp if part > 0 else np.zeros_like(mp)
    m["sinks"] = np.ascontiguousarray(inp["swa_sinks"][i])
    L = nch * 512
    c0 = 768 + 128 * part
    m["usT"] = np.ascontiguousarray(uT[c0:c0 + 128, 0:L])
    g0 = 8 * part
    prm = np.zeros((128, 4, 3), np.float32)
    bT = np.zeros((4, 128, 2, 128), np.float32); cT = np.zeros((4, 128, 2, 128), np.float32)
    for t in range(4):
        for gg in range(2):
            gl = 2 * t + gg; g = g0 + gl
            sl = slice(gg * 64, gg * 64 + 64)
            prm[sl, t, 0] = inp["s5_a_re"][i, g]; prm[sl, t, 1] = inp["s5_a_im"][i, g]; prm[sl, t, 2] = inp["s5_log_dt"][i, g]
            ch = slice(gl * 16, gl * 16 + 16)
            bT[t, ch, 0, sl] = inp["s5_b_re"][i, g].T; bT[t, ch, 1, sl] = inp["s5_b_im"][i, g].T
            cT[t, sl, 0, ch] = inp["s5_c_re"][i, g].T; cT[t, sl, 1, ch] = inp["s5_c_im"][i, g].T
    m["s5p"] = prm; m["bT"] = bT; m["cT"] = cT
    m["s5d"] = np.ascontiguousarray(inp["s5_d"][i, c0 - 768:c0 - 768 + 128].reshape(128, 1))
    return m


def prep_ssd(uT, g, half, i, inp, L=16384):
    hh = 4 * g + 2 * half
    m = {}
    m["zT"] = np.ascontiguousarray(uT[64 * hh:64 * hh + 128, :L])
    def pad(rows):
        a = np.zeros((rows.shape[0], L + 3), np.float32); a[:, 3:] = rows[:, :L]; return a
    m["xT"] = pad(uT[512 + 64 * hh:512 + 64 * hh + 128])
    m["bT_"] = pad(uT[1024 + 128 * g:1024 + 128 * g + 128])
    m["cT_"] = pad(uT[1280 + 128 * g:1280 + 128 * g + 128])
    m["dtT"] = np.ascontiguousarray(uT[1536 + hh:1536 + hh + 2, :L])
    cwf = inp["ssd_conv_w"][i]; cbf = inp["ssd_conv_b"][i]
    chs = [slice(64 * hh, 64 * hh + 128), slice(512 + 128 * g, 512 + 128 * g + 128), slice(768 + 128 * g, 768 + 128 * g + 128)]
    cw = np.zeros((128, 3, 4), np.float32); cb = np.zeros((128, 3), np.float32)
    for w, sl in enumerate(chs):
        cw[:, w, :] = cwf[:, sl].T; cb[:, w] = cbf[sl]
    m["cw"] = cw; m["cb"] = cb
    hp = np.zeros((2, 3), np.float32)
    hp[:, 0] = inp["ssd_dt_bias"][i, hh:hh + 2]; hp[:, 1] = inp["ssd_a_log"][i, hh:hh + 2]
    m["hp"] = hp
    m["dsk"] = np.repeat(inp["ssd_d"][i, hh:hh + 2], 64).reshape(128, 1).astype(np.float32)
    return m

def prep_nsa(uT, g, half, i, inp, L=16384):
    m = {}
    base = SSD_IN
    order = [2 * half, 2 * half + 1, 2 * (1 - half), 2 * (1 - half) + 1]
    q = [uT[base + 64 * (4 * g + h):base + 64 * (4 * g + h) + 64, :L] for h in order]
    m["qT4"] = np.ascontiguousarray(np.concatenate(q, 0))
    names = ["kcT", "vcT", "ksT", "vsT", "kwT", "vwT"]
    for n_, nm in enumerate(names):
        r0 = base + 512 + 128 * n_ + 64 * g
        m[nm] = np.ascontiguousarray(uT[r0:r0 + 64, :L])
    g0 = base + 512 + 768 + 12 * g + 6 * half
    m["gtT"] = np.ascontiguousarray(uT[g0:g0 + 6, :L])
    w1 = inp["nsa_cmp_w1"][i]
    m["w1d"] = np.ascontiguousarray(w1.reshape(2, 32, 64, 256).transpose(0, 2, 1, 3))
    m["peT"] = np.ascontiguousarray(inp["nsa_pe"][i].transpose(2, 0, 1))
    m["b1d"] = np.ascontiguousarray(inp["nsa_cmp_b1"][i].reshape(2, 2, 128).transpose(2, 0, 1))
    m["w2d"] = np.ascontiguousarray(inp["nsa_cmp_w2"][i].reshape(2, 2, 128, 64).transpose(2, 0, 1, 3))
    m["b2k"] = np.ascontiguousarray(inp["nsa_cmp_b2"][i, 0].reshape(64, 1))
    m["b2v"] = np.ascontiguousarray(inp["nsa_cmp_b2"][i, 1])
    return m


_PROGS = {}


def _prog(key, fn):
    if key not in _PROGS:
        _PROGS[key] = fn()
    return _PROGS[key]


def _g2(g):
    return np.ascontiguousarray(np.asarray(g, np.float32).reshape(8, 128).T)


def _run(nc, maps):
    res = run_bass_kernel_spmd(nc, maps, core_ids=list(range(8)))
    return res.results


def kernel(**inp):
    inp = {k: np.asarray(v) for k, v in inp.items()}
    x = inp["x"].astype(np.float32)
    B, S, D = x.shape
    NTC = 4096
    hT = [np.ascontiguousarray(x[c // 4, (c % 4) * NTC:(c % 4 + 1) * NTC, :].T) for c in range(8)]

    def gather_u(res, n):
        uT = [np.empty((n, S), np.float32) for _ in range(B)]
        for c in range(8):
            uT[c // 4][:, (c % 4) * NTC:(c % 4 + 1) * NTC] = res[c]["uT"]
        return uT

    nc = _prog(("tok", False, False, 2848, False), lambda: build_tok(False, False, 2848, False))
    res = _run(nc, [{"hT": hT[c], "g_in": _g2(inp["norm_mix"][0]), "w_in": np.ascontiguousarray(inp["ev_w_in"][0])} for c in range(8)])
    uT = gather_u(res, 2848)
    out = None
    for layer in range(4):
        i = layer // 2
        odd = layer % 2 == 1
        ycT = [np.empty((1024, S), np.float32) for _ in range(B)]
        if not odd:
            nc = _prog(("even",), lambda: build_even(16384, True, True))
            maps = []
            for c in range(8):
                b, g, half = c // 4, (c % 4) // 2, c % 2
                m = prep_ssd(uT[b], g, half, i, inp)
                m.update(prep_nsa(uT[b], g, half, i, inp))
                maps.append(m)
            res = _run(nc, maps)
            for c in range(8):
                b, g, half = c // 4, (c % 4) // 2, c % 2
                hh = 4 * g + 2 * half
                ycT[b][64 * hh:64 * hh + 128, :] = res[c]["ygT"]
                ycT[b][512 + 64 * hh:512 + 64 * hh + 128, :] = res[c]["onT"]
        else:
            nc = _prog(("odd",), lambda: build_odd(32, 32))
            maps = [prep_odd(uT[c // 4], c % 4, i, inp) for c in range(8)]
            res = _run(nc, maps)
            for c in range(8):
                b, part = c // 4, c % 4
                ycT[b][0:512, part * NTC:(part + 1) * NTC] = res[c]["ocT"]
                ycT[b][512 + 128 * part:512 + 128 * part + 128, :] = res[c]["ydT"]
        del uT
        last = layer == 3
        n_in = 0 if last else (1280 if not odd else 2848)
        nc = _prog(("tok", True, odd, n_in, last), lambda: build_tok(True, odd, n_in, last))
        maps = []
        for c in range(8):
            b, part = c // 4, c % 4
            m = {"hT": hT[c], "ycT": np.ascontiguousarray(ycT[b][:, part * NTC:(part + 1) * NTC]),
                 "w_out": np.ascontiguousarray((inp["od_w_out"] if odd else inp["ev_w_out"])[i]),
                 "g_mlp": _g2(inp["norm_mlp"][layer]),
                 "w_up": np.ascontiguousarray(inp["mlp_w_up"][layer]), "w_down": np.ascontiguousarray(inp["mlp_w_down"][layer])}
            if odd:
                m["glu_w"] = np.ascontiguousarray(inp["s5_glu_w"][i])
                m["glu_b"] = np.ascontiguousarray(inp["s5_glu_b"][i].reshape(4, 128).T)
            else:
                m["ssdn"] = np.ascontiguousarray(inp["ssd_norm"][i].reshape(4, 128).T)
            if n_in:
                m["g_in"] = _g2(inp["norm_mix"][layer + 1])
                m["w_in"] = np.ascontiguousarray((inp["od_w_in"][i] if not odd else inp["ev_w_in"][i + 1]))
            if last:
                m["g_fin"] = _g2(inp["norm_final"])
            maps.append(m)
        res = _run(nc, maps)
        del ycT
        if last:
            out = np.empty((B, S, D), np.float32)
            for c in range(8):
                out[c // 4, (c % 4) * NTC:(c % 4 + 1) * NTC, :] = res[c]["hTo"].T
        else:
            hT = [res[c]["hTo"] for c in range(8)]
            uT = gather_u(res, n_in)
    return out
```

```python
import numpy as np
from contextlib import ExitStack
import concourse.bass as bass
import concourse.mybir as mybir
from concourse.bass_utils import run_bass_kernel_spmd

F32 = mybir.dt.float32
BF16 = mybir.dt.bfloat16
AF = mybir.ActivationFunctionType
ALU = mybir.AluOpType
AX = mybir.AxisListType

NS = 8
SAME_ENGINE_SYNC = True


class TT:
    def __init__(self, t, name):
        self.t = t
        self.name = name
        self.last_w = None
        self.readers = []

    def __getitem__(self, idx):
        return V(self, self.t[idx])

    def v(self, ap):
        return V(self, ap)


class V:
    def __init__(self, tt, ap):
        self.tt = tt
        self.ap = ap


class Prog:
    DMAC = ('dsp', 'dpool', 'dact')
    ENG = {'pe': 'tensor', 'dve': 'vector', 'act': 'scalar', 'pool': 'gpsimd', 'sp': 'sync'}

    def __init__(self):
        self.nc = bass.Bass("TRN2", target_bir_lowering=False)
        self.es = ExitStack()
        self.q = {e: [] for e in self.ENG}
        self.ctr = ['pe', 'dve', 'act', 'pool', 'dsp', 'dpool', 'dact']
        self.cnt = {c: 0 for c in self.ctr}
        self.sems = {c: [self.es.enter_context(self.nc.semaphore(f"s_{c}{i}")) for i in range(NS)]
                     for c in self.ctr}
        self.known = {e: {} for e in self.ENG}
        self.nbuf = 0
        self.dram = {}
        self.ninstr = 0
        self.stack = [self.es]

    def sb(self, shape, dtype=F32, name=None):
        self.nbuf += 1
        name = name or f"sb{self.nbuf}"
        t = self.stack[-1].enter_context(self.nc.sbuf_tensor(name, list(shape), dtype))
        return TT(t, name)

    def ps(self, shape, dtype=F32, name=None):
        self.nbuf += 1
        name = name or f"ps{self.nbuf}"
        t = self.stack[-1].enter_context(self.nc.psum_tensor(name, list(shape), dtype))
        return TT(t, name)

    def din(self, name, shape, dtype=F32):
        t = self.nc.dram_tensor(name, list(shape), dtype, kind="ExternalInput")
        tt = TT(t.ap(), name)
        self.dram[name] = tt
        return tt

    def dout(self, name, shape, dtype=F32):
        t = self.nc.dram_tensor(name, list(shape), dtype, kind="ExternalOutput")
        tt = TT(t.ap(), name)
        self.dram[name] = tt
        return tt

    def dint(self, name, shape, dtype=F32):
        t = self.nc.dram_tensor(name, list(shape), dtype, kind="Internal")
        tt = TT(t.ap(), name)
        self.dram[name] = tt
        return tt

    def _semval(self, c, k):
        mult = 16 if c.startswith('d') and c != 'dve' else 1
        return self.sems[c][(k - 1) % NS], mult * ((k - 1) // NS + 1)

    def op(self, e, fn, outs=(), ins=(), ctr=None):
        c = ctr or e
        deps = set()
        for v in ins:
            tt = v.tt if isinstance(v, V) else v
            if tt.last_w is not None:
                deps.add(tt.last_w)
        for v in outs:
            tt = v.tt if isinstance(v, V) else v
            if tt.last_w is not None:
                deps.add(tt.last_w)
            for r in tt.readers:
                deps.add(r)
        waits = []
        best = {}
        for (f, k) in deps:
            if f == c and (c == 'pe' or not SAME_ENGINE_SYNC):
                continue
            key = (f, (k - 1) % NS) if f in self.DMAC else f
            if k > best.get(key, (None, 0))[1]:
                best[key] = (f, k)
        for key, (f, k) in best.items():
            if self.known[e].get(key, 0) >= k:
                continue
            self.known[e][key] = k
            waits.append(self._semval(f, k))
        self.cnt[c] += 1
        k_me = self.cnt[c]
        is_dma = c in ('dsp', 'dpool', 'dact')
        if is_dma and k_me > NS:
            kk = k_me - NS
            key = (c, (kk - 1) % NS)
            if self.known[e].get(key, 0) < kk:
                self.known[e][key] = kk
                waits.append(self._semval(c, kk))
        sem = self.sems[c][(k_me - 1) % NS]
        inc = 16 if is_dma else 1

        eng = getattr(self.nc, self.ENG[e])
        for (s, val) in waits:
            eng.wait_ge(s, val)
        fn(eng).then_inc(sem, inc)
        self.ninstr += 1
        for v in outs:
            tt = v.tt if isinstance(v, V) else v
            tt.last_w = (c, k_me)
            tt.readers = []
        for v in ins:
            tt = v.tt if isinstance(v, V) else v
            tt.readers.append((c, k_me))
        return (c, k_me)

    def dma(self, out, in_, q='sp'):
        e, c = {'sp': ('sp', 'dsp'), 'pool': ('pool', 'dpool'), 'act': ('act', 'dact')}[q]
        return self.op(e, lambda eng: eng.dma_start(out=out.ap, in_=in_.ap), outs=[out], ins=[in_], ctr=c)

    def mm(self, o, out, lhsT, rhs, start, stop, ins):
        return self.op('pe', lambda e: e.matmul(out, lhsT=lhsT, rhs=rhs, start=start, stop=stop), outs=[o], ins=ins)

    def tr(self, o, out, in_, ident, ins):
        return self.op('pe', lambda e: e.transpose(out=out, in_=in_, identity=ident), outs=[o], ins=ins)

    def act(self, o, out, in_, func, ins, bias=None, scale=None):
        kw = {}
        if bias is not None:
            kw['bias'] = bias
        if scale is not None:
            kw['scale'] = scale
        return self.op('act', lambda e: e.activation(out=out, in_=in_, func=func, **kw), outs=[o], ins=ins)

    def tt(self, eng, o, out, a, b, op, ins):
        return self.op(eng, lambda e: e.tensor_tensor(out=out, in0=a, in1=b, op=op), outs=[o], ins=ins)

    def ts(self, eng, o, out, in_, s1, op0, ins, s2=None, op1=None):
        if op1 is None:
            return self.op(eng, lambda e: e.tensor_scalar(out=out, in0=in_, scalar1=s1, scalar2=None, op0=op0), outs=[o], ins=ins)
        return self.op(eng, lambda e: e.tensor_scalar(out=out, in0=in_, scalar1=s1, scalar2=s2, op0=op0, op1=op1), outs=[o], ins=ins)

    def stt(self, o, out, in0, scalar, in1, op0, op1, ins, eng='dve'):
        return self.op(eng, lambda e: e.scalar_tensor_tensor(out=out, in0=in0, scalar=scalar, in1=in1, op0=op0, op1=op1), outs=[o], ins=ins)

    def cp(self, eng, o, out, in_, ins):
        if eng == 'act':
            return self.op('act', lambda e: e.copy(out=out, in_=in_), outs=[o], ins=ins)
        return self.op(eng, lambda e: e.tensor_copy(out=out, in_=in_), outs=[o], ins=ins)

    def memset(self, o, ap, val, eng='pool'):
        return self.op(eng, lambda e: e.memset(ap, val), outs=[o])

    def barrier(self, engines=None):
        for e in (engines or self.ENG):
            eng = getattr(self.nc, self.ENG[e])
            for c in self.ctr:
                k = self.cnt[c]
                if k == 0:
                    continue
                if c in self.DMAC:
                    for kk in range(max(1, k - NS + 1), k + 1):
                        key = (c, (kk - 1) % NS)
                        if self.known[e].get(key, 0) < kk:
                            self.known[e][key] = kk
                            sm, val = self._semval(c, kk)
                            eng.wait_ge(sm, val)
                else:
                    if c == e and c == 'pe':
                        continue
                    if self.known[e].get(c, 0) < k:
                        self.known[e][c] = k
                        sm, val = self._semval(c, k)
                        eng.wait_ge(sm, val)

    from contextlib import contextmanager

    @contextmanager
    def scope(self):
        st = ExitStack()
        self.stack.append(st)
        try:
            yield
        finally:
            self.barrier()
            self.stack.pop()
            st.close()

    def finish(self):
        self.barrier(['sp'])
        self.es.close()
        return self.nc


I32 = mybir.dt.int32
NEGB = -30000.0
SSD_IN = 1544


EPS = 1e-6
TT_TOK = 512


def build_tok(post, odd, n_in, final, ntt=8):
    P = Prog()
    NT = ntt * TT_TOK
    hT = P.din("hT", [1024, NT])
    if post:
        ycT = P.din("ycT", [1024, NT])
        w_out = P.din("w_out", [1024, 1024])
        g_mlp = P.din("g_mlp", [128, 8])
        w_up = P.din("w_up", [1024, 4096])
        w_down = P.din("w_down", [4096, 1024])
        if odd:
            glu_w = P.din("glu_w", [512, 512])
            glu_b = P.din("glu_b", [128, 4])
        else:
            ssdn_d = P.din("ssdn", [128, 4])
    if n_in:
        g_in = P.din("g_in", [128, 8])
        w_in = P.din("w_in", [1024, n_in])
        uT = P.dout("uT", [n_in, NT])
    if final:
        g_fin = P.din("g_fin", [128, 8])
    if post or final:
        hTo = P.dout("hTo", [1024, NT])

    def cast_rows(dst, src, K, C):
        nd = (C + 1023) // 1024
        kg = 4 if nd == 1 else 1
        sv = src.t.rearrange("(k p) c -> p k c", p=128)
        for k0 in range(0, K, kg):
            k1 = min(K, k0 + kg)
            P.op('pool', lambda e, k0=k0, k1=k1: e.dma_start(out=dst.t[:, k0:k1, :], in_=sv[:, k0:k1, :], max_dma_last_dim=4096),
                 outs=[dst], ins=[src], ctr='dpool')
    if post:
        s_up = P.dint("s_up", [128, 8, 4096], BF16)
        s_down = P.dint("s_down", [128, 32, 1024], BF16)
        s_out = P.dint("s_out", [128, 8, 1024], BF16)
        cast_rows(s_out, w_out, 8, 1024)
        if odd:
            s_glu = P.dint("s_glu", [128, 4, 512], BF16)
            cast_rows(s_glu, glu_w, 4, 512)
        cast_rows(s_up, w_up, 8, 4096)
        cast_rows(s_down, w_down, 32, 1024)
    n_oc = (n_in + 127) // 128
    if n_in:
        s_in = P.dint("s_in", [128, 8, n_in], BF16)
        cast_rows(s_in, w_in, 8, n_in)

    ones = P.sb([128, 128], F32, "ones")
    P.op('pool', lambda e: e.memset(ones.t[:], 1.0), outs=[ones])
    h = [P.sb([128, TT_TOK], F32, f"h{k}") for k in range(8)]
    hn = [P.sb([128, TT_TOK], BF16, f"hn{k}") for k in range(8)]
    sq = [P.sb([128, TT_TOK], F32, f"sq{i}") for i in range(2)]
    rstd = P.sb([128, TT_TOK], F32, "rstd")
    pss = [P.ps([128, TT_TOK], F32, f"pp{i}") for i in range(4)]
    ps_ss = P.ps([128, TT_TOK], F32, "ps_ss")
    psi = [0]

    def next_ps():
        psi[0] = (psi[0] + 1) % 4
        return pss[psi[0]]

    if post:
        yc = [P.sb([128, TT_TOK], F32, f"yc{k}") for k in range(8)]
        ycb = [P.sb([128, TT_TOK], BF16, f"ycb{k}") for k in range(8)]
        wo = P.sb([128, 8, 1024], BF16, "wo")
        P.dma(wo[:], s_out[:])
        gm = P.sb([128, 8], F32, "gm")
        P.dma(gm[:], g_mlp[:])
        a = [P.sb([128, TT_TOK], BF16, f"a{f}") for f in range(32)]
        rl = [P.sb([128, TT_TOK], F32, f"rl{i}") for i in range(2)]
        wup = [P.sb([128, 8, 512], BF16, f"wup{i}") for i in range(2)]
        wdn = [P.sb([128, 32, 128], BF16, f"wdn{i}") for i in range(2)]
        if odd:
            wg = P.sb([128, 4, 512], BF16, "wg")
            P.dma(wg[:], s_glu[:])
            gb = P.sb([128, 4], F32, "gb")
            P.dma(gb[:], glu_b[:])
            gate = [P.sb([128, TT_TOK], F32, f"gate{i}") for i in range(2)]
        else:
            ssdn = P.sb([128, 4], F32, "ssdn_s")
            P.dma(ssdn[:], ssdn_d[:])
    if n_in:
        gi = P.sb([128, 8], F32, "gi")
        P.dma(gi[:], g_in[:])
        win = [P.sb([128, 8, 128], BF16, f"win{i}") for i in range(2)]
        uo = [P.sb([128, TT_TOK], F32, f"uo{i}") for i in range(2)]
    if final:
        gf = P.sb([128, 8], F32, "gf")
        P.dma(gf[:], g_fin[:])
        fo = [P.sb([128, TT_TOK], F32, f"fo{i}") for i in range(2)]

    def rmsnorm(g, outs, out_dtype_f32=False):
        for k in range(8):
            s = sq[k % 2]
            P.op('act', lambda e, s=s, k=k: e.activation(out=s.t[:], in_=h[k].t[:], func=AF.Square), outs=[s], ins=[h[k]])
            P.op('pe', lambda e, s=s, k=k: e.matmul(ps_ss.t[:], lhsT=ones.t[:], rhs=s.t[:], start=(k == 0), stop=(k == 7)),
                 outs=[ps_ss], ins=[ones, s])
        P.op('act', lambda e: e.activation(out=rstd.t[:], in_=ps_ss.t[:], func=AF.Sqrt, bias=EPS, scale=1.0 / 1024), outs=[rstd], ins=[ps_ss])
        P.op('dve', lambda e: e.reciprocal(out=rstd.t[:], in_=rstd.t[:]), outs=[rstd], ins=[rstd])
        for k in range(8):
            P.op('dve', lambda e, k=k: e.scalar_tensor_tensor(out=outs[k].t[:], in0=h[k].t[:], scalar=g.t[:, k:k + 1], in1=rstd.t[:],
                                                             op0=ALU.mult, op1=ALU.mult), outs=[outs[k]], ins=[h[k], g, rstd])

    for tt in range(ntt):
        tok = slice(tt * TT_TOK, (tt + 1) * TT_TOK)
        for k in range(8):
            P.dma(h[k][:], V(hT, hT.t[k * 128:(k + 1) * 128, tok]))
        if post:
            for k in range(8):
                P.dma(yc[k][:], V(ycT, ycT.t[k * 128:(k + 1) * 128, tok]))
            if not odd:
                for g in range(2):
                    for kk in range(2):
                        k = 2 * g + kk
                        sq_ = sq[kk]
                        P.op('act', lambda e, sq_=sq_, k=k: e.activation(out=sq_.t[:], in_=yc[k].t[:], func=AF.Square), outs=[sq_], ins=[yc[k]])
                        P.op('pe', lambda e, sq_=sq_, kk=kk: e.matmul(ps_ss.t[:], lhsT=ones.t[:], rhs=sq_.t[:], start=(kk == 0), stop=(kk == 1)),
                             outs=[ps_ss], ins=[ones, sq_])
                    P.op('act', lambda e: e.activation(out=rstd.t[:], in_=ps_ss.t[:], func=AF.Sqrt, bias=EPS, scale=1.0 / 256), outs=[rstd], ins=[ps_ss])
                    P.op('dve', lambda e: e.reciprocal(out=rstd.t[:], in_=rstd.t[:]), outs=[rstd], ins=[rstd])
                    for kk in range(2):
                        k = 2 * g + kk
                        P.op('dve', lambda e, k=k: e.scalar_tensor_tensor(out=ycb[k].t[:], in0=yc[k].t[:], scalar=ssdn.t[:, k:k + 1], in1=rstd.t[:],
                                                                         op0=ALU.mult, op1=ALU.mult), outs=[ycb[k]], ins=[yc[k], ssdn, rstd])
            for k in (range(4) if odd else range(4, 8)):
                eng = 'pool' if k % 2 else 'dve'
                P.op(eng, lambda e, k=k: e.tensor_copy(out=ycb[k].t[:], in_=yc[k].t[:]), outs=[ycb[k]], ins=[yc[k]])
            if odd:
                for k in range(4, 8):
                    P.op('pool', lambda e, k=k: e.tensor_copy(out=hn[k].t[:], in_=yc[k].t[:]), outs=[hn[k]], ins=[yc[k]])
                for j in range(4):
                    pp = next_ps()
                    for k in range(4):
                        P.op('pe', lambda e, pp=pp, j=j, k=k: e.matmul(pp.t[:], lhsT=wg.t[:, k, j * 128:(j + 1) * 128], rhs=hn[4 + k].t[:],
                                                                      start=(k == 0), stop=(k == 3)), outs=[pp], ins=[wg, hn[4 + k]])
                    gt = gate[j % 2]
                    P.op('act', lambda e, pp=pp, gt=gt, j=j: e.activation(out=gt.t[:], in_=pp.t[:], func=AF.Sigmoid, bias=gb.t[:, j:j + 1], scale=1.0),
                         outs=[gt], ins=[pp, gb])
                    P.op('dve', lambda e, gt=gt, j=j: e.tensor_tensor(out=ycb[4 + j].t[:], in0=yc[4 + j].t[:], in1=gt.t[:], op=ALU.mult),
                         outs=[ycb[4 + j]], ins=[yc[4 + j], gt])
            for j in range(8):
                pp = next_ps()
                for k in range(8):
                    P.op('pe', lambda e, pp=pp, j=j, k=k: e.matmul(pp.t[:], lhsT=wo.t[:, k, j * 128:(j + 1) * 128], rhs=ycb[k].t[:],
                                                                  start=(k == 0), stop=(k == 7)), outs=[pp], ins=[wo, ycb[k]])
                P.op('dve', lambda e, pp=pp, j=j: e.tensor_tensor(out=h[j].t[:], in0=h[j].t[:], in1=pp.t[:], op=ALU.add), outs=[h[j]], ins=[h[j], pp])
            rmsnorm(gm, hn)
            for fg in range(8):
                wb = wup[fg % 2]
                P.dma(wb[:], V(s_up, s_up.t[:, :, fg * 512:(fg + 1) * 512]))
                for fi in range(4):
                    f = fg * 4 + fi
                    pp = next_ps()
                    for k in range(8):
                        P.op('pe', lambda e, pp=pp, wb=wb, fi=fi, k=k: e.matmul(pp.t[:], lhsT=wb.t[:, k, fi * 128:(fi + 1) * 128], rhs=hn[k].t[:],
                                                                               start=(k == 0), stop=(k == 7)), outs=[pp], ins=[wb, hn[k]])
                    r = rl[f % 2]
                    P.op('act', lambda e, pp=pp, r=r: e.activation(out=r.t[:], in_=pp.t[:], func=AF.Relu), outs=[r], ins=[pp])
                    P.op('pool', lambda e, r=r, f=f: e.tensor_tensor(out=a[f].t[:], in0=r.t[:], in1=r.t[:], op=ALU.mult), outs=[a[f]], ins=[r])
            for j in range(8):
                wb = wdn[j % 2]
                P.dma(wb[:], V(s_down, s_down.t[:, :, j * 128:(j + 1) * 128]))
                pp = next_ps()
                for f in range(32):
                    P.op('pe', lambda e, pp=pp, wb=wb, f=f: e.matmul(pp.t[:], lhsT=wb.t[:, f, :], rhs=a[f].t[:], start=(f == 0), stop=(f == 31)),
                         outs=[pp], ins=[wb, a[f]])
                P.op('dve', lambda e, pp=pp, j=j: e.tensor_tensor(out=h[j].t[:], in0=h[j].t[:], in1=pp.t[:], op=ALU.add), outs=[h[j]], ins=[h[j], pp])
        if post and not final:
            for k in range(8):
                P.dma(V(hTo, hTo.t[k * 128:(k + 1) * 128, tok]), h[k][:])
        if n_in:
            rmsnorm(gi, hn)
            for o in range(n_oc):
                cw = min(128, n_in - o * 128)
                wb = win[o % 2]
                P.dma(V(wb, wb.t[:, :, 0:cw]), V(s_in, s_in.t[:, :, o * 128:o * 128 + cw]))
                pp = next_ps()
                for k in range(8):
                    P.op('pe', lambda e, pp=pp, wb=wb, k=k, cw=cw: e.matmul(pp.t[0:cw, :], lhsT=wb.t[:, k, 0:cw], rhs=hn[k].t[:],
                                                                           start=(k == 0), stop=(k == 7)), outs=[pp], ins=[wb, hn[k]])
                u = uo[o % 2]
                eng = 'act' if o % 2 else 'dve'
                if eng == 'act':
                    P.op('act', lambda e, pp=pp, u=u, cw=cw: e.copy(out=u.t[0:cw, :], in_=pp.t[0:cw, :]), outs=[u], ins=[pp])
                else:
                    P.op('dve', lambda e, pp=pp, u=u, cw=cw: e.tensor_copy(out=u.t[0:cw, :], in_=pp.t[0:cw, :]), outs=[u], ins=[pp])
                P.dma(V(uT, uT.t[o * 128:o * 128 + cw, tok]), V(u, u.t[0:cw, :]))
        if final:
            for k in range(8):
                s = sq[k % 2]
                P.op('act', lambda e, s=s, k=k: e.activation(out=s.t[:], in_=h[k].t[:], func=AF.Square), outs=[s], ins=[h[k]])
                P.op('pe', lambda e, s=s, k=k: e.matmul(ps_ss.t[:], lhsT=ones.t[:], rhs=s.t[:], start=(k == 0), stop=(k == 7)),
                     outs=[ps_ss], ins=[ones, s])
            P.op('act', lambda e: e.activation(out=rstd.t[:], in_=ps_ss.t[:], func=AF.Sqrt, bias=EPS, scale=1.0 / 1024), outs=[rstd], ins=[ps_ss])
            P.op('dve', lambda e: e.reciprocal(out=rstd.t[:], in_=rstd.t[:]), outs=[rstd], ins=[rstd])
            for k in range(8):
                o_ = fo[k % 2]
                P.op('dve', lambda e, k=k, o_=o_: e.scalar_tensor_tensor(out=o_.t[:], in0=h[k].t[:], scalar=gf.t[:, k:k + 1], in1=rstd.t[:],
                                                                        op0=ALU.mult, op1=ALU.mult), outs=[o_], ins=[h[k], gf, rstd])
                P.dma(V(hTo, hTo.t[k * 128:(k + 1) * 128, tok]), o_[:])
    return P.finish()


TWO_PI = 2 * np.pi
T5 = 512


def range_reduce(P, x, n, tmp_i, tmp_f, tmp_c):
    xs, ni, nf, c1 = x.t[:, :n], tmp_i.t[:, :n], tmp_f.t[:, :n], tmp_c.t[:, :n]
    P.op('dve', lambda e: e.tensor_scalar(out=ni, in0=xs, scalar1=1.0 / TWO_PI, scalar2=None, op0=ALU.mult), outs=[tmp_i], ins=[x])
    P.op('dve', lambda e: e.tensor_copy(out=nf, in_=ni), outs=[tmp_f], ins=[tmp_i])
    P.op('dve', lambda e: e.scalar_tensor_tensor(out=xs, in0=nf, scalar=-6.28125, in1=xs, op0=ALU.mult, op1=ALU.add), outs=[x], ins=[tmp_f, x])
    P.op('dve', lambda e: e.scalar_tensor_tensor(out=xs, in0=nf, scalar=-(TWO_PI - 6.28125), in1=xs, op0=ALU.mult, op1=ALU.add), outs=[x], ins=[tmp_f, x])
    P.op('dve', lambda e: e.tensor_single_scalar(out=c1, in_=xs, scalar=np.pi, op=ALU.is_gt), outs=[tmp_c], ins=[x])
    P.op('dve', lambda e: e.scalar_tensor_tensor(out=xs, in0=c1, scalar=-TWO_PI, in1=xs, op0=ALU.mult, op1=ALU.add), outs=[x], ins=[tmp_c, x])
    P.op('dve', lambda e: e.tensor_single_scalar(out=c1, in_=xs, scalar=-np.pi, op=ALU.is_lt), outs=[tmp_c], ins=[x])
    P.op('dve', lambda e: e.scalar_tensor_tensor(out=xs, in0=c1, scalar=TWO_PI, in1=xs, op0=ALU.mult, op1=ALU.add), outs=[x], ins=[tmp_c, x])
    P.op('dve', lambda e: e.tensor_scalar(out=xs, in0=xs, scalar1=np.pi, scalar2=-np.pi, op0=ALU.min, op1=ALU.max), outs=[x], ins=[x])


def build_odd(nq=32, nch=32):
    P = Prog()
    NQT = nq * 128
    L = nch * T5
    qT = P.din("qT", [512, NQT]); kT = P.din("kT", [128, NQT + 128]); vT = P.din("vT", [128, NQT + 128])
    mprev0 = P.din("mprev0", [128, 512]); sinks = P.din("sinks", [8])
    usT = P.din("usT", [128, L]); s5p = P.din("s5p", [128, 4, 3])
    bT = P.din("bT", [4, 128, 2, 128]); cT = P.din("cT", [4, 128, 2, 128]); s5d = P.din("s5d", [128, 1])
    ocT = P.dout("ocT", [512, NQT]); ydT = P.dout("ydT", [128, L])

    ps = [P.ps([128, 512], F32, f"ps{i}") for i in range(8)]
    io = P.sb([128, 4, 128], I32, "io")
    P.op('pool', lambda e: e.iota(io.t[:], pattern=[[0, 4], [1, 128]], base=0, channel_multiplier=-1), outs=[io])
    mcur = P.sb([128, 512], BF16, "mcur"); mprev = P.sb([128, 512], BF16, "mprev"); mp0 = P.sb([128, 512], BF16, "mp0")
    iov = io.t[:].rearrange("p h q -> p (h q)")
    P.op('dve', lambda e: e.tensor_single_scalar(out=mcur.t[:], in_=iov, scalar=0.0, op=ALU.is_ge), outs=[mcur], ins=[io])
    P.op('dve', lambda e: e.tensor_single_scalar(out=mprev.t[:], in_=iov, scalar=0.0, op=ALU.is_lt), outs=[mprev], ins=[io])
    mp0f = P.sb([128, 512], F32, "mp0f")
    P.dma(mp0f[:], mprev0[:])
    P.op('dve', lambda e: e.tensor_copy(out=mp0.t[:], in_=mp0f.t[:]), outs=[mp0], ins=[mp0f])
    ident = P.sb([128, 128], F32, "ident")
    P.op('dve', lambda e: e.tensor_single_scalar(out=ident.t[:], in_=io.t[:, 0, :], scalar=0.0, op=ALU.is_equal), outs=[ident], ins=[io])
    esink = P.sb([128, 8], F32, "esink")
    P.dma(esink[:], V(sinks, sinks.t.partition_broadcast(128)))
    P.op('act', lambda e: e.activation(out=esink.t[:], in_=esink.t[:], func=AF.Exp), outs=[esink], ins=[esink])
    zl = P.sb([1, 128], BF16, "zl"); zr_ = P.sb([1, 512], BF16, "zr_")
    P.op('pool', lambda e: e.memset(zl.t[:], 0.0), outs=[zl])
    P.op('pool', lambda e: e.memset(zr_.t[:], 0.0), outs=[zr_])

    prm = P.sb([128, 4, 3], F32, "prm")
    P.dma(prm[:], s5p[:])
    dsk = P.sb([128, 1], F32, "dsk")
    P.dma(dsk[:], s5d[:])
    bts = P.sb([128, 4, 2, 128], F32, "bts"); cts = P.sb([128, 4, 2, 128], F32, "cts")
    for i in range(4):
        P.dma(V(bts, bts.t[:, i]), bT[i])
        P.dma(V(cts, cts.t[:, i]), cT[i])
    P.op('dve', lambda e: e.tensor_scalar(out=cts.t[:, :, 1, :], in0=cts.t[:, :, 1, :], scalar1=-1.0, scalar2=None, op0=ALU.mult), outs=[cts], ins=[cts])
    sm = {n: P.sb([128, 4], F32, "sm_" + n) for n in ["step", "ars", "th", "rho", "c", "s", "xr", "xi", "den", "fr", "fi", "t1", "t2", "thc"]}
    tmp_i = P.sb([128, T5], I32, "tmp_i"); tmp_f = P.sb([128, T5], F32, "tmp_f"); tmp_c = P.sb([128, T5], F32, "tmp_c")
    are, aim, ldt = prm.t[:, :, 0], prm.t[:, :, 1], prm.t[:, :, 2]

    def tt_(eng, out, a, b, op, o_tt, ins):
        P.op(eng, lambda e: e.tensor_tensor(out=out, in0=a, in1=b, op=op), outs=[o_tt], ins=ins)
    P.op('act', lambda e: e.activation(out=sm["step"].t[:], in_=ldt, func=AF.Exp), outs=[sm["step"]], ins=[prm])
    tt_('dve', sm["ars"].t[:], are, sm["step"].t[:], ALU.mult, sm["ars"], [prm, sm["step"]])
    tt_('dve', sm["th"].t[:], aim, sm["step"].t[:], ALU.mult, sm["th"], [prm, sm["step"]])
    P.op('act', lambda e: e.activation(out=sm["rho"].t[:], in_=sm["ars"].t[:], func=AF.Exp), outs=[sm["rho"]], ins=[sm["ars"]])
    P.op('dve', lambda e: e.tensor_copy(out=sm["s"].t[:], in_=sm["th"].t[:]), outs=[sm["s"]], ins=[sm["th"]])
    range_reduce(P, sm["s"], 4, tmp_i, tmp_f, tmp_c)
    P.op('act', lambda e: e.activation(out=sm["s"].t[:], in_=sm["s"].t[:], func=AF.Sin), outs=[sm["s"]], ins=[sm["s"]])
    P.op('dve', lambda e: e.tensor_scalar(out=sm["c"].t[:], in0=sm["th"].t[:], scalar1=np.pi / 2, scalar2=None, op0=ALU.add), outs=[sm["c"]], ins=[sm["th"]])
    range_reduce(P, sm["c"], 4, tmp_i, tmp_f, tmp_c)
    P.op('act', lambda e: e.activation(out=sm["c"].t[:], in_=sm["c"].t[:], func=AF.Sin), outs=[sm["c"]], ins=[sm["c"]])
    tt_('dve', sm["xr"].t[:], sm["rho"].t[:], sm["c"].t[:], ALU.mult, sm["xr"], [sm["rho"], sm["c"]])
    P.op('dve', lambda e: e.tensor_scalar(out=sm["xr"].t[:], in0=sm["xr"].t[:], scalar1=-1.0, scalar2=None, op0=ALU.add), outs=[sm["xr"]], ins=[sm["xr"]])
    tt_('dve', sm["xi"].t[:], sm["rho"].t[:], sm["s"].t[:], ALU.mult, sm["xi"], [sm["rho"], sm["s"]])
    tt_('dve', sm["t1"].t[:], are, are, ALU.mult, sm["t1"], [prm])
    tt_('dve', sm["t2"].t[:], aim, aim, ALU.mult, sm["t2"], [prm])
    tt_('dve', sm["den"].t[:], sm["t1"].t[:], sm["t2"].t[:], ALU.add, sm["den"], [sm["t1"], sm["t2"]])
    P.op('dve', lambda e: e.reciprocal(out=sm["den"].t[:], in_=sm["den"].t[:]), outs=[sm["den"]], ins=[sm["den"]])
    tt_('dve', sm["t1"].t[:], sm["xr"].t[:], are, ALU.mult, sm["t1"], [sm["xr"], prm])
    tt_('dve', sm["t2"].t[:], sm["xi"].t[:], aim, ALU.mult, sm["t2"], [sm["xi"], prm])
    tt_('dve', sm["fr"].t[:], sm["t1"].t[:], sm["t2"].t[:], ALU.add, sm["fr"], [sm["t1"], sm["t2"]])
    tt_('dve', sm["fr"].t[:], sm["fr"].t[:], sm["den"].t[:], ALU.mult, sm["fr"], [sm["fr"], sm["den"]])
    tt_('dve', sm["t1"].t[:], sm["xi"].t[:], are, ALU.mult, sm["t1"], [sm["xi"], prm])
    tt_('dve', sm["t2"].t[:], sm["xr"].t[:], aim, ALU.mult, sm["t2"], [sm["xr"], prm])
    tt_('dve', sm["fi"].t[:], sm["t1"].t[:], sm["t2"].t[:], ALU.subtract, sm["fi"], [sm["t1"], sm["t2"]])
    tt_('dve', sm["fi"].t[:], sm["fi"].t[:], sm["den"].t[:], ALU.mult, sm["fi"], [sm["fi"], sm["den"]])
    jt_i = P.sb([128, T5], I32, "jt_i"); jt = P.sb([128, T5], F32, "jt")
    P.op('pool', lambda e: e.iota(jt_i.t[:], pattern=[[1, T5]], base=1, channel_multiplier=0), outs=[jt_i])
    P.op('dve', lambda e: e.tensor_copy(out=jt.t[:], in_=jt_i.t[:]), outs=[jt], ins=[jt_i])
    Cn = [P.sb([128, T5], F32, f"Cn{i}") for i in range(4)]; Sn = [P.sb([128, T5], F32, f"Sn{i}") for i in range(4)]
    Fr = [P.sb([128, T5], F32, f"Fr{i}") for i in range(4)]; Fi = [P.sb([128, T5], F32, f"Fi{i}") for i in range(4)]
    for i in range(4):
        th_i = sm["th"].t[:, i:i + 1]
        P.op('dve', lambda e, i=i, th_i=th_i: e.tensor_scalar(out=Sn[i].t[:], in0=jt.t[:], scalar1=th_i, scalar2=None, op0=ALU.mult), outs=[Sn[i]], ins=[jt, sm["th"]])
        P.op('dve', lambda e, i=i: e.tensor_scalar(out=Cn[i].t[:], in0=Sn[i].t[:], scalar1=np.pi / 2, scalar2=None, op0=ALU.add), outs=[Cn[i]], ins=[Sn[i]])
        range_reduce(P, Sn[i], T5, tmp_i, tmp_f, tmp_c)
        range_reduce(P, Cn[i], T5, tmp_i, tmp_f, tmp_c)
        P.op('act', lambda e, i=i: e.activation(out=Sn[i].t[:], in_=Sn[i].t[:], func=AF.Sin), outs=[Sn[i]], ins=[Sn[i]])
        P.op('act', lambda e, i=i: e.activation(out=Cn[i].t[:], in_=Cn[i].t[:], func=AF.Sin), outs=[Cn[i]], ins=[Cn[i]])
        fr_i, fi_i = sm["fr"].t[:, i:i + 1], sm["fi"].t[:, i:i + 1]
        P.op('dve', lambda e, i=i, fi_i=fi_i: e.tensor_scalar(out=tmp_f.t[:], in0=Sn[i].t[:], scalar1=fi_i, scalar2=None, op0=ALU.mult), outs=[tmp_f], ins=[Sn[i], sm["fi"]])
        P.op('dve', lambda e, i=i, fr_i=fr_i: e.scalar_tensor_tensor(out=Fr[i].t[:], in0=Cn[i].t[:], scalar=fr_i, in1=tmp_f.t[:], op0=ALU.mult, op1=ALU.add),
             outs=[Fr[i]], ins=[Cn[i], sm["fr"], tmp_f])
        P.op('dve', lambda e, i=i, fr_i=fr_i: e.tensor_scalar(out=tmp_f.t[:], in0=Sn[i].t[:], scalar1=fr_i, scalar2=None, op0=ALU.mult), outs=[tmp_f], ins=[Sn[i], sm["fr"]])
        P.op('dve', lambda e, i=i, fi_i=fi_i: e.scalar_tensor_tensor(out=Fi[i].t[:], in0=Cn[i].t[:], scalar=fi_i, in1=tmp_f.t[:], op0=ALU.mult, op1=ALU.subtract),
             outs=[Fi[i]], ins=[Cn[i], sm["fi"], tmp_f])
    us = [P.sb([128, T5], F32, f"us{i}") for i in range(2)]
    m = [[P.sb([128, T5], F32, f"m{s}_{j}") for j in range(4)] for s in range(2)]
    zin = [[P.sb([128, T5], F32, f"zin{s}_{j}") for j in range(2)] for s in range(2)]
    zz = [[P.sb([128, T5], F32, f"zz{s}_{j}") for j in range(2)] for s in range(2)]
    nn = [[P.sb([128, T5], F32, f"nn{s}_{j}") for j in range(4)] for s in range(2)]
    xst = [[P.sb([128, T5], F32, f"xst{i}_{j}") for j in range(2)] for i in range(4)]
    yv = [P.sb([128, T5], F32, f"yv{i}") for i in range(2)]
    for c in range(nch):
        u = us[c % 2]
        P.dma(u[:], V(usT, usT.t[:, c * T5:(c + 1) * T5]))
        yps = ps[4 + c % 2]
        for pr in range(2):
            tiles = (2 * pr, 2 * pr + 1)
            for i in tiles:
                s = i % 2
                brp, bip = ps[2 * s], ps[2 * s + 1]
                P.op('pe', lambda e, brp=brp, i=i, u=u: e.matmul(brp.t[:], lhsT=bts.t[:, i, 0, :], rhs=u.t[:], start=True, stop=True), outs=[brp], ins=[bts, u])
                P.op('pe', lambda e, bip=bip, i=i, u=u: e.matmul(bip.t[:], lhsT=bts.t[:, i, 1, :], rhs=u.t[:], start=True, stop=True), outs=[bip], ins=[bts, u])
            for i in tiles:
                s = i % 2
                brp, bip = ps[2 * s], ps[2 * s + 1]
                mm_ = m[s]
                tt_('dve', mm_[0].t[:], brp.t[:], Fr[i].t[:], ALU.mult, mm_[0], [brp, Fr[i]])
                tt_('dve', mm_[1].t[:], bip.t[:], Fi[i].t[:], ALU.mult, mm_[1], [bip, Fi[i]])
                tt_('dve', mm_[2].t[:], brp.t[:], Fi[i].t[:], ALU.mult, mm_[2], [brp, Fi[i]])
                tt_('dve', mm_[3].t[:], bip.t[:], Fr[i].t[:], ALU.mult, mm_[3], [bip, Fr[i]])
            for i in tiles:
                s = i % 2
                mm_ = m[s]
                tt_('pool', zin[s][0].t[:], mm_[0].t[:], mm_[1].t[:], ALU.subtract, zin[s][0], [mm_[0], mm_[1]])
                tt_('pool', zin[s][1].t[:], mm_[2].t[:], mm_[3].t[:], ALU.add, zin[s][1], [mm_[2], mm_[3]])
            for i in tiles:
                s = i % 2
                rho_b = sm["rho"].t[:, i:i + 1].to_broadcast([128, T5])
                for j in range(2):
                    init = 0.0 if c == 0 else xst[i][j].t[:, T5 - 1:T5]
                    ins_ = [sm["rho"], zin[s][j]] + ([] if c == 0 else [xst[i][j]])
                    P.op('dve', lambda e, s=s, j=j, rho_b=rho_b, init=init: e.tensor_tensor_scan(out=zz[s][j].t[:], data0=rho_b, data1=zin[s][j].t[:], initial=init,
                                                                                                  op0=ALU.mult, op1=ALU.add), outs=[zz[s][j]], ins=ins_)
            for i in tiles:
                s = i % 2
                n_ = nn[s]
                tt_('pool', n_[0].t[:], zz[s][0].t[:], Cn[i].t[:], ALU.mult, n_[0], [zz[s][0], Cn[i]])
                tt_('pool', n_[1].t[:], zz[s][1].t[:], Sn[i].t[:], ALU.mult, n_[1], [zz[s][1], Sn[i]])
                tt_('pool', n_[2].t[:], zz[s][0].t[:], Sn[i].t[:], ALU.mult, n_[2], [zz[s][0], Sn[i]])
                tt_('pool', n_[3].t[:], zz[s][1].t[:], Cn[i].t[:], ALU.mult, n_[3], [zz[s][1], Cn[i]])
            for i in tiles:
                s = i % 2
                n_ = nn[s]
                tt_('pool', xst[i][0].t[:], n_[0].t[:], n_[1].t[:], ALU.subtract, xst[i][0], [n_[0], n_[1]])
                tt_('pool', xst[i][1].t[:], n_[2].t[:], n_[3].t[:], ALU.add, xst[i][1], [n_[2], n_[3]])
            for i in tiles:
                P.op('pe', lambda e, yps=yps, i=i: e.matmul(yps.t[:], lhsT=cts.t[:, i, 0, :], rhs=xst[i][0].t[:], start=(i == 0), stop=False), outs=[yps], ins=[cts, xst[i][0]])
                P.op('pe', lambda e, yps=yps, i=i: e.matmul(yps.t[:], lhsT=cts.t[:, i, 1, :], rhs=xst[i][1].t[:], start=False, stop=(i == 3)), outs=[yps], ins=[cts, xst[i][1]])
        y_ = yv[c % 2]
        P.op('dve', lambda e, y_=y_, u=u, yps=yps: e.scalar_tensor_tensor(out=y_.t[:], in0=u.t[:], scalar=dsk.t[:, 0:1], in1=yps.t[:], op0=ALU.mult, op1=ALU.add),
             outs=[y_], ins=[u, dsk, yps])
        P.op('act', lambda e, y_=y_: e.activation(out=y_.t[:], in_=y_.t[:], func=AF.Gelu_apprx_tanh), outs=[y_], ins=[y_])
        P.dma(V(ydT, ydT.t[:, c * T5:(c + 1) * T5]), y_[:])

    NR = 3
    kf = [P.sb([64, 2, 128], F32, f"kf{i}") for i in range(NR)]
    kb = [P.sb([64, 2, 128], BF16, f"kb{i}") for i in range(NR)]
    vf = [P.sb([128, 128], F32, f"vf{i}") for i in range(NR)]
    vb = [P.sb([128, 2, 65], BF16, f"vb{i}") for i in range(NR)]
    for i in range(NR):
        P.op('pool', lambda e, i=i: e.memset(vb[i].t[:], 1.0), outs=[vb[i]])
    qf = [P.sb([64, 8, 128], F32, f"qf{i}") for i in range(2)]
    qb = [P.sb([64, 8, 128], BF16, f"qb{i}") for i in range(2)]
    ex = [P.sb([128, 512], BF16, f"ex{i}") for i in range(2)]
    pm = [P.sb([128, 512], BF16, f"pm{i}") for i in range(4)]
    lsum = P.sb([128, 4], F32, "lsum")
    osb = [P.sb([128, 512], F32, f"osb{i}") for i in range(2)]
    otr = [P.sb([128, 128], F32, f"otr{i}") for i in range(2)]
    pmi = [0]

    def load_kv(j):
        r = j % NR
        P.dma(kf[r][:], V(kT, kT.t[:, j * 128:(j + 1) * 128].rearrange("(g d) k -> d g k", g=2)))
        P.op('pool', lambda e: e.tensor_copy(out=kb[r].t[:], in_=kf[r].t[:]), outs=[kb[r]], ins=[kf[r]])
        P.dma(vf[r][:], V(vT, vT.t[:, j * 128:(j + 1) * 128]))
        tp = ps[6]
        P.op('pe', lambda e: e.transpose(out=tp.t[:, 0:128], in_=vf[r].t[:], identity=ident.t[:]), outs=[tp], ins=[vf[r], ident])
        P.op('dve', lambda e: e.tensor_copy(out=vb[r].t[:, :, 0:64], in_=tp.t[:, 0:128].rearrange("k (g d) -> k g d", g=2)), outs=[vb[r]], ins=[tp])

    load_kv(0)
    for t in range(nq):
        load_kv(t + 1)
        qf_, qb_ = qf[t % 2], qb[t % 2]
        P.dma(qf_[:], V(qT, qT.t[:, t * 128:(t + 1) * 128].rearrange("(h d) q -> d h q", h=8)))
        P.op('pool', lambda e, qf_=qf_, qb_=qb_: e.tensor_copy(out=qb_.t[:], in_=qf_.t[:]), outs=[qb_], ins=[qf_])
        o_ = osb[t % 2]
        for g in range(2):
            ops_ = ps[7]
            P.op('pe', lambda e, ops_=ops_: e.matmul(ops_.t[:, 0:260], lhsT=zl.t[:], rhs=zr_.t[:, 0:260], start=True, stop=False), outs=[ops_], ins=[zl, zr_])
            for which in range(2):
                r = (t + which) % NR
                sp = ps[which]
                P.op('pe', lambda e, sp=sp, r=r, g=g, qb_=qb_: e.matmul(sp.t[:], lhsT=kb[r].t[:, g, :], rhs=qb_.t[:, 4 * g:4 * g + 4, :].rearrange("d h q -> d (h q)"),
                                                                      start=True, stop=True), outs=[sp], ins=[kb[r], qb_])
                e_ = ex[which]
                P.op('act', lambda e, sp=sp, e_=e_: e.activation(out=e_.t[:], in_=sp.t[:], func=AF.Exp, scale=0.125), outs=[e_], ins=[sp])
                mk = mcur if which == 1 else (mp0 if t == 0 else mprev)
                p_ = pm[pmi[0] % 4]; pmi[0] += 1
                P.op('dve', lambda e, p_=p_, e_=e_, mk=mk: e.tensor_tensor(out=p_.t[:], in0=e_.t[:], in1=mk.t[:], op=ALU.mult), outs=[p_], ins=[e_, mk])
                for h in range(4):
                    last = (which == 1 and h == 3)
                    P.op('pe', lambda e, ops_=ops_, p_=p_, r=r, g=g, h=h, last=last: e.matmul(ops_.t[:, h * 65:(h + 1) * 65], lhsT=p_.t[:, h * 128:(h + 1) * 128], rhs=vb[r].t[:, g, :],
                                                                                            start=False, stop=last), outs=[ops_], ins=[p_, vb[r]])
            ov = ops_.t[:, 0:260].rearrange("q (h e) -> q h e", h=4)
            P.op('dve', lambda e, ov=ov, g=g: e.tensor_tensor(out=lsum.t[:], in0=ov[:, :, 64], in1=esink.t[:, 4 * g:4 * g + 4], op=ALU.add), outs=[lsum], ins=[ops_, esink])
            P.op('dve', lambda e: e.reciprocal(out=lsum.t[:], in_=lsum.t[:]), outs=[lsum], ins=[lsum])
            for h in range(4):
                P.op('dve', lambda e, ov=ov, o_=o_, g=g, h=h: e.tensor_scalar(out=o_.t[:, (4 * g + h) * 64:(4 * g + h + 1) * 64], in0=ov[:, h, 0:64], scalar1=lsum.t[:, h:h + 1],
                                                                           scalar2=None, op0=ALU.mult), outs=[o_], ins=[ops_, lsum])
        for cc in range(4):
            tp = ps[6]
            P.op('pe', lambda e, tp=tp, o_=o_, cc=cc: e.transpose(out=tp.t[:, 0:128], in_=o_.t[:, cc * 128:(cc + 1) * 128], identity=ident.t[:]), outs=[tp], ins=[o_, ident])
            ot = otr[cc % 2]
            P.op('act', lambda e, tp=tp, ot=ot: e.copy(out=ot.t[:], in_=tp.t[:, 0:128]), outs=[ot], ins=[tp])
            P.dma(V(ocT, ocT.t[cc * 128:(cc + 1) * 128, t * 128:(t + 1) * 128]), ot[:])
    return P.finish()


VSC = 2048


def nsa_phase(P, L, ident, nsa_in, onT):
    qT4, kcT, vcT, ksT, vsT, kwT, vwT, gtT, w1d, peT, b1d, w2d, b2k, b2v = nsa_in
    NQ = L // 128
    NC = L // 16 - 1
    NCT = (NC + 1 + 127) // 128
    NCp = NCT * 128
    NJ = L // 64
    NJC = (NJ + 127) // 128
    JW = NJC * 128
    kc_bf = P.sb([64, NCp], F32, "n_kc"); vc = P.sb([128, NCT, 65], F32, "n_vc")
    P.memset(kc_bf, kc_bf.t[:], 0.0); P.memset(vc, vc.t[:], 1.0)
    with P.scope():
        psB = [P.ps([128, 512], F32, f"nB_ps{i}") for i in range(3)]
        w1 = P.sb([64, 32, 256], F32, "nB_w1"); hid = P.sb([128, 2, NCp], F32, "nB_hid")
        kraw = P.sb([64, 512 * 16 + 16], F32, "nB_kraw")
        pes = P.sb([64, 2, 32], F32, "nB_pe"); b1s = P.sb([128, 2, 2], F32, "nB_b1"); w2s = P.sb([128, 2, 2, 64], F32, "nB_w2")
        b2ks = P.sb([64, 1], F32, "nB_b2k"); b2vs = P.sb([128, 64], F32, "nB_b2v"); hidb = P.sb([128, 2], F32, "nB_hidb")
        P.dma(pes[:], peT[:]); P.dma(b1s[:], b1d[:]); P.dma(w2s[:], w2d[:]); P.dma(b2ks[:], b2k[:])
        P.dma(b2vs[:], V(b2v, b2v.t.partition_broadcast(128)))
        P.memset(hid, hid.t[:], 0.0)
        for kv in range(2):
            src = kcT if kv == 0 else vcT
            for jj in range(4):
                P.dma(V(w1, w1.t[:, jj * 8:(jj + 1) * 8, :]), V(w1d, w1d.t[kv, :, jj * 8:(jj + 1) * 8, :]))
            for hc in range(2):
                pp = psB[2]
                for j in range(32):
                    P.mm(pp, pp.t[:, 0:1], w1.t[:, j, hc * 128:(hc + 1) * 128], pes.t[:, kv, j:j + 1], j == 0, j == 31, [w1, pes])
                P.tt('dve', hidb, hidb.t[:, hc:hc + 1], pp.t[:, 0:1], b1s.t[:, kv, hc:hc + 1], ALU.add, [pp, b1s])
            for c0 in range(0, NC, 512):
                n = min(512, NC - c0)
                P.dma(V(kraw, kraw.t[:, 0:16 * n + 16]), V(src, src.t[:, 16 * c0:16 * c0 + 16 * n + 16]))
                for hc in range(2):
                    pp = psB[hc]
                    for j in range(32):
                        P.mm(pp, pp.t[:, 0:n], w1.t[:, j, hc * 128:(hc + 1) * 128], kraw.t[:, j:j + 16 * (n - 1) + 1:16], j == 0, j == 31, [w1, kraw])
                    P.act(hid, hid.t[:, hc, c0:c0 + n], pp.t[:, 0:n], AF.Gelu_apprx_tanh, [pp, hidb], bias=hidb.t[:, hc:hc + 1], scale=1.0)
            if kv == 0:
                for c0 in range(0, NC, 512):
                    n = min(512, NC - c0)
                    pp = psB[2]
                    for hc in range(2):
                        P.mm(pp, pp.t[0:64, 0:n], w2s.t[:, 0, hc, :], hid.t[:, hc, c0:c0 + n], hc == 0, hc == 1, [w2s, hid])
                    P.act(kc_bf, kc_bf.t[:, c0:c0 + n], pp.t[0:64, 0:n], AF.Identity, [pp, b2ks], bias=b2ks.t[:, 0:1], scale=1.0)
            else:
                for ct in range(NCT):
                    pp = psB[2]
                    for hc in range(2):
                        P.mm(pp, pp.t[:, 0:64], hid.t[:, hc, ct * 128:(ct + 1) * 128], w2s.t[:, 1, hc, :], hc == 0, hc == 1, [hid, w2s])
                    P.tt('dve', vc, vc.t[:, ct, 0:64], pp.t[:, 0:64], b2vs.t[:], ALU.add, [pp, b2vs])
    with P.scope():
        ks_bf = P.sb([64, L], BF16, "n_ks"); kw_bf = P.sb([64, L], BF16, "n_kw")
        vs = P.sb([128, NQ, 65], BF16, "n_vs"); vw = P.sb([128, NQ, 65], BF16, "n_vw")
        P.memset(vs, vs.t[:], 1.0); P.memset(vw, vw.t[:], 1.0)
        WEXP = min(64, NQ) * 128
        wexp = P.sb([128, WEXP], BF16, "n_wexp")
        S_ps = [P.ps([128, 512], F32, f"nC_S{i}") for i in range(3)]
        Oc_ps = P.ps([128, 512], F32, "nC_Oc"); Os_ps = P.ps([128, 512], F32, "nC_Osw")
        Ow_ps = Os_ps
        imp_ps = P.ps([128, 1024], F32, "nC_imp"); tp_ps = P.ps([128, 512], F32, "nC_tp")
        with P.scope():
            iw = P.sb([128, 2048], I32, "nc_iw"); wa = P.sb([128, 2048], F32, "nc_wa"); wb_ = P.sb([128, 2048], F32, "nc_wb")
            for pc in range(WEXP // 2048 if WEXP >= 2048 else 1):
                w = min(2048, WEXP)
                P.op('pool', lambda e, pc=pc, w=w: e.iota(iw.t[:, 0:w], pattern=[[1, w]], base=2048 * pc, channel_multiplier=-64), outs=[iw])
                P.ts('dve', wa, wa.t[:, 0:w], iw.t[:, 0:w], 0.0, ALU.is_ge, [iw])
                P.ts('dve', wb_, wb_.t[:, 0:w], iw.t[:, 0:w], 63.0, ALU.is_le, [iw])
                P.tt('dve', wexp, wexp.t[:, 2048 * pc:2048 * pc + w], wa.t[:, 0:w], wb_.t[:, 0:w], ALU.mult, [wa, wb_])
            for (src, dst) in ((ksT, ks_bf), (kwT, kw_bf)):
                for c0 in range(0, L, 8192):
                    w = min(8192, L - c0)
                    P.op('pool', lambda e, src=src, dst=dst, c0=c0, w=w: e.dma_start(out=dst.t[:, c0:c0 + w], in_=src.t[:, c0:c0 + w], max_dma_last_dim=4096),
                         outs=[dst], ins=[src], ctr='dpool')
            vraw = P.sb([64, VSC], F32, "nc_vraw")
            for (src, dst) in ((vsT, vs), (vwT, vw)):
                for s0 in range(0, L, VSC):
                    P.dma(vraw[:], V(src, src.t[:, s0:s0 + VSC]))
                    for c in range(VSC // 128):
                        kt = s0 // 128 + c
                        P.tr(tp_ps, tp_ps.t[:, 0:64], vraw.t[:, c * 128:(c + 1) * 128], ident.t[0:64, 0:64], [vraw, ident])
                        P.cp('act' if c % 2 else 'dve', dst, dst.t[:, kt, 0:64], tp_ps.t[:, 0:64], [tp_ps])
        ioA = P.sb([128, 2, 128], I32, "n_ioA")
        P.op('pool', lambda e: e.iota(ioA.t[:], pattern=[[0, 2], [1, 128]], base=0, channel_multiplier=-1), outs=[ioA])
        mcur = P.sb([128, 2, 128], BF16, "n_mcur"); mprev = P.sb([128, 2, 128], BF16, "n_mprev")
        P.ts('dve', mcur, mcur.t[:], ioA.t[:], 0.0, ALU.is_ge, [ioA])
        P.ts('dve', mprev, mprev.t[:], ioA.t[:], 0.0, ALU.is_lt, [ioA])
        ioR = P.sb([128, 128], I32, "n_ioR"); Rt = P.sb([128, 128], F32, "n_R")
        P.op('pool', lambda e: e.iota(ioR.t[:], pattern=[[1, 128]], base=0, channel_multiplier=-16), outs=[ioR])
        P.cp('dve', Rt, Rt.t[:], ioR.t[:], [ioR])
        ioO = P.sb([128, 33], I32, "n_ioO"); ova = P.sb([128, 33], F32, "n_ova"); OVt = P.sb([128, 33], F32, "n_OVt")
        P.op('pool', lambda e: e.iota(ioO.t[:], pattern=[[-4, 33]], base=0, channel_multiplier=1), outs=[ioO])
        P.ts('dve', ova, ova.t[:], ioO.t[:], -1.0, ALU.is_ge, [ioO])
        P.ts('dve', OVt, OVt.t[:], ioO.t[:], 3.0, ALU.is_le, [ioO])
        P.tt('dve', OVt, OVt.t[:], OVt.t[:], ova.t[:], ALU.mult, [OVt, ova])
        ioJ = P.sb([128, JW], I32, "n_ioJ"); ioP = P.sb([128, 1], I32, "n_ioP"); Jt = P.sb([128, JW], F32, "n_J"); pge = P.sb([128, 1], F32, "n_pge")
        P.op('pool', lambda e: e.iota(ioJ.t[:], pattern=[[1, JW]], base=0, channel_multiplier=0), outs=[ioJ])
        P.op('pool', lambda e: e.iota(ioP.t[:], pattern=[[0, 1]], base=0, channel_multiplier=1), outs=[ioP])
        P.ts('dve', pge, pge.t[:], ioP.t[:], 64.0, ALU.is_ge, [ioP])
        P.ts('dve', Jt, Jt.t[:], ioJ.t[:], pge.t[:, 0:1], ALU.subtract, [ioJ, pge])
        zl = P.sb([1, 128], F32, "n_zl"); zr = P.sb([1, 512], F32, "n_zr"); zlb = P.sb([1, 128], BF16, "n_zlb"); zrb = P.sb([1, 512], BF16, "n_zrb")
        for t_ in (zl, zr, zlb, zrb):
            P.memset(t_, t_.t[:], 0.0)
        q4f = [P.sb([64, 4, 128], F32, f"n_q4f{i}") for i in range(2)]; q4b = [P.sb([64, 4, 128], BF16, f"n_q4b{i}") for i in range(2)]
        gtf = P.sb([6, 128], F32, "n_gtf"); gs = P.sb([128, 6], F32, "n_gs")
        Ef = [P.sb([128, 512], F32, f"n_Ef{i}") for i in range(2)]; cmk = P.sb([128, 128], F32, "n_cmk")
        Eb = [P.sb([128, 256], BF16, f"n_Eb{i}") for i in range(4)]
        rl4 = P.sb([128, 4], F32, "n_rl4"); rl2 = P.sb([128, 2], F32, "n_rl2"); coef = P.sb([128, 2], F32, "n_coef")
        imp = P.sb([128, JW], F32, "n_imp"); imp2 = P.sb([128, JW], F32, "n_imp2"); fbA = P.sb([128, JW], F32, "n_fbA")
        m8a = P.sb([128, 8], F32, "n_m8a"); m8b = P.sb([128, 8], F32, "n_m8b")
        nb = P.sb([128, JW], F32, "n_nb"); nbT = [P.sb([128, 128], BF16, f"n_nbT{i}") for i in range(NJC)]
        acc = P.sb([128, 128], F32, "n_acc"); accT = [P.sb([128, 128], F32, f"n_accT{i}") for i in range(2)]
        ebi = [0]

        def next_eb():
            ebi[0] += 1
            return Eb[ebi[0] % 4]

        def combine(O_ps, br, first):
            ov = O_ps.t[:, 0:130].rearrange("q (h e) -> q h e", h=2)
            P.ts('dve', rl2, rl2.t[:], ov[:, :, 64], 1e-30, ALU.max, [O_ps])
            P.op('dve', lambda e: e.reciprocal(out=rl2.t[:], in_=rl2.t[:]), outs=[rl2], ins=[rl2])
            P.tt('dve', coef, coef.t[:], rl2.t[:], gs.t[:, :].rearrange("q (h b) -> q h b", h=2)[:, :, br], ALU.mult, [rl2, gs])
            for hh in range(2):
                if first:
                    P.ts('dve', acc, acc.t[:, hh * 64:(hh + 1) * 64], ov[:, hh, 0:64], coef.t[:, hh:hh + 1], ALU.mult, [O_ps, coef])
                else:
                    P.stt(acc, acc.t[:, hh * 64:(hh + 1) * 64], ov[:, hh, 0:64], coef.t[:, hh:hh + 1], acc.t[:, hh * 64:(hh + 1) * 64], ALU.mult, ALU.add,
                          [O_ps, coef, acc])

        for qt in range(NQ):
            tok = slice(qt * 128, (qt + 1) * 128)
            qf, qb = q4f[qt % 2], q4b[qt % 2]
            P.dma(qf[:], V(qT4, qT4.t[:, tok].rearrange("(h d) q -> d h q", h=4)))
            P.cp('pool', qb, qb.t[:], qf.t[:], [qf])
            P.dma(gtf[:], V(gtT, gtT.t[:, tok]))
            P.tr(tp_ps, tp_ps.t[:, 0:6], gtf.t[:], ident.t[0:6, 0:6], [gtf, ident])
            P.act(gs, gs.t[:], tp_ps.t[:, 0:6], AF.Sigmoid, [tp_ps])
            q4v = qb.t[:].rearrange("d h q -> d (h q)")
            qov = qb.t[:, 0:2, :].rearrange("d h q -> d (h q)")
            P.mm(Oc_ps, Oc_ps.t[:, 0:260], zl.t[:], zr.t[:, 0:260], True, False, [zl, zr])
            P.mm(imp_ps, imp_ps.t[:, 0:512], zl.t[:], zr.t[:, 0:512], True, False, [zl, zr])
            P.mm(imp_ps, imp_ps.t[:, 512:1024], zl.t[:], zr.t[:, 0:512], True, False, [zl, zr])
            nct = (8 * qt + 6) // 128 + 1
            def cmp_S(ct):
                sp = S_ps[ct % 3]
                P.mm(sp, sp.t[:], kc_bf.t[:, ct * 128:(ct + 1) * 128], qf.t[:].rearrange("d h q -> d (h q)"), True, True, [kc_bf, qf])
            cmp_S(0)
            for ct in range(nct):
                if ct + 1 < nct:
                    cmp_S(ct + 1)
                sp = S_ps[ct % 3]
                ef = Ef[ct % 2]
                P.act(ef, ef.t[:], sp.t[:], AF.Exp, [sp], scale=0.125)
                thr = 2048 * ct + 31 - 128 * qt
                if thr > -2032:
                    P.ts('dve', cmk, cmk.t[:], Rt.t[:], float(thr), ALU.is_ge, [Rt])
                    P.tt('dve', ef, ef.t[:].rearrange("c (h q) -> c h q", h=4), ef.t[:].rearrange("c (h q) -> c h q", h=4),
                         cmk.t[:, :].unsqueeze(1).to_broadcast([128, 4, 128]), ALU.mult, [ef, cmk])
                last = ct == nct - 1
                for h in range(4):
                    P.mm(Oc_ps, Oc_ps.t[:, h * 65:(h + 1) * 65], ef.t[:, h * 128:(h + 1) * 128], vc.t[:, ct, :], False, last and h == 3, [ef, vc])
                ncol = min(33, NJ - 32 * ct)
                for h in range(4):
                    P.mm(imp_ps, imp_ps.t[:, h * 256 + 32 * ct:h * 256 + 32 * ct + ncol], ef.t[:, h * 128:(h + 1) * 128], OVt.t[:, 0:ncol], False, last and h in (1, 3),
                         [ef, OVt])
            ocv = Oc_ps.t[:, 0:260].rearrange("q (h e) -> q h e", h=4)
            P.ts('dve', rl4, rl4.t[:], ocv[:, :, 64], 1e-30, ALU.max, [Oc_ps])
            P.op('dve', lambda e: e.reciprocal(out=rl4.t[:], in_=rl4.t[:]), outs=[rl4], ins=[rl4])
            P.ts('dve', imp, imp.t[:, 0:JW], imp_ps.t[:, 0:JW], rl4.t[:, 0:1], ALU.mult, [imp_ps, rl4])
            for h in range(1, 4):
                P.stt(imp, imp.t[:, 0:JW], imp_ps.t[:, h * 256:h * 256 + JW], rl4.t[:, h:h + 1], imp.t[:, 0:JW], ALU.mult, ALU.add, [imp_ps, rl4, imp])
            combine(Oc_ps, 0, True)
            P.ts('dve', fbA, fbA.t[:], Jt.t[:], float(2 * qt - 1), ALU.is_ge, [Jt], s2=1e9, op1=ALU.mult)
            P.tt('dve', imp2, imp2.t[:], imp.t[:], fbA.t[:], ALU.add, [imp, fbA])
            P.ts('dve', fbA, fbA.t[:], Jt.t[:], float(2 * qt), ALU.is_gt, [Jt], s2=-2e9, op1=ALU.mult)
            P.tt('dve', imp2, imp2.t[:], imp2.t[:], fbA.t[:], ALU.add, [imp2, fbA])
            P.memset(imp2, imp2.t[:, 0:1], 1e9, eng='dve')
            P.op('dve', lambda e: e.max(out=m8a.t[:], in_=imp2.t[:]), outs=[m8a], ins=[imp2])
            P.op('dve', lambda e: e.match_replace(out=imp.t[:], in_to_replace=m8a.t[:], in_values=imp2.t[:], imm_value=-3e9), outs=[imp], ins=[m8a, imp2])
            P.op('dve', lambda e: e.max(out=m8b.t[:], in_=imp.t[:]), outs=[m8b], ins=[imp])
            P.ts('dve', nb, nb.t[:], imp2.t[:], m8b.t[:, 7:8], ALU.is_ge, [imp2, m8b])
            P.ts('dve', nb, nb.t[:], nb.t[:], -NEGB, ALU.mult, [nb], s2=NEGB, op1=ALU.add)
            njc = (2 * qt + 1) // 128 + 1
            for jc in range(njc):
                P.tr(tp_ps, tp_ps.t[:, 0:128], nb.t[:, jc * 128:(jc + 1) * 128], ident.t[:], [nb, ident])
                P.cp('act', nbT[jc], nbT[jc].t[:], tp_ps.t[:, 0:128], [tp_ps])
            P.mm(Os_ps, Os_ps.t[:, 0:130], zlb.t[:], zrb.t[:, 0:130], True, False, [zlb, zrb])
            def slc_S(kt):
                sp = S_ps[kt % 3]
                P.mm(sp, sp.t[:, 0:256], ks_bf.t[:, kt * 128:(kt + 1) * 128], qov, True, False, [ks_bf, qb])
                jc = kt // 64
                P.mm(sp, sp.t[:, 0:256], wexp.t[:, 128 * (kt % 64):128 * (kt % 64) + 128], nbT[jc].t[:, :].unsqueeze(1).to_broadcast([128, 2, 128]),
                     False, True, [wexp, nbT[jc]])
            slc_S(0)
            if qt >= 1:
                slc_S(1)
            for kt in range(qt + 1):
                if kt + 2 <= qt:
                    slc_S(kt + 2)
                sp = S_ps[kt % 3]
                eb = next_eb()
                P.act(eb, eb.t[:], sp.t[:, 0:256], AF.Exp, [sp], scale=0.125)
                if kt == qt:
                    P.tt('dve', eb, eb.t[:], eb.t[:], mcur.t[:].rearrange("k h q -> k (h q)"), ALU.mult, [eb, mcur])
                for hh in range(2):
                    P.mm(Os_ps, Os_ps.t[:, hh * 65:(hh + 1) * 65], eb.t[:, hh * 128:(hh + 1) * 128], vs.t[:, kt, :], False, kt == qt and hh == 1, [eb, vs])
            combine(Os_ps, 1, False)
            P.mm(Ow_ps, Ow_ps.t[:, 0:130], zlb.t[:], zrb.t[:, 0:130], True, False, [zlb, zrb])
            k0 = max(0, qt - 4)

            def win_S(kt):
                sp = S_ps[kt % 3]
                P.mm(sp, sp.t[:, 0:256], kw_bf.t[:, kt * 128:(kt + 1) * 128], qov, True, True, [kw_bf, qb])
            win_S(k0)
            for kt in range(k0, qt + 1):
                if kt + 1 <= qt:
                    win_S(kt + 1)
                sp = S_ps[kt % 3]
                eb = next_eb()
                P.act(eb, eb.t[:], sp.t[:, 0:256], AF.Exp, [sp], scale=0.125)
                if kt == qt:
                    P.tt('dve', eb, eb.t[:], eb.t[:], mcur.t[:].rearrange("k h q -> k (h q)"), ALU.mult, [eb, mcur])
                if kt == qt - 4:
                    P.tt('dve', eb, eb.t[:], eb.t[:], mprev.t[:].rearrange("k h q -> k (h q)"), ALU.mult, [eb, mprev])
                for hh in range(2):
                    P.mm(Ow_ps, Ow_ps.t[:, hh * 65:(hh + 1) * 65], eb.t[:, hh * 128:(hh + 1) * 128], vw.t[:, kt, :], False, kt == qt and hh == 1, [eb, vw])
            combine(Ow_ps, 2, False)
            P.tr(tp_ps, tp_ps.t[:, 0:128], acc.t[:], ident.t[:], [acc, ident])
            at = accT[qt % 2]
            P.cp('act', at, at.t[:], tp_ps.t[:, 0:128], [tp_ps])
            P.dma(V(onT, onT.t[:, tok]), at[:])


SC = 2048


def ssd_phase(P, L, ps, ident, ssd_in, ygT):
    zT, xT, bT_, cT_, dtT, cw, cb, hp, dsk_d = ssd_in
    NSC = L // SC
    io = P.sb([128, 128], I32, "s_io")
    P.op('pool', lambda e: e.iota(io.t[:], pattern=[[1, 128]], base=0, channel_multiplier=-1), outs=[io])
    negm = P.sb([128, 128], F32, "s_negm")
    P.ts('dve', negm, negm.t[:], io.t[:], 0.0, ALU.is_ge, [io], s2=None)
    P.ts('dve', negm, negm.t[:], negm.t[:], -NEGB, ALU.mult, [negm], s2=NEGB, op1=ALU.add)
    io2 = P.sb([2, SC // 128, 128], I32, "s_io2")
    P.op('pool', lambda e: e.iota(io2.t[:], pattern=[[0, SC // 128], [1, 128]], base=0, channel_multiplier=0), outs=[io2])
    rmask = P.sb([2, SC], F32, "s_rmask")
    P.ts('dve', rmask, rmask.t[:], io2.t[:].rearrange("p a b -> p (a b)"), 0.0, ALU.is_gt, [io2])
    io3 = P.sb([2, 2, 128], I32, "s_io3")
    P.op('pool', lambda e: e.iota(io3.t[:], pattern=[[1, 2], [0, 128]], base=0, channel_multiplier=-1), outs=[io3])
    sel = P.sb([2, 2, 128], F32, "s_sel")
    P.ts('dve', sel, sel.t[:], io3.t[:], 0.0, ALU.is_equal, [io3])
    cws = P.sb([128, 3, 4], F32, "s_cw"); cbs = P.sb([128, 3], F32, "s_cb"); hps = P.sb([2, 3], F32, "s_hp"); dsk = P.sb([128, 1], F32, "s_dsk")
    P.dma(cws[:], cw[:]); P.dma(cbs[:], cb[:]); P.dma(hps[:], hp[:]); P.dma(dsk[:], dsk_d[:])
    na = P.sb([2, 1], F32, "s_na")
    P.act(na, na.t[:], hps.t[:, 1:2], AF.Exp, [hps])
    P.ts('dve', na, na.t[:], na.t[:], -1.0, ALU.mult, [na])
    hf = P.sb([128, 128], F32, "s_hf"); hb = P.sb([128, 128], BF16, "s_hb")
    P.memset(hf, hf.t[:], 0.0); P.memset(hb, hb.t[:], 0.0)
    raw = P.sb([128, SC + 3], F32, "s_raw"); acc = P.sb([128, SC], F32, "s_acc")
    xs = P.sb([128, SC], F32, "s_xs"); Bs = P.sb([128, SC], BF16, "s_Bs"); Cs = P.sb([128, SC], BF16, "s_Cs")
    zs = P.sb([128, SC], F32, "s_zs"); ysc = P.sb([128, SC], F32, "s_ysc")
    dtr = P.sb([2, SC], F32, "s_dtr"); dts = P.sb([2, SC], F32, "s_dt"); acum = P.sb([2, SC], F32, "s_acum")
    small = P.sb([128, 4], F32, "s_small"); Btok = P.sb([128, 128], BF16, "s_Btok")
    Dsb = [P.sb([128, 128], F32, f"s_D{r}") for r in range(2)]
    Esb = [P.sb([128, 128], F32, f"s_E{r}") for r in range(2)]
    Msb = [P.sb([128, 128], BF16, f"s_M{r}") for r in range(2)]
    EBs = [P.sb([128, 128], F32, f"s_EB{r}") for r in range(2)]
    Csr = [P.sb([128, 128], BF16, f"s_Csr{r}") for r in range(2)]
    xd = P.sb([128, 128], BF16, "s_xd"); xdd = P.sb([128, 128], BF16, "s_xdd")
    identb = P.sb([128, 128], BF16, "s_identb")
    P.cp('dve', identb, identb.t[:], ident.t[:], [ident])
    tp1, g_ps, bc_ps, y_ps, S_ps = ps[0], ps[1], ps[2], ps[3], ps[4]
    psb = P.ps([128, 128], BF16, "s_psb")

    def conv_silu(src, which, out_tt, s):
        P.dma(raw[:], V(src, src.t[:, s * SC:s * SC + SC + 3]))
        P.ts('dve', acc, acc.t[:], raw.t[:, 0:SC], cws.t[:, which, 0:1], ALU.mult, [raw, cws])
        for k in range(1, 4):
            P.stt(acc, acc.t[:], raw.t[:, k:SC + k], cws.t[:, which, k:k + 1], acc.t[:], ALU.mult, ALU.add, [raw, cws, acc])
        P.act(out_tt, out_tt.t[:], acc.t[:], AF.Silu, [acc, cbs], bias=cbs.t[:, which:which + 1], scale=1.0)

    for s in range(NSC):
        conv_silu(xT, 0, xs, s)
        conv_silu(bT_, 1, Bs, s)
        conv_silu(cT_, 2, Cs, s)
        P.dma(acc[:], V(zT, zT.t[:, s * SC:(s + 1) * SC]))
        P.act(zs, zs.t[:], acc.t[:], AF.Silu, [acc])
        P.dma(dtr[:], V(dtT, dtT.t[:, s * SC:(s + 1) * SC]))
        P.act(dtr, dtr.t[:], dtr.t[:], AF.Exp, [dtr, hps], bias=hps.t[:, 0:1], scale=1.0)
        P.act(dts, dts.t[:], dtr.t[:], AF.Ln, [dtr], bias=1.0, scale=1.0)
        P.ts('dve', dtr, dtr.t[:], dts.t[:], na.t[:, 0:1], ALU.mult, [dts, na])
        P.op('dve', lambda e: e.tensor_tensor_scan(out=acum.t[:], data0=rmask.t[:], data1=dtr.t[:], initial=0.0, op0=ALU.mult, op1=ALU.add),
             outs=[acum], ins=[rmask, dtr])
        for c in range(SC // 128):
            o = slice(c * 128, (c + 1) * 128)
            P.tr(tp1, tp1.t[:, 0:128], xs.t[:, o], ident.t[:], [xs, ident])
            P.tr(tp1, tp1.t[:, 128:130], dts.t[0:2, o], ident.t[0:2, 0:2], [dts, ident])
            P.tr(tp1, tp1.t[:, 130:132], acum.t[0:2, o], ident.t[0:2, 0:2], [acum, ident])
            P.cp('dve', small, small.t[:], tp1.t[:, 128:132], [tp1])
            P.tr(psb, psb.t[:], Bs.t[:, o], identb.t[:], [Bs, identb])
            P.cp('act', Btok, Btok.t[:], psb.t[:], [psb])
            P.mm(g_ps, g_ps.t[:, 0:128], Bs.t[:, o], Cs.t[:, o], True, True, [Bs, Cs])
            for r in range(2):
                P.mm(bc_ps, bc_ps.t[:, r * 128:(r + 1) * 128], sel.t[:, r, :], acum.t[0:2, o], True, True, [sel, acum])
            bcs = [bc_ps.t[:, r * 128:(r + 1) * 128] for r in range(2)]
            for r in range(2):
                P.stt(Dsb[r], Dsb[r].t[:], bcs[r], small.t[:, 2 + r:3 + r], negm.t[:], ALU.subtract, ALU.add, [bc_ps, small, negm])
            for r in range(2):
                P.act(Esb[r], Esb[r].t[:], Dsb[r].t[:], AF.Exp, [Dsb[r]])
            for r in range(2):
                P.act(EBs[r], EBs[r].t[:], bcs[r], AF.Exp, [bc_ps])
            for r in range(2):
                P.tt('dve', Msb[r], Msb[r].t[:], g_ps.t[:, 0:128], Esb[r].t[:], ALU.mult, [g_ps, Esb[r]])
            for r in range(2):
                P.tt('pool', Csr[r], Csr[r].t[:], Cs.t[:, o], EBs[r].t[:], ALU.mult, [Cs, EBs[r]])
            for r in range(2):
                P.ts('dve', xdd, xdd.t[:, r * 64:(r + 1) * 64], tp1.t[:, r * 64:(r + 1) * 64], small.t[:, r:r + 1], ALU.mult, [tp1, small, Esb[r]],
                     s2=Esb[r].t[:, 127:128], op1=ALU.mult)
                P.ts('dve', xd, xd.t[:, r * 64:(r + 1) * 64], tp1.t[:, r * 64:(r + 1) * 64], small.t[:, r:r + 1], ALU.mult, [tp1, small])
            for r in range(2):
                P.mm(y_ps, y_ps.t[64 * r:64 * r + 64, 0:128], xd.t[:, r * 64:(r + 1) * 64], Msb[r].t[:], True, False, [xd, Msb[r]])
                P.mm(y_ps, y_ps.t[64 * r:64 * r + 64, 0:128], hb.t[:, r * 64:(r + 1) * 64], Csr[r].t[:], False, True, [hb, Csr[r]])
            P.mm(S_ps, S_ps.t[:, 0:128], Btok.t[:], xdd.t[:], True, True, [Btok, xdd])
            for r in range(2):
                P.stt(hf, hf.t[:, r * 64:(r + 1) * 64], hf.t[:, r * 64:(r + 1) * 64], EBs[r].t[:, 127:128], S_ps.t[:, r * 64:(r + 1) * 64],
                      ALU.mult, ALU.add, [hf, EBs[r], S_ps])
            P.cp('pool', hb, hb.t[:], hf.t[:], [hf])
            P.stt(ysc, ysc.t[:, o], xs.t[:, o], dsk.t[:, 0:1], y_ps.t[:, 0:128], ALU.mult, ALU.add, [xs, dsk, y_ps])
        P.tt('pool', ysc, ysc.t[:], ysc.t[:], zs.t[:], ALU.mult, [ysc, zs])
        P.dma(V(ygT, ygT.t[:, s * SC:(s + 1) * SC]), ysc[:])


def build_even(L=16384, do_ssd=True, do_nsa=True):
    P = Prog()
    io = P.sb([128, 128], I32, "io0")
    P.op('pool', lambda e: e.iota(io.t[:], pattern=[[1, 128]], base=0, channel_multiplier=-1), outs=[io])
    ident = P.sb([128, 128], F32, "ident")
    P.ts('dve', ident, ident.t[:], io.t[:], 0.0, ALU.is_equal, [io])
    if do_ssd:
        zT = P.din("zT", [128, L]); xT = P.din("xT", [128, L + 3]); bT_ = P.din("bT_", [128, L + 3]); cT_ = P.din("cT_", [128, L + 3])
        dtT = P.din("dtT", [2, L]); cw = P.din("cw", [128, 3, 4]); cb = P.din("cb", [128, 3]); hp = P.din("hp", [2, 3]); dsk = P.din("dsk", [128, 1])
        ygT = P.dout("ygT", [128, L])
        with P.scope():
            ps = [P.ps([128, 512], F32, f"ps{i}") for i in range(5)]
            ssd_phase(P, L, ps, ident, (zT, xT, bT_, cT_, dtT, cw, cb, hp, dsk), ygT)
    if do_nsa:
        nsa_in = (P.din("qT4", [256, L]), P.din("kcT", [64, L]), P.din("vcT", [64, L]), P.din("ksT", [64, L]), P.din("vsT", [64, L]),
                  P.din("kwT", [64, L]), P.din("vwT", [64, L]), P.din("gtT", [6, L]), P.din("w1d", [2, 64, 32, 256]), P.din("peT", [64, 2, 32]),
                  P.din("b1d", [128, 2, 2]), P.din("w2d", [128, 2, 2, 64]), P.din("b2k", [64, 1]), P.din("b2v", [64]))
        onT = P.dout("onT", [128, L])
        nsa_phase(P, L, ident, nsa_in, onT)
    return P.finish()


def prep_odd(uT, part, i, inp, nq=32, nch=32):
    NQT = nq * 128
    s0 = part * NQT
    m = {}
    m["qT"] = np.ascontiguousarray(uT[0:512, s0:s0 + NQT])
    kv = np.zeros((256, NQT + 128), np.float32)
    lo = s0 - 128
    if lo >= 0:
        kv[:, :] = uT[512:768, lo:s0 + NQT]
    else:
        kv[:, 128:] = uT[512:768, s0:s0 + NQT]
    m["kT"] = np.ascontiguousarray(kv[0:128]); m["vT"] = np.ascontiguousarray(kv[128:256])
    k = np.arange(128)[:, None]; q = np.arange(128)[None, :]
    mp = np.tile((k > q).astype(np.float32), (1, 4))
    m["mprev0"] = mp if part > 0 else np.zeros_like(mp)
    m["sinks"] = np.ascontiguousarray(inp["swa_sinks"][i])
    L = nch * 512
    c0 = 768 + 128 * part
    m["usT"] = np.ascontiguousarray(uT[c0:c0 + 128, 0:L])
    g0 = 8 * part
    prm = np.zeros((128, 4, 3), np.float32)
    bT = np.zeros((4, 128, 2, 128), np.float32); cT = np.zeros((4, 128, 2, 128), np.float32)
    for t in range(4):
        for gg in range(2):
            gl = 2 * t + gg; g = g0 + gl
            sl = slice(gg * 64, gg * 64 + 64)
            prm[sl, t, 0] = inp["s5_a_re"][i, g]; prm[sl, t, 1] = inp["s5_a_im"][i, g]; prm[sl, t, 2] = inp["s5_log_dt"][i, g]
            ch = slice(gl * 16, gl * 16 + 16)
            bT[t, ch, 0, sl] = inp["s5_b_re"][i, g].T; bT[t, ch, 1, sl] = inp["s5_b_im"][i, g].T
            cT[t, sl, 0, ch] = inp["s5_c_re"][i, g].T; cT[t, sl, 1, ch] = inp["s5_c_im"][i, g].T
    m["s5p"] = prm; m["bT"] = bT; m["cT"] = cT
    m["s5d"] = np.ascontiguousarray(inp["s5_d"][i, c0 - 768:c0 - 768 + 128].reshape(128, 1))
    return m


def prep_ssd(uT, g, half, i, inp, L=16384):
    hh = 4 * g + 2 * half
    m = {}
    m["zT"] = np.ascontiguousarray(uT[64 * hh:64 * hh + 128, :L])
    def pad(rows):
        a = np.zeros((rows.shape[0], L + 3), np.float32); a[:, 3:] = rows[:, :L]; return a
    m["xT"] = pad(uT[512 + 64 * hh:512 + 64 * hh + 128])
    m["bT_"] = pad(uT[1024 + 128 * g:1024 + 128 * g + 128])
    m["cT_"] = pad(uT[1280 + 128 * g:1280 + 128 * g + 128])
    m["dtT"] = np.ascontiguousarray(uT[1536 + hh:1536 + hh + 2, :L])
    cwf = inp["ssd_conv_w"][i]; cbf = inp["ssd_conv_b"][i]
    chs = [slice(64 * hh, 64 * hh + 128), slice(512 + 128 * g, 512 + 128 * g + 128), slice(768 + 128 * g, 768 + 128 * g + 128)]
    cw = np.zeros((128, 3, 4), np.float32); cb = np.zeros((128, 3), np.float32)
    for w, sl in enumerate(chs):
        cw[:, w, :] = cwf[:, sl].T; cb[:, w] = cbf[sl]
    m["cw"] = cw; m["cb"] = cb
    hp = np.zeros((2, 3), np.float32)
    hp[:, 0] = inp["ssd_dt_bias"][i, hh:hh + 2]; hp[:, 1] = inp["ssd_a_log"][i, hh:hh + 2]
    m["hp"] = hp
    m["dsk"] = np.repeat(inp["ssd_d"][i, hh:hh + 2], 64).reshape(128, 1).astype(np.float32)
    return m

def prep_nsa(uT, g, half, i, inp, L=16384):
    m = {}
    base = SSD_IN
    order = [2 * half, 2 * half + 1, 2 * (1 - half), 2 * (1 - half) + 1]
    q = [uT[base + 64 * (4 * g + h):base + 64 * (4 * g + h) + 64, :L] for h in order]
    m["qT4"] = np.ascontiguousarray(np.concatenate(q, 0))
    names = ["kcT", "vcT", "ksT", "vsT", "kwT", "vwT"]
    for n_, nm in enumerate(names):
        r0 = base + 512 + 128 * n_ + 64 * g
        m[nm] = np.ascontiguousarray(uT[r0:r0 + 64, :L])
    g0 = base + 512 + 768 + 12 * g + 6 * half
    m["gtT"] = np.ascontiguousarray(uT[g0:g0 + 6, :L])
    w1 = inp["nsa_cmp_w1"][i]
    m["w1d"] = np.ascontiguousarray(w1.reshape(2, 32, 64, 256).transpose(0, 2, 1, 3))
    m["peT"] = np.ascontiguousarray(inp["nsa_pe"][i].transpose(2, 0, 1))
    m["b1d"] = np.ascontiguousarray(inp["nsa_cmp_b1"][i].reshape(2, 2, 128).transpose(2, 0, 1))
    m["w2d"] = np.ascontiguousarray(inp["nsa_cmp_w2"][i].reshape(2, 2, 128, 64).transpose(2, 0, 1, 3))
    m["b2k"] = np.ascontiguousarray(inp["nsa_cmp_b2"][i, 0].reshape(64, 1))
    m["b2v"] = np.ascontiguousarray(inp["nsa_cmp_b2"][i, 1])
    return m


_PROGS = {}


def _prog(key, fn):
    if key not in _PROGS:
        _PROGS[key] = fn()
    return _PROGS[key]


def _g2(g):
    return np.ascontiguousarray(np.asarray(g, np.float32).reshape(8, 128).T)


def _run(nc, maps):
    res = run_bass_kernel_spmd(nc, maps, core_ids=list(range(8)))
    return res.results


def kernel(**inp):
    inp = {k: np.asarray(v) for k, v in inp.items()}
    x = inp["x"].astype(np.float32)
    B, S, D = x.shape
    NTC = 4096
    hT = [np.ascontiguousarray(x[c // 4, (c % 4) * NTC:(c % 4 + 1) * NTC, :].T) for c in range(8)]

    def gather_u(res, n):
        uT = [np.empty((n, S), np.float32) for _ in range(B)]
        for c in range(8):
            uT[c // 4][:, (c % 4) * NTC:(c % 4 + 1) * NTC] = res[c]["uT"]
        return uT

    nc = _prog(("tok", False, False, 2848, False), lambda: build_tok(False, False, 2848, False))
    res = _run(nc, [{"hT": hT[c], "g_in": _g2(inp["norm_mix"][0]), "w_in": np.ascontiguousarray(inp["ev_w_in"][0])} for c in range(8)])
    uT = gather_u(res, 2848)
    out = None
    for layer in range(4):
        i = layer // 2
        odd = layer % 2 == 1
        ycT = [np.empty((1024, S), np.float32) for _ in range(B)]
        if not odd:
            nc = _prog(("even",), lambda: build_even(16384, True, True))
            maps = []
            for c in range(8):
                b, g, half = c // 4, (c % 4) // 2, c % 2
                m = prep_ssd(uT[b], g, half, i, inp)
                m.update(prep_nsa(uT[b], g, half, i, inp))
                maps.append(m)
            res = _run(nc, maps)
            for c in range(8):
                b, g, half = c // 4, (c % 4) // 2, c % 2
                hh = 4 * g + 2 * half
                ycT[b][64 * hh:64 * hh + 128, :] = res[c]["ygT"]
                ycT[b][512 + 64 * hh:512 + 64 * hh + 128, :] = res[c]["onT"]
        else:
            nc = _prog(("odd",), lambda: build_odd(32, 32))
            maps = [prep_odd(uT[c // 4], c % 4, i, inp) for c in range(8)]
            res = _run(nc, maps)
            for c in range(8):
                b, part = c // 4, c % 4
                ycT[b][0:512, part * NTC:(part + 1) * NTC] = res[c]["ocT"]
                ycT[b][512 + 128 * part:512 + 128 * part + 128, :] = res[c]["ydT"]
        del uT
        last = layer == 3
        n_in = 0 if last else (1280 if not odd else 2848)
        nc = _prog(("tok", True, odd, n_in, last), lambda: build_tok(True, odd, n_in, last))
        maps = []
        for c in range(8):
            b, part = c // 4, c % 4
            m = {"hT": hT[c], "ycT": np.ascontiguousarray(ycT[b][:, part * NTC:(part + 1) * NTC]),
                 "w_out": np.ascontiguousarray((inp["od_w_out"] if odd else inp["ev_w_out"])[i]),
                 "g_mlp": _g2(inp["norm_mlp"][layer]),
                 "w_up": np.ascontiguousarray(inp["mlp_w_up"][layer]), "w_down": np.ascontiguousarray(inp["mlp_w_down"][layer])}
            if odd:
                m["glu_w"] = np.ascontiguousarray(inp["s5_glu_w"][i])
                m["glu_b"] = np.ascontiguousarray(inp["s5_glu_b"][i].reshape(4, 128).T)
            else:
                m["ssdn"] = np.ascontiguousarray(inp["ssd_norm"][i].reshape(4, 128).T)
            if n_in:
                m["g_in"] = _g2(inp["norm_mix"][layer + 1])
                m["w_in"] = np.ascontiguousarray((inp["od_w_in"][i] if not odd else inp["ev_w_in"][i + 1]))
            if last:
                m["g_fin"] = _g2(inp["norm_final"])
            maps.append(m)
        res = _run(nc, maps)
        del ycT
        if last:
            out = np.empty((B, S, D), np.float32)
            for c in range(8):
                out[c // 4, (c % 4) * NTC:(c % 4 + 1) * NTC, :] = res[c]["hTo"].T
        else:
            hT = [res[c]["hTo"] for c in range(8)]
            uT = gather_u(res, n_in)
    return out
```
